# Optimizing a Trainium2 kernel written in Bass

```python
import math
import jax, jax.numpy as jnp
from jax import lax
import numpy as np

D_MODEL = 1024
BATCH = 8
SEQ = 4096
DEPTH = 4

CHUNK = 64
N_MIXERS = 3
N_A = len(range(0, DEPTH, N_MIXERS))
N_B = len(range(1, DEPTH, N_MIXERS))
N_C = len(range(2, DEPTH, N_MIXERS))

DN_ALPHA = (2.0 * DEPTH) ** 0.25
DN_BETA = (8.0 * DEPTH) ** -0.25
LN_EPS = 1e-5
RMS_EPS = 1e-6

SSM_EXPAND = 2
SSM_D_INNER = SSM_EXPAND * D_MODEL
SSM_HEAD_DIM = 64
SSM_HEADS = SSM_D_INNER // SSM_HEAD_DIM
SSM_GROUPS = 8
SSM_HPG = SSM_HEADS // SSM_GROUPS
SSM_STATE = 128
SSM_CONV = 4
SSM_CHUNK = CHUNK
SSM_CONV_DIM = SSM_D_INNER + 2 * SSM_GROUPS * SSM_STATE
SSM_IN_DIM = 2 * SSM_D_INNER + 2 * SSM_GROUPS * SSM_STATE + SSM_HEADS

SG_BLOCK = 128
SG_WIDTH = 2 * D_MODEL
SG_GROUPS = 8
SG_GROUP_DIM = SG_WIDTH // SG_GROUPS

MLA_HEADS = 16
MLA_Q_RANK = 384
MLA_KV_RANK = 256
MLA_NOPE = 64
MLA_ROPE = 32
MLA_V = 64
MLA_IN_DIM = MLA_Q_RANK + MLA_KV_RANK + MLA_ROPE
ROPE_THETA = 10000.0
Q_BLOCK = 128

FFN_HIDDEN = 2816
FFN_CONV = 3

kernel_name = "hybrid_ssd_gmlp_mla_deepnorm_trunk"


def layer_norm(x, g, b):
    xf = x.astype(jnp.float32)
    mu = jnp.mean(xf, -1, keepdims=True)
    var = jnp.mean(jnp.square(xf - mu), -1, keepdims=True)
    return ((xf - mu) * lax.rsqrt(var + LN_EPS) * g + b).astype(x.dtype)


def rms_norm(x, g):
    xf = x.astype(jnp.float32)
    return (xf * lax.rsqrt(jnp.mean(xf * xf, -1, keepdims=True) + RMS_EPS) * g).astype(x.dtype)


def causal_dwconv(x, w, b):
    K, C = w.shape
    y = lax.conv_general_dilated(
        x, w[:, None, :].astype(x.dtype), window_strides=(1,), padding=[(K - 1, 0)],
        dimension_numbers=('NWC', 'WIO', 'NWC'), feature_group_count=C)
    return y + b


def ssd_scan(x, dt, A, Bm, Cm):
    Bsz, L, G, R, P = x.shape
    N = Bm.shape[-1]
    Q = SSM_CHUNK
    nc = L // Q
    x = x.reshape(Bsz, nc, Q, G, R, P)
    dt = dt.reshape(Bsz, nc, Q, G, R)
    Bm = Bm.reshape(Bsz, nc, Q, G, N)
    Cm = Cm.reshape(Bsz, nc, Q, G, N)
    Acum = jnp.cumsum(dt * A, axis=2)
    seg = Acum[:, :, :, None] - Acum[:, :, None, :]
    tri = jnp.tril(jnp.ones((Q, Q), dtype=bool))
    Lmat = jnp.exp(jnp.where(tri[:, :, None, None], seg, -jnp.inf))
    CB = jnp.einsum('bcign,bcjgn->bcijg', Cm, Bm)
    y_diag = jnp.einsum('bcijg,bcijgr,bcjgr,bcjgrp->bcigrp', CB, Lmat, dt, x)
    decay = jnp.exp(Acum[:, :, -1:] - Acum)
    states = jnp.einsum('bcjgn,bcjgr,bcjgrp->bcgrpn', Bm, decay * dt, x)
    chunk_decay = jnp.exp(Acum[:, :, -1])

    def step(h, inp):
        s, d = inp
        return h * d[..., None, None] + s, h

    h0 = jnp.zeros((Bsz, G, R, P, N), jnp.float32)
    _, prev = lax.scan(step, h0, (jnp.moveaxis(states, 1, 0), jnp.moveaxis(chunk_decay, 1, 0)))
    prev = jnp.moveaxis(prev, 0, 1)
    y_off = jnp.einsum('bcign,bcgrpn,bcigr->bcigrp', Cm, prev, jnp.exp(Acum))
    return (y_diag + y_off).reshape(Bsz, L, G, R, P)


def mamba2_mixer(x, w_in, conv_w, conv_b, dt_bias, a_log, d_skip, norm_g, w_out):
    Bsz, L, _ = x.shape
    zxbcdt = x @ w_in
    z, xbc, dt = jnp.split(zxbcdt, [SSM_D_INNER, SSM_D_INNER + SSM_CONV_DIM], axis=-1)
    xbc = jax.nn.silu(causal_dwconv(xbc, conv_w, conv_b))
    xs, Bm, Cm = jnp.split(xbc, [SSM_D_INNER, SSM_D_INNER + SSM_GROUPS * SSM_STATE], axis=-1)
    xs = xs.reshape(Bsz, L, SSM_GROUPS, SSM_HPG, SSM_HEAD_DIM).astype(jnp.float32)
    Bm = Bm.reshape(Bsz, L, SSM_GROUPS, SSM_STATE).astype(jnp.float32)
    Cm = Cm.reshape(Bsz, L, SSM_GROUPS, SSM_STATE).astype(jnp.float32)
    dt = jax.nn.softplus(dt.astype(jnp.float32) + dt_bias.astype(jnp.float32))
    dt = dt.reshape(Bsz, L, SSM_GROUPS, SSM_HPG)
    A = -jnp.exp(a_log.astype(jnp.float32)).reshape(SSM_GROUPS, SSM_HPG)
    y = ssd_scan(xs, dt, A, Bm, Cm)
    y = y + d_skip.astype(jnp.float32).reshape(SSM_GROUPS, SSM_HPG)[:, :, None] * xs
    y = y.reshape(Bsz, L, SSM_D_INNER) * jax.nn.silu(z.astype(jnp.float32))
    y = rms_norm(y.reshape(Bsz, L, SSM_GROUPS, -1),
                 norm_g.astype(jnp.float32).reshape(SSM_GROUPS, -1)).reshape(Bsz, L, SSM_D_INNER)
    return y.astype(x.dtype) @ w_out


def spatial_gating_mixer(x, w_in, b_in, ln_g, ln_b, w_s, b_s, w_out):
    Bsz, L, _ = x.shape
    h = jax.nn.gelu(x @ w_in + b_in)
    u, v = jnp.split(h, 2, axis=-1)
    v = layer_norm(v, ln_g, ln_b)
    nb = L // SG_BLOCK
    v = v.reshape(Bsz, nb, SG_BLOCK, SG_GROUPS, SG_GROUP_DIM)
    causal = jnp.tril(jnp.ones((SG_BLOCK, SG_BLOCK), dtype=bool))
    ws = jnp.where(causal, w_s, 0)
    v = jnp.einsum('gts,bnsgc->bntgc', ws, v) + b_s.T[:, :, None]
    return (u * v.reshape(Bsz, L, SG_WIDTH)) @ w_out


def rope_cos_sin(positions, dtype):
    half = MLA_ROPE // 2
    inv = ROPE_THETA ** (-(jnp.arange(half, dtype=jnp.float32) * 2.0 / MLA_ROPE))
    ang = positions.astype(jnp.float32)[..., None] * inv
    return jnp.cos(ang).astype(dtype), jnp.sin(ang).astype(dtype)


def apply_rope(t, cos, sin):
    half = MLA_ROPE // 2
    t1, t2 = t[..., :half], t[..., half:]
    return jnp.concatenate([t1 * cos - t2 * sin, t2 * cos + t1 * sin], axis=-1)


def mla_mixer(x, positions, w_in, q_norm_g, w_q_b, kv_norm_g, w_kv_b, w_out):
    Bsz, L, _ = x.shape
    H = MLA_HEADS
    q_lat, kv_lat, k_rope = jnp.split(x @ w_in, [MLA_Q_RANK, MLA_Q_RANK + MLA_KV_RANK], axis=-1)
    q = (rms_norm(q_lat, q_norm_g) @ w_q_b).reshape(Bsz, L, H, MLA_NOPE + MLA_ROPE)
    q_nope, q_rope = jnp.split(q, [MLA_NOPE], axis=-1)
    kv = (rms_norm(kv_lat, kv_norm_g) @ w_kv_b).reshape(Bsz, L, H, MLA_NOPE + MLA_V)
    k_nope, v = jnp.split(kv, [MLA_NOPE], axis=-1)
    cos, sin = rope_cos_sin(positions, x.dtype)
    q_rope = apply_rope(q_rope, cos[:, :, None], sin[:, :, None])
    k_rope = apply_rope(k_rope, cos, sin)
    scale = (MLA_NOPE + MLA_ROPE) ** -0.5
    nq = L // Q_BLOCK
    key_chunk = jnp.arange(L) // CHUNK
    qn_blocks = jnp.moveaxis(q_nope.reshape(Bsz, nq, Q_BLOCK, H, MLA_NOPE), 1, 0)
    qr_blocks = jnp.moveaxis(q_rope.reshape(Bsz, nq, Q_BLOCK, H, MLA_ROPE), 1, 0)

    def attend(args):
        qn, qr, qi = args
        s = (jnp.einsum('bqhd,bkhd->bhqk', qn, k_nope)
             + jnp.einsum('bqhd,bkd->bhqk', qr, k_rope)).astype(jnp.float32) * scale
        q_chunk = (qi * Q_BLOCK + jnp.arange(Q_BLOCK)) // CHUNK
        mask = key_chunk[None, :] <= q_chunk[:, None]
        p = jax.nn.softmax(jnp.where(mask, s, -jnp.inf), axis=-1).astype(v.dtype)
        return jnp.einsum('bhqk,bkhd->bqhd', p, v)

    o = lax.map(attend, (qn_blocks, qr_blocks, jnp.arange(nq)))
    o = jnp.moveaxis(o, 0, 1).reshape(Bsz, L, H * MLA_V)
    return o @ w_out


def conv_ffn(x, w_in, conv_w, conv_b, w_out):
    h = causal_dwconv(x @ w_in, conv_w, conv_b)
    g, u = jnp.split(h, 2, axis=-1)
    return (jax.nn.silu(g) * u) @ w_out


def setup_inputs(seed: int = 0) -> dict:
    key = jax.random.key(seed)
    ks = iter(jax.random.split(key, 40))
    f32 = jnp.float32

    def nrm(shape, scale):
        return jax.random.normal(next(ks), shape, f32) * scale

    def gain(shape):
        return 1.0 + nrm(shape, 0.02)

    x = jax.random.normal(next(ks), (BATCH, SEQ, D_MODEL), f32)
    offsets = jax.random.randint(next(ks), (BATCH, 1), 0, 4096, dtype=jnp.int32)
    positions = offsets + jnp.arange(SEQ, dtype=jnp.int32)[None, :]

    ssm_w_in = nrm((N_A, D_MODEL, SSM_IN_DIM), D_MODEL ** -0.5)
    ssm_conv_w = nrm((N_A, SSM_CONV, SSM_CONV_DIM), SSM_CONV ** -0.5)
    ssm_conv_b = nrm((N_A, SSM_CONV_DIM), 0.02)
    dt0 = jnp.exp(jax.random.uniform(next(ks), (N_A, SSM_HEADS), f32, math.log(1e-3), math.log(1e-1)))
    ssm_dt_bias = dt0 + jnp.log(-jnp.expm1(-dt0))
    ssm_a_log = jnp.log(jax.random.uniform(next(ks), (N_A, SSM_HEADS), f32, 1.0, 16.0))
    ssm_d = gain((N_A, SSM_HEADS))
    ssm_norm_g = gain((N_A, SSM_D_INNER))
    ssm_w_out = nrm((N_A, SSM_D_INNER, D_MODEL), SSM_D_INNER ** -0.5 * DN_BETA)

    sg_w_in = nrm((N_B, D_MODEL, 2 * SG_WIDTH), D_MODEL ** -0.5)
    sg_b_in = nrm((N_B, 2 * SG_WIDTH), 0.02)
    sg_ln_g = gain((N_B, SG_WIDTH))
    sg_ln_b = nrm((N_B, SG_WIDTH), 0.02)
    sg_w_s = nrm((N_B, SG_GROUPS, SG_BLOCK, SG_BLOCK), SG_BLOCK ** -0.5)
    sg_b_s = gain((N_B, SG_GROUPS, SG_BLOCK))
    sg_w_out = nrm((N_B, SG_WIDTH, D_MODEL), SG_WIDTH ** -0.5 * DN_BETA)

    mla_w_in = nrm((N_C, D_MODEL, MLA_IN_DIM), D_MODEL ** -0.5)
    mla_q_norm_g = gain((N_C, MLA_Q_RANK))
    mla_w_q_b = nrm((N_C, MLA_Q_RANK, MLA_HEADS * (MLA_NOPE + MLA_ROPE)), MLA_Q_RANK ** -0.5)
    mla_kv_norm_g = gain((N_C, MLA_KV_RANK))
    mla_w_kv_b = nrm((N_C, MLA_KV_RANK, MLA_HEADS * (MLA_NOPE + MLA_V)), MLA_KV_RANK ** -0.5)
    mla_w_out = nrm((N_C, MLA_HEADS * MLA_V, D_MODEL), (MLA_HEADS * MLA_V) ** -0.5 * DN_BETA)

    ffn_w_in = nrm((DEPTH, D_MODEL, 2 * FFN_HIDDEN), D_MODEL ** -0.5)
    ffn_conv_w = nrm((DEPTH, FFN_CONV, 2 * FFN_HIDDEN), FFN_CONV ** -0.5)
    ffn_conv_b = nrm((DEPTH, 2 * FFN_HIDDEN), 0.02)
    ffn_w_out = nrm((DEPTH, FFN_HIDDEN, D_MODEL), FFN_HIDDEN ** -0.5 * DN_BETA)

    ln_g = gain((DEPTH, 2, D_MODEL))
    ln_b = nrm((DEPTH, 2, D_MODEL), 0.02)

    return {
        "x": x, "positions": positions,
        "ssm_w_in": ssm_w_in, "ssm_conv_w": ssm_conv_w, "ssm_conv_b": ssm_conv_b,
        "ssm_dt_bias": ssm_dt_bias, "ssm_a_log": ssm_a_log, "ssm_d": ssm_d,
        "ssm_norm_g": ssm_norm_g, "ssm_w_out": ssm_w_out,
        "sg_w_in": sg_w_in, "sg_b_in": sg_b_in, "sg_ln_g": sg_ln_g, "sg_ln_b": sg_ln_b,
        "sg_w_s": sg_w_s, "sg_b_s": sg_b_s, "sg_w_out": sg_w_out,
        "mla_w_in": mla_w_in, "mla_q_norm_g": mla_q_norm_g, "mla_w_q_b": mla_w_q_b,
        "mla_kv_norm_g": mla_kv_norm_g, "mla_w_kv_b": mla_w_kv_b, "mla_w_out": mla_w_out,
        "ffn_w_in": ffn_w_in, "ffn_conv_w": ffn_conv_w, "ffn_conv_b": ffn_conv_b,
        "ffn_w_out": ffn_w_out,
        "ln_g": ln_g, "ln_b": ln_b,
    }


def reference(x, positions,
              ssm_w_in, ssm_conv_w, ssm_conv_b, ssm_dt_bias, ssm_a_log, ssm_d, ssm_norm_g, ssm_w_out,
              sg_w_in, sg_b_in, sg_ln_g, sg_ln_b, sg_w_s, sg_b_s, sg_w_out,
              mla_w_in, mla_q_norm_g, mla_w_q_b, mla_kv_norm_g, mla_w_kv_b, mla_w_out,
              ffn_w_in, ffn_conv_w, ffn_conv_b, ffn_w_out,
              ln_g, ln_b):
    for i in range(DEPTH):
        m, j = i % N_MIXERS, i // N_MIXERS
        if m == 0:
            y = mamba2_mixer(x, ssm_w_in[j], ssm_conv_w[j], ssm_conv_b[j], ssm_dt_bias[j],
                             ssm_a_log[j], ssm_d[j], ssm_norm_g[j], ssm_w_out[j])
        elif m == 1:
            y = spatial_gating_mixer(x, sg_w_in[j], sg_b_in[j], sg_ln_g[j], sg_ln_b[j],
                                     sg_w_s[j], sg_b_s[j], sg_w_out[j])
        else:
            y = mla_mixer(x, positions, mla_w_in[j], mla_q_norm_g[j], mla_w_q_b[j],
                          mla_kv_norm_g[j], mla_w_kv_b[j], mla_w_out[j])
        x = layer_norm(DN_ALPHA * x + y, ln_g[i, 0], ln_b[i, 0])
        f = conv_ffn(x, ffn_w_in[i], ffn_conv_w[i], ffn_conv_b[i], ffn_w_out[i])
        x = layer_norm(DN_ALPHA * x + f, ln_g[i, 1], ln_b[i, 1])
    return x
```

```python
import math, os
from contextlib import ExitStack
import numpy as np
import concourse.bass as bass
import concourse.mybir as mybir
from concourse.bass_utils import run_bass_kernel_spmd

F32 = mybir.dt.float32
BF16 = mybir.dt.bfloat16
I32 = mybir.dt.int32
ALU = mybir.AluOpType
AF = mybir.ActivationFunctionType

D = 1024
L = 4096
DEPTH = 4
ALPHA = (2.0 * DEPTH) ** 0.25
LN_EPS = 1e-5
RMS_EPS = 1e-6
FH = 2816
ENGS = ("pe", "act", "dve", "pool", "sp")
NDSEM = 24
ARENA_ELEMS = 106400


class Buf:
    def __init__(self, name):
        self.name = name
        self.st = {}

    def state(self, key):
        s = self.st.get(key)
        if s is None:
            s = {"w": None, "r": {}}
            self.st[key] = s
        return s


class Prog:
    def __init__(self, nc):
        self.nc = nc
        self.ins = {e: [] for e in ENGS}
        self.clock = {e: {f: 0 for f in ENGS} for e in ENGS}
        self.seen_dma = {e: set() for e in ENGS}
        self.ndma = 0
        self.dma_clock = {}
        self.active = True

    def _deps_for(self, reads, writes):
        deps = []
        for (b, k) in reads:
            keys = (k, None) if k is not None else tuple(b.st.keys()) + (None,)
            for kk in set(keys):
                s = b.st.get(kk)
                if s is not None and s["w"] is not None:
                    deps.append((s["w"], "raw"))
        for (b, k) in writes:
            keys = (k, None) if k is not None else tuple(b.st.keys()) + (None,)
            for kk in set(keys):
                s = b.st.get(kk)
                if s is None:
                    continue
                if s["w"] is not None:
                    deps.append((s["w"], "waw"))
                for t in s["r"].values():
                    deps.append((t, "war"))
        return deps

    def _record(self, tok, reads, writes):
        for (b, k) in reads:
            s = b.state(k)
            s["r"][tok[1] if tok[0] == "c" else tok] = tok
        for (b, k) in writes:
            if k is None:
                b.st = {}
            s = b.state(k)
            s["w"] = tok
            s["r"] = {}

    def _filter(self, eng, deps):
        waits = []
        clk = self.clock[eng]
        for (t, kind) in deps:
            if t[0] == "c":
                _, f, idx = t
                if f == eng and eng in ("pe", "sp"):
                    continue
                if clk[f] >= idx:
                    continue
                waits.append(t)
                oc = self.ins[f][idx - 1]["clock"]
                for g in ENGS:
                    if oc[g] > clk[g]:
                        clk[g] = oc[g]
                if clk[f] < idx:
                    clk[f] = idx
            else:
                if t in self.seen_dma[eng]:
                    continue
                self.seen_dma[eng].add(t)
                waits.append(t)
                oc = self.dma_clock[t]
                for g in ENGS:
                    if oc[g] > clk[g]:
                        clk[g] = oc[g]
        return waits

    def op(self, eng, fn, reads=(), writes=()):
        if not self.active:
            return None
        waits = self._filter(eng, self._deps_for(reads, writes))
        idx = len(self.ins[eng]) + 1
        self.ins[eng].append({"fn": fn, "waits": waits, "dma": None, "sig": False, "clock": dict(self.clock[eng])})
        tok = ("c", eng, idx)
        self._record(tok, reads, writes)
        return tok

    def dma(self, out_ap, in_ap, reads=(), writes=(), q="sp", **kw):
        if not self.active:
            return None
        waits = self._filter(q, self._deps_for(reads, writes))
        n = self.ndma
        self.ndma += 1
        tok = ("d", n)
        self.ins[q].append({"fn": (lambda e: e.dma_start(out=out_ap, in_=in_ap, **kw)), "waits": waits, "dma": n,
                            "sig": False, "clock": dict(self.clock[q])})
        self.dma_clock[tok] = dict(self.clock[q])
        self._record(tok, reads, writes)
        return tok

    def barrier(self):
        last = {e: max([i + 1 for i, r in enumerate(self.ins[e]) if r["fn"] is not None and r["dma"] is None] or [0]) for e in ENGS}
        nd = self.ndma
        for e in ENGS:
            deps = [(("c", f, last[f]), "raw") for f in ENGS if (f != e or e not in ("pe", "sp")) and f != "sp" and last[f] > 0]
            deps += [(("d", n), "raw") for n in range(max(0, nd - NDSEM), nd)]
            waits = self._filter(e, deps)
            if waits:
                self.ins[e].append({"fn": None, "waits": waits, "dma": None, "sig": False, "clock": dict(self.clock[e])})

    def finalize(self, block_ctx, sems, dsems):
        for e in ENGS:
            for rec in self.ins[e]:
                for t in rec["waits"]:
                    if t[0] == "c":
                        self.ins[t[1]][t[2] - 1]["sig"] = True
        sigval = {}
        for e in ENGS:
            c = 0
            for i, rec in enumerate(self.ins[e]):
                if rec["sig"]:
                    c += 1
                    sigval[(e, i + 1)] = c
        ndma = self.ndma

        def play(eng_name):
            def body(e):
                for rec in self.ins[eng_name]:
                    for t in rec["waits"]:
                        if t[0] == "c":
                            e.wait_ge(sems[t[1]], sigval[(t[1], t[2])])
                        else:
                            n = t[1]
                            e.wait_ge(dsems[n % NDSEM], 16 * (n // NDSEM + 1))
                    if rec["fn"] is None:
                        continue
                    if rec["dma"] is not None:
                        n = rec["dma"]
                        if n >= NDSEM:
                            e.wait_ge(dsems[n % NDSEM], 16 * (n // NDSEM))
                        rec["fn"](e).then_inc(dsems[n % NDSEM], 16)
                    else:
                        ins = rec["fn"](e)
                        if rec["sig"]:
                            ins.then_inc(sems[eng_name], 1)
                if eng_name == "sp":
                    for j in range(min(NDSEM, ndma)):
                        e.wait_ge(dsems[j], 16 * ((ndma - 1 - j) // NDSEM + 1))
            return body

        block_ctx.tensor(play("pe"))
        block_ctx.scalar(play("act"))
        block_ctx.vector(play("dve"))
        block_ctx.gpsimd(play("pool"))
        block_ctx.sync(play("sp"))


class StopBuild(Exception):
    pass


def stop_at(n):
    if int(os.environ.get('K_STOP', 99)) == n:
        raise StopBuild()


class T:
    def __init__(self, ap, name):
        self.ap = ap
        self.b = Buf(name)

    def __getitem__(self, k):
        return self.ap[k]


class KB:
    def __init__(self, nc, P, arena, ps, dram):
        self.nc, self.P, self.arena, self.ps, self.dram = nc, P, arena, ps, dram
        self.psb = ps[:].bitcast(BF16)
        self.PB = [Buf("psum%d" % i) for i in range(8)]
        self.off = 0
        self.base = 0
        self.cast_rr = 0

    def alloc(self, name, shape, dt, parts=128):
        size = {F32: 4, BF16: 2, I32: 4}[dt]
        n = int(np.prod(shape))
        nb = (n * size + 63) // 64 * 64
        e0 = self.off // 2
        self.off += nb
        assert self.off <= ARENA_ELEMS * 2, ("arena overflow", name, self.off)
        v = self.arena[0:parts, e0:e0 + n * size // 2]
        if dt != BF16:
            v = v.bitcast(dt)
        if len(shape) == 2:
            v = v.rearrange("p (a b) -> p a b", a=shape[0])
        elif len(shape) == 3:
            v = v.rearrange("p (a b c) -> p a b c", a=shape[0], b=shape[1])
        return T(v, name)

    def phase(self):
        self.P.barrier()
        self.off = self.base

    def bank(self, b, n=512, p0=0, p1=128, c0=0):
        return self.ps[p0:p1, b * 512 + c0: b * 512 + c0 + n]

    def bankb(self, b, n=1024, p0=0, p1=128, c0=0):
        return self.psb[p0:p1, b * 1024 + c0: b * 1024 + c0 + n]

    def setup_consts(self):
        P = self.P
        c = self.dram["consts"]
        self.idf = self.alloc("idf", [128], F32)
        self.masks = self.alloc("masks", [5, 128], F32)
        self.idb = self.alloc("idb", [128], BF16)
        self.mhalf = self.alloc("mhalf", [1], F32)
        P.dma(self.idf.ap, c[:, 0:128], writes=[(self.idf.b, None)])
        P.dma(self.masks.ap, c[:, 128:768].rearrange("p (a b) -> p a b", a=5), writes=[(self.masks.b, None)])
        P.op("dve", lambda e: e.tensor_copy(out=self.idb.ap, in_=self.idf.ap), reads=[(self.idf.b, None)], writes=[(self.idb.b, None)])
        P.op("pool", lambda e: e.memset(self.mhalf.ap, -0.5), writes=[(self.mhalf.b, None)])
        self.base = self.off

    def load_w(self, dst, src3, C, N, stg):
        P = self.P
        ncol = max(1, 2048 // N) if N <= 2048 else 1
        nsplit = (N + 2047) // 2048
        i = 0
        for c0 in range(0, C, ncol):
            c1 = min(C, c0 + ncol)
            for s in range(nsplit):
                n0 = s * 2048
                n1 = min(N, n0 + 2048)
                st = stg[self.cast_rr % len(stg)]
                w = (c1 - c0) * (n1 - n0)
                sv = st.ap[:, 0:w].rearrange("p (a b) -> p a b", a=c1 - c0)
                P.dma(sv, src3[:, c0:c1, n0:n1], writes=[(st.b, None)])
                eng = ("act", "dve")[self.cast_rr % 2]
                dv = dst.ap[:, c0:c1, n0:n1]
                if eng == "act":
                    P.op(eng, lambda e, dv=dv, sv=sv: e.copy(out=dv, in_=sv), reads=[(st.b, None)], writes=[(dst.b, None)])
                else:
                    P.op(eng, lambda e, dv=dv, sv=sv: e.tensor_copy(out=dv, in_=sv), reads=[(st.b, None)], writes=[(dst.b, None)])
                self.cast_rr += 1

    def bcast_load(self, name, src_row, n):
        t = self.alloc(name, [n], F32)
        self.P.dma(t.ap, src_row.partition_broadcast(128), writes=[(t.b, None)])
        return t

    def make_xT_block(self, xin, tok0, xres_view, xres_buf, xres_key, xbf, xT, xT_key, col0, tbank):
        P = self.P
        P.dma(xres_view, xin[tok0:tok0 + 128, :], writes=[(xres_buf, xres_key)])
        P.op("act", lambda e: e.copy(out=xbf.ap, in_=xres_view), reads=[(xres_buf, xres_key)], writes=[(xbf.b, None)])
        for c in range(8):
            P.op("pe", lambda e, c=c: e.transpose(out=self.bankb(tbank, 128, c0=c * 128), in_=xbf.ap[:, c * 128:(c + 1) * 128], identity=self.idb.ap),
                 reads=[(xbf.b, None), (self.idb.b, None)], writes=[(self.PB[tbank], None)])
        src = self.bankb(tbank, 1024).rearrange("p (c t) -> p c t", c=8)
        P.op("act", lambda e: e.copy(out=xT.ap[:, 0:8, col0:col0 + 128], in_=src), reads=[(self.PB[tbank], None)], writes=[(xT.b, xT_key)])

    def epilogue(self, ybank, xres_view, xres_dep, lng, lnb, xout, tok0, tmp):
        P = self.P
        r, o, st, mv, sc = tmp["r"], tmp["o"], tmp["st"], tmp["mv"], tmp["sc"]
        y = self.ps[:, ybank * 512: ybank * 512 + 1024]
        P.op("dve", lambda e: e.scalar_tensor_tensor(out=r.ap, in0=xres_view, scalar=ALPHA, in1=y, op0=ALU.mult, op1=ALU.add),
             reads=[xres_dep, (self.PB[ybank], None), (self.PB[ybank + 1], None)], writes=[(r.b, None)])
        self.layernorm(r, 1024, lng, lnb, o, st, mv, sc, LN_EPS)
        P.dma(xout[tok0:tok0 + 128, :], o.ap, reads=[(o.b, None)], writes=[(self.xbuf_dep[id(xout)], tok0)])

    def layernorm(self, r, n, lng, lnb, o, st, mv, sc, eps, out_ap=None, out_buf=None):
        P = self.P
        nchunk = n // 512
        for i in range(nchunk):
            P.op("dve", lambda e, i=i: e.bn_stats(out=st.ap[:, i * 6:(i + 1) * 6], in_=r.ap[:, i * 512:(i + 1) * 512]),
                 reads=[(r.b, None)], writes=[(st.b, i)])
        P.op("dve", lambda e: e.bn_aggr(out=mv.ap, in_=st.ap[:, 0:6 * nchunk]), reads=[(st.b, None)], writes=[(mv.b, None)])
        P.op("pool", lambda e: e.tensor_scalar(out=sc.ap[:, 0:1], in0=mv.ap[:, 1:2], scalar1=1.0, scalar2=eps, op0=ALU.mult, op1=ALU.add),
             reads=[(mv.b, None)], writes=[(sc.b, 0)])
        P.op("pool", lambda e: e.tensor_tensor(out=sc.ap[:, 1:2], in0=sc.ap[:, 0:1], in1=self.mhalf.ap, op=ALU.pow),
             reads=[(sc.b, 0), (self.mhalf.b, None)], writes=[(sc.b, 1)])
        P.op("dve", lambda e: e.tensor_scalar(out=sc.ap[:, 2:3], in0=mv.ap[:, 0:1], scalar1=sc.ap[:, 1:2], scalar2=-1.0, op0=ALU.mult, op1=ALU.mult),
             reads=[(mv.b, None), (sc.b, 1)], writes=[(sc.b, 2)])
        P.op("act", lambda e: e.activation(out=r.ap, in_=r.ap, func=AF.Identity, scale=sc.ap[:, 1:2], bias=sc.ap[:, 2:3]),
             reads=[(r.b, None), (sc.b, 1), (sc.b, 2)], writes=[(r.b, None)])
        P.op("pool", lambda e: e.tensor_tensor(out=o.ap, in0=r.ap, in1=lng.ap, op=ALU.mult), reads=[(r.b, None), (lng.b, None)], writes=[(o.b, None)])
        oo = o.ap if out_ap is None else out_ap
        ob = o if out_buf is None else out_buf
        P.op("dve", lambda e: e.tensor_tensor(out=oo, in0=o.ap, in1=lnb.ap, op=ALU.add), reads=[(o.b, None), (lnb.b, None)], writes=[(ob.b, None)])

    def ln_tmp(self, tag, inplace=False):
        r = self.alloc("r" + tag, [1024], F32)
        return {"r": r, "o": r if inplace else self.alloc("o" + tag, [1024], F32),
                "st": self.alloc("st" + tag, [24], F32), "mv": self.alloc("mv" + tag, [2], F32), "sc": self.alloc("sc" + tag, [4], F32)}

    def ffn_phase(self, li, xin, xout):
        P, d = self.P, self.dram
        self.phase()
        TT, PAD = 256, 2
        w1 = self.alloc("w1", [8, 2 * FH], BF16)
        w2 = self.alloc("w2", [22, D], BF16)
        cw = self.alloc("cw", [44, 3], F32)
        cb = self.alloc("cb", [44], F32)
        lng = self.bcast_load("lng", d["ln_g"][li * 2 + 1:li * 2 + 2, :], D)
        lnb = self.bcast_load("lnb", d["ln_b"][li * 2 + 1:li * 2 + 2, :], D)
        mark = self.off
        stg = [self.alloc("stg%d" % i, [2048], F32) for i in range(3)]
        P.dma(cw.ap, d["ffn_cw"][li], writes=[(cw.b, None)])
        P.dma(cb.ap, d["ffn_cb"][li], writes=[(cb.b, None)])
        self.load_w(w1, d["ffn_w_in"][li].rearrange("(c p) n -> p c n", p=128), 8, 2 * FH, stg)
        self.load_w(w2, d["ffn_w_out"][li].rearrange("(c p) n -> p c n", p=128), 22, D, stg)
        P.barrier()
        self.off = mark
        xres = [self.alloc("xres%d" % i, [2, D], F32) for i in range(2)]
        xbf = [self.alloc("xbf%d" % i, [D], BF16) for i in range(2)]
        xT = [self.alloc("xT%d" % i, [8, TT + PAD], BF16) for i in range(2)]
        acc = [self.alloc("acc%d" % i, [TT], F32) for i in range(4)]
        sg = [self.alloc("sg%d" % i, [TT], F32) for i in range(2)]
        pTs = [self.alloc("pT%d" % i, [22, TT], BF16) for i in range(2)]
        tmp = self.ln_tmp("f", inplace=True)
        P.op("pool", lambda e: e.memset(xT[0].ap[:, :, 0:PAD], 0.0), writes=[(xT[0].b, "pad")])
        TB, UP, DB = 0, (1, 2, 3, 4), 5
        NTF = int(os.environ.get('K_NTF', L // TT))
        jjb = [0]

        def load_tile(t):
            cur = xT[t % 2]
            for blk in range(2):
                self.make_xT_block(xin, t * TT + blk * 128, xres[t % 2].ap[:, blk, :], xres[t % 2].b, blk, xbf[blk], cur, blk, PAD + blk * 128, TB)
            if t > 0:
                prv = xT[(t - 1) % 2]
                P.op("pool", lambda e, cur=cur, prv=prv: e.tensor_copy(out=cur.ap[:, :, 0:PAD], in_=prv.ap[:, :, TT:TT + PAD]),
                     reads=[(prv.b, 1)], writes=[(cur.b, "pad")])

        def down_epi(t):
            pT = pTs[t % 2]
            for blk in range(2):
                for half in range(2):
                    for c in range(22):
                        P.op("pe", lambda e, c=c, half=half, blk=blk, pT=pT: e.matmul(out=self.bank(DB + half), lhsT=pT.ap[:, c, blk * 128:(blk + 1) * 128],
                                                                                      rhs=w2.ap[:, c, half * 512:(half + 1) * 512], start=(c == 0), stop=(c == 21)),
                             reads=[(pT.b, c), (w2.b, None)], writes=[(self.PB[DB + half], None)])
                self.epilogue(DB, xres[t % 2].ap[:, blk, :], (xres[t % 2].b, blk), lng, lnb, xout, t * TT + blk * 128, tmp)

        def up_chunk(t, c):
            cur = xT[t % 2]
            pT = pTs[t % 2]
            for which in range(2):
                j = which * 22 + c
                bk = UP[jjb[0] % 4]
                a = acc[jjb[0] % 4]
                jjb[0] += 1
                for k in range(8):
                    P.op("pe", lambda e, k=k, j=j, bk=bk, cur=cur: e.matmul(out=self.bank(bk, TT + PAD), lhsT=w1.ap[:, k, j * 128:(j + 1) * 128],
                                                                           rhs=cur.ap[:, k, :], start=(k == 0), stop=(k == 7)),
                         reads=[(w1.b, None), (cur.b, None)], writes=[(self.PB[bk], None)])
                P.op("act", lambda e, j=j, bk=bk, a=a: e.activation(out=a.ap, in_=self.bank(bk, TT, c0=2), func=AF.Identity,
                                                                     scale=cw.ap[:, j, 2:3], bias=cb.ap[:, j:j + 1]),
                     reads=[(self.PB[bk], None), (cw.b, None), (cb.b, None)], writes=[(a.b, None)])
                for kk in (1, 0):
                    P.op("dve", lambda e, j=j, bk=bk, a=a, kk=kk: e.scalar_tensor_tensor(out=a.ap, in0=self.bank(bk, TT, c0=kk), scalar=cw.ap[:, j, kk:kk + 1],
                                                                                        in1=a.ap, op0=ALU.mult, op1=ALU.add),
                         reads=[(self.PB[bk], None), (cw.b, None), (a.b, None)], writes=[(a.b, None)])
                s_ = sg[c % 2]
                if which == 0:
                    P.op("act", lambda e, a=a, s_=s_: e.activation(out=s_.ap, in_=a.ap, func=AF.Silu), reads=[(a.b, None)], writes=[(s_.b, None)])
                else:
                    P.op("pool", lambda e, a=a, s_=s_, c=c, pT=pT: e.tensor_tensor(out=pT.ap[:, c, :], in0=s_.ap, in1=a.ap, op=ALU.mult),
                         reads=[(a.b, None), (s_.b, None)], writes=[(pT.b, c)])

        load_tile(0)
        for t in range(NTF):
            for c in range(22):
                if c == 6 and t > 0:
                    down_epi(t - 1)
                if c == 14 and t + 1 < NTF:
                    load_tile(t + 1)
                up_chunk(t, c)
        down_epi(NTF - 1)

    def ssd_phase(self, li, j, xin, xout):
        P, d = self.P, self.dram
        self.phase()
        PAD = 3
        PB = self.PB
        win = self.alloc("win", [8, 6176], BF16)
        wout = self.alloc("wout", [16, D], BF16)
        cw = self.alloc("scw", [32, 4], F32)
        cb = self.alloc("scb", [32], F32)
        small = self.bcast_load("ssmall", d["ssm_small"][j:j + 1, :], 96)
        ng = self.bcast_load("sng", d["ssm_norm_g"][j:j + 1, :], 2048)
        lng = self.bcast_load("lng", d["ln_g"][li * 2:li * 2 + 1, :], D)
        lnb = self.bcast_load("lnb", d["ln_b"][li * 2:li * 2 + 1, :], D)
        mark = self.off
        stg = [self.alloc("stg%d" % i, [2048], F32) for i in range(3)]
        P.dma(cw.ap, d["ssm_cw"][j], writes=[(cw.b, None)])
        P.dma(cb.ap, d["ssm_cb"][j], writes=[(cb.b, None)])
        self.load_w(win, d["ssm_w_in"][j].rearrange("(c p) n -> p c n", p=128), 8, 6176, stg)
        self.load_w(wout, d["ssm_w_out"][j].rearrange("(c p) n -> p c n", p=128), 16, D, stg)
        P.barrier()
        self.off = mark
        A_b = self.alloc("A_b", [32], F32)
        H = self.alloc("H", [2048], F32)
        Hb0 = self.alloc("Hb0", [2048], BF16)
        Hb1 = self.alloc("Hb1", [2048], BF16)
        xres = [self.alloc("xres%d" % i, [D], F32) for i in range(2)]
        xbf = self.alloc("xbf", [D], BF16)
        xT = [self.alloc("xT%d" % i, [8, 128 + PAD], BF16) for i in range(2)]
        acc = [self.alloc("acc%d" % i, [128], F32) for i in range(4)]
        xsf = [self.alloc("xsf%d" % i, [128], F32) for i in range(2)]
        BT = [self.alloc("BT%d" % i, [128], BF16) for i in range(2)]
        CT = [self.alloc("CT%d" % i, [128], BF16) for i in range(2)]
        dtt = self.alloc("dtt", [32], F32)
        dte = self.alloc("dte", [32], F32)
        dt_ = self.alloc("dt_", [32], F32)
        dtA = self.alloc("dtA", [32], F32)
        ex = self.alloc("ex", [4, 32], F32)
        dtw = self.alloc("dtw", [32], F32)
        Lg = self.alloc("Lg", [4, 128], F32)
        E = self.alloc("E", [4, 128], F32)
        CBm = self.alloc("CBm", [128], F32)
        M = self.alloc("M", [4, 128], BF16)
        xs_tok = self.alloc("xs_tok", [4, 64], F32)
        Btok = self.alloc("Btok", [128], BF16)
        zs = self.alloc("zs", [256], F32)
        xdt = self.alloc("xdt", [4, 64], BF16)
        xw = self.alloc("xw", [2, 4, 64], BF16)
        dtw2 = self.alloc("dtw2", [2, 32], F32)
        t1 = self.alloc("t1", [4, 64], F32)
        t2 = self.alloc("t2", [4, 64], F32)
        ss = self.alloc("ss", [4], F32)
        hcd = self.alloc("hcd", [4, 64], F32)
        yn = self.alloc("yn", [256], BF16)
        ynT = self.alloc("ynT", [16, 128], BF16)
        tmp = self.ln_tmp("s", inplace=True)
        dtb, alog, dsk = small.ap[:, 0:32], small.ap[:, 32:64], small.ap[:, 64:96]
        Mle, Mgt = self.masks.ap[:, 0, :], self.masks.ap[:, 1, :]
        P.op("act", lambda e: e.activation(out=A_b.ap, in_=alog, func=AF.Exp), reads=[(small.b, None)], writes=[(A_b.b, None)])
        P.op("dve", lambda e: e.tensor_scalar(out=A_b.ap, in0=A_b.ap, scalar1=-1.0, scalar2=None, op0=ALU.mult), reads=[(A_b.b, None)], writes=[(A_b.b, None)])
        P.op("pool", lambda e: e.memset(H.ap, 0.0), writes=[(H.b, None)])
        P.op("pool", lambda e: e.memset(Hb0.ap, 0.0), writes=[(Hb0.b, None)])
        P.op("pool", lambda e: e.memset(xT[0].ap[:, :, 0:PAD], 0.0), writes=[(xT[0].b, "pad")])
        cc = 0
        for t in range(int(os.environ.get('K_NT', L // 128))):
            cur = xT[t % 2]
            xr = xres[t % 2]
            self.make_xT_block(xin, t * 128, xr.ap, xr.b, None, xbf, cur, "blk", PAD, 0)
            if t > 0:
                prv = xT[(t - 1) % 2]
                P.op("pool", lambda e, cur=cur, prv=prv: e.tensor_copy(out=cur.ap[:, :, 0:PAD], in_=prv.ap[:, :, 128:128 + PAD]),
                     reads=[(prv.b, "blk")], writes=[(cur.b, "pad")])
            for k in range(8):
                P.op("pe", lambda e, k=k, cur=cur: e.matmul(out=self.bank(3, 32), lhsT=cur.ap[:, k, PAD:PAD + 128], rhs=win.ap[:, k, 6144:6176],
                                                          start=(k == 0), stop=(k == 7)),
                     reads=[(cur.b, None), (win.b, None)], writes=[(PB[3], None)])
            P.op("dve", lambda e: e.tensor_tensor(out=dtt.ap, in0=self.bank(3, 32), in1=dtb, op=ALU.add), reads=[(PB[3], None), (small.b, None)], writes=[(dtt.b, None)])
            P.op("act", lambda e: e.activation(out=dte.ap, in_=dtt.ap, func=AF.Exp), reads=[(dtt.b, None)], writes=[(dte.b, None)])
            P.op("act", lambda e: e.activation(out=dt_.ap, in_=dte.ap, func=AF.Ln, bias=1.0), reads=[(dte.b, None)], writes=[(dt_.b, None)])
            P.op("dve", lambda e: e.tensor_tensor(out=dtA.ap, in0=dt_.ap, in1=A_b.ap, op=ALU.mult), reads=[(dt_.b, None), (A_b.b, None)], writes=[(dtA.b, None)])
            for m in range(4):
                P.op("pe", lambda e, m=m: e.matmul(out=self.bank(3, 32, c0=64 + m * 32), lhsT=self.masks.ap[:, m, :], rhs=dtA.ap, start=True, stop=True),
                     reads=[(self.masks.b, None), (dtA.b, None)], writes=[(PB[3], None)])
            P.op("act", lambda e: e.activation(out=ex.ap, in_=self.bank(3, 128, c0=64).rearrange("p (a b) -> p a b", a=4), func=AF.Exp),
                 reads=[(PB[3], None)], writes=[(ex.b, None)])
            P.op("dve", lambda e: e.tensor_tensor(out=dtw.ap, in0=dt_.ap, in1=ex.ap[:, 1, :], op=ALU.mult), reads=[(dt_.b, None), (ex.b, None)], writes=[(dtw.b, None)])
            stop_at(1)
            for c_ in range(2):
                P.op("dve", lambda e, c_=c_: e.tensor_scalar(out=dtw2.ap[:, c_, :], in0=dtw.ap, scalar1=self.masks.ap[:, 2 + c_, 0:1], scalar2=None, op0=ALU.mult),
                     reads=[(dtw.b, None), (self.masks.b, None)], writes=[(dtw2.b, c_)])
            for g in range(8):
                hs = slice(4 * g, 4 * g + 4)
                bt, ct = BT[g % 2], CT[g % 2]
                chunks = [(2048 + (2 * g) * 128, xsf[0]), (2048 + (2 * g + 1) * 128, xsf[1]), (4096 + g * 128, bt), (5120 + g * 128, ct)]
                for (col, dst) in chunks:
                    jc = (col - 2048) // 128
                    bk = (1, 2)[cc % 2]
                    co = 0
                    a = acc[cc % 4]
                    cc += 1
                    for k in range(8):
                        P.op("pe", lambda e, k=k, col=col, bk=bk, cur=cur, co=co: e.matmul(out=self.bank(bk, 128 + PAD, c0=co), lhsT=win.ap[:, k, col:col + 128], rhs=cur.ap[:, k, :],
                                                                                   start=(k == 0), stop=(k == 7)),
                             reads=[(win.b, None), (cur.b, None)], writes=[(PB[bk], None)])
                    P.op("act", lambda e, jc=jc, bk=bk, a=a, co=co: e.activation(out=a.ap, in_=self.bank(bk, 128, c0=co + 3), func=AF.Identity,
                                                                           scale=cw.ap[:, jc, 3:4], bias=cb.ap[:, jc:jc + 1]),
                         reads=[(PB[bk], None), (cw.b, None), (cb.b, None)], writes=[(a.b, None)])
                    for kk in (2, 1, 0):
                        P.op("dve", lambda e, jc=jc, bk=bk, a=a, kk=kk, co=co: e.scalar_tensor_tensor(out=a.ap, in0=self.bank(bk, 128, c0=co + kk), scalar=cw.ap[:, jc, kk:kk + 1],
                                                                                              in1=a.ap, op0=ALU.mult, op1=ALU.add),
                             reads=[(PB[bk], None), (cw.b, None), (a.b, None)], writes=[(a.b, None)])
                    P.op("act", lambda e, a=a, dst=dst: e.activation(out=dst.ap, in_=a.ap, func=AF.Silu), reads=[(a.b, None)], writes=[(dst.b, None)])
                stop_at(2)
                for i in range(2):
                    P.op("pe", lambda e, i=i: e.transpose(out=self.bank(5, 128, c0=128 + i * 128), in_=xsf[i].ap, identity=self.idf.ap),
                         reads=[(xsf[i].b, None), (self.idf.b, None)], writes=[(PB[5], None)])
                P.op("act", lambda e: e.copy(out=xs_tok.ap.rearrange("p a b -> p (a b)"), in_=self.bank(5, 256, c0=128)), reads=[(PB[5], None)], writes=[(xs_tok.b, None)])
                P.op("pe", lambda e, bt=bt: e.transpose(out=self.bankb(5, 128, c0=768), in_=bt.ap, identity=self.idb.ap),
                     reads=[(bt.b, None), (self.idb.b, None)], writes=[(PB[5], None)])
                P.op("dve", lambda e: e.tensor_copy(out=Btok.ap, in_=self.bankb(5, 128, c0=768)), reads=[(PB[5], None)], writes=[(Btok.b, None)])
                stop_at(3)
                for k in range(8):
                    P.op("pe", lambda e, k=k, g=g, cur=cur: e.matmul(out=self.bank(6, 256), lhsT=cur.ap[:, k, PAD:PAD + 128], rhs=win.ap[:, k, g * 256:(g + 1) * 256],
                                                                   start=(k == 0), stop=(k == 7)),
                         reads=[(cur.b, None), (win.b, None)], writes=[(PB[6], None)])
                P.op("act", lambda e: e.activation(out=zs.ap, in_=self.bank(6, 256), func=AF.Silu), reads=[(PB[6], None)], writes=[(zs.b, None)])
                stop_at(4)
                P.op("pool", lambda e, hs=hs: e.tensor_tensor(out=xdt.ap, in0=xs_tok.ap, in1=dt_.ap[:, hs].unsqueeze(2).to_broadcast([128, 4, 64]), op=ALU.mult),
                     reads=[(xs_tok.b, None), (dt_.b, None)], writes=[(xdt.b, None)])
                for c_ in range(2):
                    P.op("pool", lambda e, hs=hs, c_=c_: e.tensor_tensor(out=xw.ap[:, c_], in0=xs_tok.ap, in1=dtw2.ap[:, c_, hs].unsqueeze(2).to_broadcast([128, 4, 64]), op=ALU.mult),
                         reads=[(xs_tok.b, None), (dtw2.b, None)], writes=[(xw.b, c_)])
                stop_at(5)
                P.op("pool", lambda e, hs=hs: e.tensor_tensor(out=Lg.ap, in0=self.masks.ap[:, 1:2, :].to_broadcast([128, 4, 128]),
                                                              in1=dtA.ap[:, hs].unsqueeze(2).to_broadcast([128, 4, 128]), op=ALU.mult),
                     reads=[(self.masks.b, None), (dtA.b, None)], writes=[(Lg.b, None)])
                for hh in range(4):
                    P.op("pe", lambda e, hh=hh: e.matmul(out=self.bank(4, 128, c0=hh * 128), lhsT=Lg.ap[:, hh, :], rhs=Mle, start=True, stop=True),
                         reads=[(Lg.b, None), (self.masks.b, None)], writes=[(PB[4], None)])
                P.op("act", lambda e: e.activation(out=E.ap.rearrange("p a b -> p (a b)"), in_=self.bank(4, 512), func=AF.Exp), reads=[(PB[4], None)], writes=[(E.b, None)])
                stop_at(6)
                P.op("pe", lambda e, bt=bt, ct=ct: e.matmul(out=self.bank(5, 128), lhsT=bt.ap, rhs=ct.ap, start=True, stop=True),
                     reads=[(bt.b, None), (ct.b, None)], writes=[(PB[5], None)])
                P.op("dve", lambda e: e.tensor_tensor(out=CBm.ap, in0=self.bank(5, 128), in1=Mle, op=ALU.mult), reads=[(PB[5], None), (self.masks.b, None)], writes=[(CBm.b, None)])
                P.op("dve", lambda e: e.tensor_tensor(out=M.ap, in0=E.ap, in1=CBm.ap.unsqueeze(1).to_broadcast([128, 4, 128]), op=ALU.mult),
                     reads=[(E.b, None), (CBm.b, None)], writes=[(M.b, None)])
                for hh in range(4):
                    P.op("pe", lambda e, hh=hh: e.matmul(out=self.bank(6, 64, c0=256 + hh * 64), lhsT=M.ap[:, hh, :], rhs=xdt.ap[:, hh, :], start=True, stop=True),
                         reads=[(M.b, None), (xdt.b, None)], writes=[(PB[6], None)])
                stop_at(7)
                P.op("pe", lambda e: e.matmul(out=self.bank(7, 256, c0=256), lhsT=Btok.ap, rhs=xw.ap[:, 0].rearrange("p a b -> p (a b)"), start=True, stop=True),
                     reads=[(Btok.b, None), (xw.b, None)], writes=[(PB[7], None)])
                P.op("pe", lambda e: e.matmul(out=self.bank(3, 256, c0=256), lhsT=Btok.ap, rhs=xw.ap[:, 1].rearrange("p a b -> p (a b)"), start=True, stop=True),
                     reads=[(Btok.b, None), (xw.b, None)], writes=[(PB[3], None)])
                stop_at(8)
                for hh in range(4):
                    h = 4 * g + hh
                    P.op("pe", lambda e, hh=hh, h=h, ct=ct: e.matmul(out=self.bank(7, 64, p0=0, p1=64, c0=hh * 64), lhsT=ct.ap[:, 0:64], rhs=Hb0.ap[:, h * 64:(h + 1) * 64],
                                                                      start=True, stop=True),
                         reads=[(ct.b, None), (Hb0.b, g)], writes=[(PB[7], None)])
                stop_at(81)
                Hg = H.ap[:, g * 256:(g + 1) * 256].rearrange("p (a b) -> p a b", a=4)
                Hgf = H.ap[:, g * 256:(g + 1) * 256]
                P.op("dve", lambda e, hs=hs, Hg=Hg: e.tensor_tensor(out=hcd.ap, in0=Hg, in1=ex.ap[:, 2, hs].unsqueeze(2).to_broadcast([128, 4, 64]), op=ALU.mult),
                     reads=[(H.b, g), (ex.b, None)], writes=[(hcd.b, None)])
                stop_at(811)
                P.op("dve", lambda e, Hgf=Hgf: e.scalar_tensor_tensor(out=Hgf, in0=self.bank(7, 256, c0=256), scalar=1.0, in1=hcd.ap.rearrange("p a b -> p (a b)"), op0=ALU.mult, op1=ALU.add),
                     reads=[(hcd.b, None), (PB[7], None)], writes=[(H.b, g)])
                stop_at(812)
                P.op("act", lambda e, g=g, Hgf=Hgf: e.copy(out=Hb1.ap[:, g * 256:(g + 1) * 256], in_=Hgf), reads=[(H.b, g)], writes=[(Hb1.b, g)])
                stop_at(82)
                for hh in range(4):
                    h = 4 * g + hh
                    P.op("pe", lambda e, hh=hh, h=h, ct=ct: e.matmul(out=self.bank(7, 64, p0=64, p1=128, c0=hh * 64), lhsT=ct.ap[:, 64:128], rhs=Hb1.ap[:, h * 64:(h + 1) * 64],
                                                                      start=True, stop=True),
                         reads=[(ct.b, None), (Hb1.b, g)], writes=[(PB[7], None)])
                P.op("dve", lambda e, hs=hs, Hg=Hg: e.tensor_tensor(out=hcd.ap, in0=Hg, in1=ex.ap[:, 3, hs].unsqueeze(2).to_broadcast([128, 4, 64]), op=ALU.mult),
                     reads=[(H.b, g), (ex.b, None)], writes=[(hcd.b, None)])
                P.op("dve", lambda e, Hgf=Hgf: e.scalar_tensor_tensor(out=Hgf, in0=self.bank(3, 256, c0=256), scalar=1.0, in1=hcd.ap.rearrange("p a b -> p (a b)"), op0=ALU.mult, op1=ALU.add),
                     reads=[(hcd.b, None), (PB[3], None)], writes=[(H.b, g)])
                P.op("act", lambda e, g=g, Hgf=Hgf: e.copy(out=Hb0.ap[:, g * 256:(g + 1) * 256], in_=Hgf), reads=[(H.b, g)], writes=[(Hb0.b, g)])
                stop_at(9)
                P.op("dve", lambda e, hs=hs: e.tensor_tensor(out=t1.ap, in0=self.bank(7, 256).rearrange("p (a b) -> p a b", a=4),
                                                             in1=ex.ap[:, 0, hs].unsqueeze(2).to_broadcast([128, 4, 64]), op=ALU.mult),
                     reads=[(PB[7], None), (PB[7], None), (ex.b, None)], writes=[(t1.b, None)])
                P.op("dve", lambda e: e.tensor_tensor(out=t1.ap.rearrange("p a b -> p (a b)"), in0=t1.ap.rearrange("p a b -> p (a b)"), in1=self.bank(6, 256, c0=256), op=ALU.add),
                     reads=[(t1.b, None), (PB[6], None)], writes=[(t1.b, None)])
                P.op("pool", lambda e, hs=hs: e.tensor_tensor(out=t2.ap, in0=xs_tok.ap, in1=dsk[:, hs].unsqueeze(2).to_broadcast([128, 4, 64]), op=ALU.mult),
                     reads=[(xs_tok.b, None), (small.b, None)], writes=[(t2.b, None)])
                P.op("pool", lambda e: e.tensor_tensor(out=t1.ap, in0=t1.ap, in1=t2.ap, op=ALU.add), reads=[(t1.b, None), (t2.b, None)], writes=[(t1.b, None)])
                P.op("pool", lambda e: e.tensor_tensor(out=t1.ap.rearrange("p a b -> p (a b)"), in0=t1.ap.rearrange("p a b -> p (a b)"), in1=zs.ap, op=ALU.mult),
                     reads=[(t1.b, None), (zs.b, None)], writes=[(t1.b, None)])
                P.op("act", lambda e: e.activation(out=t2.ap.rearrange("p a b -> p (a b)"), in_=t1.ap.rearrange("p a b -> p (a b)"), func=AF.Square, accum_out=ss.ap[:, 0:1]),
                     reads=[(t1.b, None)], writes=[(t2.b, None), (ss.b, 0)])
                P.op("pool", lambda e: e.tensor_scalar(out=ss.ap[:, 1:2], in0=ss.ap[:, 0:1], scalar1=1.0 / 256.0, scalar2=RMS_EPS, op0=ALU.mult, op1=ALU.add),
                     reads=[(ss.b, 0)], writes=[(ss.b, 1)])
                P.op("pool", lambda e: e.tensor_tensor(out=ss.ap[:, 2:3], in0=ss.ap[:, 1:2], in1=self.mhalf.ap, op=ALU.pow),
                     reads=[(ss.b, 1), (self.mhalf.b, None)], writes=[(ss.b, 2)])
                P.op("dve", lambda e, g=g: e.scalar_tensor_tensor(out=yn.ap, in0=t1.ap.rearrange("p a b -> p (a b)"), scalar=ss.ap[:, 2:3], in1=ng.ap[:, g * 256:(g + 1) * 256],
                                                                  op0=ALU.mult, op1=ALU.mult),
                     reads=[(t1.b, None), (ss.b, 2), (ng.b, None)], writes=[(yn.b, None)])
                for i in range(2):
                    P.op("pe", lambda e, i=i: e.transpose(out=self.bankb(0, 128, c0=i * 128), in_=yn.ap[:, i * 128:(i + 1) * 128], identity=self.idb.ap),
                         reads=[(yn.b, None), (self.idb.b, None)], writes=[(PB[0], None)])
                P.op("act", lambda e, g=g: e.copy(out=ynT.ap[:, 2 * g:2 * g + 2, :], in_=self.bankb(0, 256).rearrange("p (a b) -> p a b", a=2)),
                     reads=[(PB[0], None)], writes=[(ynT.b, g)])
            for half in range(2):
                for c in range(16):
                    P.op("pe", lambda e, c=c, half=half: e.matmul(out=self.bank(1 + half), lhsT=ynT.ap[:, c, :], rhs=wout.ap[:, c, half * 512:(half + 1) * 512],
                                                                  start=(c == 0), stop=(c == 15)),
                         reads=[(ynT.b, None), (wout.b, None)], writes=[(PB[1 + half], None)])
            self.epilogue(1, xr.ap, (xr.b, None), lng, lnb, xout, t * 128, tmp)


    def ssd_phase2(self, li, j, xin, xout):
        P, d = self.P, self.dram
        self.phase()
        PB = self.PB
        NT = int(os.environ.get('K_NT', L // 128))
        NG = int(os.environ.get('K_NG', 8))
        NB = 5
        xTall = self.alloc("xTall", [8, 3 + L], BF16)
        cw = self.alloc("scw", [32, 4], F32)
        cb = self.alloc("scb", [32], F32)
        small = self.bcast_load("ssmall", d["ssm_small"][j:j + 1, :], 96)
        A_b = self.alloc("A_b", [32], F32)
        P.dma(cw.ap, d["ssm_cw"][j], writes=[(cw.b, None)])
        P.dma(cb.ap, d["ssm_cb"][j], writes=[(cb.b, None)])
        dtb, alog, dsk = small.ap[:, 0:32], small.ap[:, 32:64], small.ap[:, 64:96]
        Mle = self.masks.ap[:, 0, :]
        P.op("act", lambda e: e.activation(out=A_b.ap, in_=alog, func=AF.Exp), reads=[(small.b, None)], writes=[(A_b.b, None)])
        P.op("dve", lambda e: e.tensor_scalar(out=A_b.ap, in0=A_b.ap, scalar1=-1.0, scalar2=None, op0=ALU.mult), reads=[(A_b.b, None)], writes=[(A_b.b, None)])
        mark = self.off
        xld = [self.alloc("xld%d" % i, [D], F32) for i in range(2)]
        xbf = [self.alloc("xbf%d" % i, [D], BF16) for i in range(2)]
        P.op("pool", lambda e: e.memset(xTall.ap[:, :, 0:3], 0.0), writes=[(xTall.b, "pad")])
        for t in range(NT):
            self.make_xT_block(xin, t * 128, xld[t % 2].ap, xld[t % 2].b, None, xbf[t % 2], xTall, t, 3 + t * 128, (0, 5)[t % 2])
        P.barrier()
        self.off = mark
        wst = self.alloc("wst", [8, 772], F32)
        wg = [self.alloc("wg%d" % i, [8, 772], BF16) for i in range(2)]
        ngg = [self.alloc("ngg%d" % i, [256], F32) for i in range(2)]
        H = self.alloc("H", [4, 64], F32)
        Hb0 = self.alloc("Hb0", [256], BF16)
        Hb1 = self.alloc("Hb1", [256], BF16)
        acc = [self.alloc("acc%d" % i, [384], F32) for i in range(4)]
        XS0 = [self.alloc("XS0_%d" % i, [384], F32) for i in range(2)]
        XS1 = [self.alloc("XS1_%d" % i, [384], F32) for i in range(2)]
        BTs = [self.alloc("BTs_%d" % i, [384], BF16) for i in range(2)]
        CTs = [self.alloc("CTs_%d" % i, [384], BF16) for i in range(2)]

        def U(name, shape, dt):
            return [self.alloc("%s_%d" % (name, i), shape, dt) for i in range(NB)]
        dttu, dteu, dtu, dtAu, dtwu = U("dtt", [4], F32), U("dte", [4], F32), U("dt_", [4], F32), U("dtA", [4], F32), U("dtw", [4], F32)
        exu, dtw2u = U("ex", [4, 4], F32), U("dtw2", [2, 4], F32)
        Lgu, Eu, CBmu, Mu = U("Lg", [4, 128], F32), U("E", [4, 128], F32), U("CBm", [128], F32), U("M", [4, 128], BF16)
        xstu, Btoku, zsu = U("xs_tok", [4, 64], F32), U("Btok", [128], BF16), U("zs", [256], F32)
        xdtu, xwu = U("xdt", [4, 64], BF16), U("xw", [2, 4, 64], BF16)
        t1u, t2u, hcdu = U("t1", [4, 64], F32), U("t2", [4, 64], F32), U("hcd", [4, 64], F32)
        ssu, ynu, ynTu = U("ss", [4], F32), U("yn", [256], BF16), U("ynTu", [2, 128], BF16)
        wsrc = d["ssm_w_in"][j].rearrange("(c p) n -> p c n", p=128)
        ynTd = d["ynT"].rearrange("(c p) t -> p c t", p=128)

        def load_group(g):
            segs = [(2048 + g * 256, 256, 0), (4096 + g * 128, 128, 256), (5120 + g * 128, 128, 384), (g * 256, 256, 512), (6144 + 4 * g, 4, 768)]
            for (c0, w, o) in segs:
                P.dma(wst.ap[:, :, o:o + w], wsrc[:, :, c0:c0 + w], writes=[(wst.b, o)])
            wgt = wg[g % 2]
            P.op("pool", lambda e, wgt=wgt: e.tensor_copy(out=wgt.ap[:, 0:4, :], in_=wst.ap[:, 0:4, :]), reads=[(wst.b, None)], writes=[(wgt.b, 0)])
            P.op("act", lambda e, wgt=wgt: e.copy(out=wgt.ap[:, 4:8, :], in_=wst.ap[:, 4:8, :]), reads=[(wst.b, None)], writes=[(wgt.b, 1)])
            P.dma(ngg[g % 2].ap, d["ssm_norm_g"][j:j + 1, g * 256:(g + 1) * 256].partition_broadcast(128), writes=[(ngg[g % 2].b, None)])

        load_group(0)
        ccb = [0]
        for g in range(NG):
            if g + 1 < NG:
                load_group(g + 1)
            W = wg[g % 2]
            ng = ngg[g % 2]
            hs = slice(4 * g, 4 * g + 4)
            P.op("pool", lambda e: e.memset(H.ap, 0.0), writes=[(H.b, None)])
            P.op("pool", lambda e: e.memset(Hb0.ap, 0.0), writes=[(Hb0.b, None)])
            Hf = H.ap.rearrange("p a b -> p (a b)")
            def unit(t, sidx):
                def SS(k):
                    P.active = (k == sidx)
                u = t % NB
                xc = slice(t * 128, t * 128 + 131)
                xk = slice(3 + t * 128, 3 + (t + 1) * 128)
                dtt, dte, dt_, dtA, dtw, ex, dtw2 = dttu[u], dteu[u], dtu[u], dtAu[u], dtwu[u], exu[u], dtw2u[u]
                Lg, E, CBm, M = Lgu[u], Eu[u], CBmu[u], Mu[u]
                xs_tok, Btok, zs, xdt, xw = xstu[u], Btoku[u], zsu[u], xdtu[u], xwu[u]
                t1, t2, hcd, ss, yn, ynT = t1u[u], t2u[u], hcdu[u], ssu[u], ynu[u], ynTu[u]
                s3 = (t // 3) % 2
                o3 = (t % 3) * 128
                btT, ctT = BTs[s3], CTs[s3]
                btv, ctv = btT.ap[:, o3:o3 + 128], ctT.ap[:, o3:o3 + 128]
                SS(0)
                for k in range(8):
                    P.op("pe", lambda e, k=k, xk=xk, W=W: e.matmul(out=self.bank(0, 4, c0=256), lhsT=xTall.ap[:, k, xk], rhs=W.ap[:, k, 768:772], start=(k == 0), stop=(k == 7)),
                         reads=[(xTall.b, None), (W.b, None)], writes=[(PB[0], None)])
                P.op("dve", lambda e, dtt=dtt, hs=hs: e.tensor_tensor(out=dtt.ap, in0=self.bank(0, 4, c0=256), in1=dtb[:, hs], op=ALU.add), reads=[(PB[0], None), (small.b, None)], writes=[(dtt.b, None)])
                P.op("act", lambda e, dtt=dtt, dte=dte: e.activation(out=dte.ap, in_=dtt.ap, func=AF.Exp), reads=[(dtt.b, None)], writes=[(dte.b, None)])
                P.op("act", lambda e, dte=dte, dt_=dt_: e.activation(out=dt_.ap, in_=dte.ap, func=AF.Ln, bias=1.0), reads=[(dte.b, None)], writes=[(dt_.b, None)])
                P.op("dve", lambda e, dt_=dt_, dtA=dtA, hs=hs: e.tensor_tensor(out=dtA.ap, in0=dt_.ap, in1=A_b.ap[:, hs], op=ALU.mult), reads=[(dt_.b, None), (A_b.b, None)], writes=[(dtA.b, None)])
                SS(1)
                for m in range(4):
                    P.op("pe", lambda e, m=m, dtA=dtA: e.matmul(out=self.bank(0, 4, c0=320 + m * 4), lhsT=self.masks.ap[:, m, :], rhs=dtA.ap, start=True, stop=True),
                         reads=[(self.masks.b, None), (dtA.b, None)], writes=[(PB[0], None)])
                P.op("act", lambda e, ex=ex: e.activation(out=ex.ap, in_=self.bank(0, 16, c0=320).rearrange("p (a b) -> p a b", a=4), func=AF.Exp), reads=[(PB[0], None)], writes=[(ex.b, None)])
                P.op("dve", lambda e, dtw=dtw, dt_=dt_, ex=ex: e.tensor_tensor(out=dtw.ap, in0=dt_.ap, in1=ex.ap[:, 1, :], op=ALU.mult), reads=[(dt_.b, None), (ex.b, None)], writes=[(dtw.b, None)])
                for c_ in range(2):
                    P.op("dve", lambda e, c_=c_, dtw=dtw, dtw2=dtw2: e.tensor_scalar(out=dtw2.ap[:, c_, :], in0=dtw.ap, scalar1=self.masks.ap[:, 2 + c_, 0:1], scalar2=None, op0=ALU.mult),
                         reads=[(dtw.b, None), (self.masks.b, None)], writes=[(dtw2.b, c_)])
                SS(0)
                if t % 3 == 0:
                    Wd = min(3, NT - t) * 128
                    xcs = slice(t * 128, t * 128 + Wd + 3)
                    chunks = [(0, 2 * g, XS0[s3]), (128, 2 * g + 1, XS1[s3]), (256, 16 + g, btT), (384, 24 + g, ctT)]
                    for (col, jc, dst) in chunks:
                        bk = (1, 2)[ccb[0] % 2]
                        a = acc[ccb[0] % 4]
                        ccb[0] += 1
                        for k in range(8):
                            P.op("pe", lambda e, k=k, col=col, bk=bk, xcs=xcs, Wd=Wd, W=W: e.matmul(out=self.bank(bk, Wd + 3), lhsT=W.ap[:, k, col:col + 128], rhs=xTall.ap[:, k, xcs], start=(k == 0), stop=(k == 7)),
                                 reads=[(W.b, None), (xTall.b, None)], writes=[(PB[bk], None)])
                        P.op("act", lambda e, jc=jc, bk=bk, a=a, Wd=Wd: e.activation(out=a.ap[:, 0:Wd], in_=self.bank(bk, Wd, c0=3), func=AF.Identity, scale=cw.ap[:, jc, 3:4], bias=cb.ap[:, jc:jc + 1]),
                             reads=[(PB[bk], None), (cw.b, None), (cb.b, None)], writes=[(a.b, None)])
                        for kk in (2, 1, 0):
                            P.op("dve", lambda e, jc=jc, bk=bk, a=a, kk=kk, Wd=Wd: e.scalar_tensor_tensor(out=a.ap[:, 0:Wd], in0=self.bank(bk, Wd, c0=kk), scalar=cw.ap[:, jc, kk:kk + 1], in1=a.ap[:, 0:Wd], op0=ALU.mult, op1=ALU.add),
                                 reads=[(PB[bk], None), (cw.b, None), (a.b, None)], writes=[(a.b, None)])
                        P.op("act", lambda e, a=a, dst=dst, Wd=Wd: e.activation(out=dst.ap[:, 0:Wd], in_=a.ap[:, 0:Wd], func=AF.Silu), reads=[(a.b, None)], writes=[(dst.b, None)])
                SS(1)
                for i, xf in enumerate((XS0[s3], XS1[s3])):
                    P.op("pe", lambda e, i=i, xf=xf, o3=o3: e.transpose(out=self.bank(5, 128, c0=128 + i * 128), in_=xf.ap[:, o3:o3 + 128], identity=self.idf.ap),
                         reads=[(xf.b, None), (self.idf.b, None)], writes=[(PB[5], None)])
                P.op("act", lambda e, xs_tok=xs_tok: e.copy(out=xs_tok.ap.rearrange("p a b -> p (a b)"), in_=self.bank(5, 256, c0=128)), reads=[(PB[5], None)], writes=[(xs_tok.b, None)])
                P.op("pe", lambda e, btv=btv: e.transpose(out=self.bankb(5, 128, c0=768), in_=btv, identity=self.idb.ap), reads=[(btT.b, None), (self.idb.b, None)], writes=[(PB[5], None)])
                P.op("dve", lambda e, Btok=Btok: e.tensor_copy(out=Btok.ap, in_=self.bankb(5, 128, c0=768)), reads=[(PB[5], None)], writes=[(Btok.b, None)])
                for k in range(8):
                    P.op("pe", lambda e, k=k, xk=xk, W=W: e.matmul(out=self.bank(6, 256), lhsT=xTall.ap[:, k, xk], rhs=W.ap[:, k, 512:768], start=(k == 0), stop=(k == 7)),
                         reads=[(xTall.b, None), (W.b, None)], writes=[(PB[6], None)])
                P.op("act", lambda e, zs=zs: e.activation(out=zs.ap, in_=self.bank(6, 256), func=AF.Silu), reads=[(PB[6], None)], writes=[(zs.b, None)])
                P.op("pool", lambda e, xdt=xdt, xs_tok=xs_tok, dt_=dt_: e.tensor_tensor(out=xdt.ap, in0=xs_tok.ap, in1=dt_.ap.unsqueeze(2).to_broadcast([128, 4, 64]), op=ALU.mult),
                     reads=[(xs_tok.b, None), (dt_.b, None)], writes=[(xdt.b, None)])
                for c_ in range(2):
                    P.op("pool", lambda e, c_=c_, xw=xw, xs_tok=xs_tok, dtw2=dtw2: e.tensor_tensor(out=xw.ap[:, c_], in0=xs_tok.ap, in1=dtw2.ap[:, c_, :].unsqueeze(2).to_broadcast([128, 4, 64]), op=ALU.mult),
                         reads=[(xs_tok.b, None), (dtw2.b, None)], writes=[(xw.b, c_)])
                SS(0)
                P.op("pool", lambda e, Lg=Lg, dtA=dtA: e.tensor_tensor(out=Lg.ap, in0=self.masks.ap[:, 1:2, :].to_broadcast([128, 4, 128]), in1=dtA.ap.unsqueeze(2).to_broadcast([128, 4, 128]), op=ALU.mult),
                     reads=[(self.masks.b, None), (dtA.b, None)], writes=[(Lg.b, None)])
                SS(1)
                for hh in range(4):
                    P.op("pe", lambda e, hh=hh, Lg=Lg: e.matmul(out=self.bank(4, 128, c0=hh * 128), lhsT=Lg.ap[:, hh, :], rhs=Mle, start=True, stop=True),
                         reads=[(Lg.b, None), (self.masks.b, None)], writes=[(PB[4], None)])
                P.op("act", lambda e, E=E: e.activation(out=E.ap.rearrange("p a b -> p (a b)"), in_=self.bank(4, 512), func=AF.Exp), reads=[(PB[4], None)], writes=[(E.b, None)])
                P.op("pe", lambda e, btv=btv, ctv=ctv: e.matmul(out=self.bank(5, 128), lhsT=btv, rhs=ctv, start=True, stop=True), reads=[(btT.b, None), (ctT.b, None)], writes=[(PB[5], None)])
                P.op("dve", lambda e, CBm=CBm: e.tensor_tensor(out=CBm.ap, in0=self.bank(5, 128), in1=Mle, op=ALU.mult), reads=[(PB[5], None), (self.masks.b, None)], writes=[(CBm.b, None)])
                P.op("dve", lambda e, M=M, E=E, CBm=CBm: e.tensor_tensor(out=M.ap, in0=E.ap, in1=CBm.ap.unsqueeze(1).to_broadcast([128, 4, 128]), op=ALU.mult), reads=[(E.b, None), (CBm.b, None)], writes=[(M.b, None)])
                SS(2)
                P.op("pe", lambda e, Btok=Btok, xw=xw: e.matmul(out=self.bank(7, 256, c0=256), lhsT=Btok.ap, rhs=xw.ap[:, 0].rearrange("p a b -> p (a b)"), start=True, stop=True),
                     reads=[(Btok.b, None), (xw.b, None)], writes=[(PB[7], None)])
                P.op("pe", lambda e, Btok=Btok, xw=xw: e.matmul(out=self.bank(3, 256, c0=256), lhsT=Btok.ap, rhs=xw.ap[:, 1].rearrange("p a b -> p (a b)"), start=True, stop=True),
                     reads=[(Btok.b, None), (xw.b, None)], writes=[(PB[3], None)])
                for hh in range(4):
                    P.op("pe", lambda e, hh=hh, ctv=ctv: e.matmul(out=self.bank(7, 64, p0=0, p1=64, c0=hh * 64), lhsT=ctv[:, 0:64], rhs=Hb0.ap[:, hh * 64:(hh + 1) * 64], start=True, stop=True),
                         reads=[(ctT.b, None), (Hb0.b, None)], writes=[(PB[7], None)])
                for hh in range(4):
                    P.op("dve", lambda e, hh=hh, ex=ex: e.scalar_tensor_tensor(out=H.ap[:, hh, :], in0=H.ap[:, hh, :], scalar=ex.ap[:, 2, hh:hh + 1], in1=self.bank(7, 64, c0=256 + hh * 64), op0=ALU.mult, op1=ALU.add),
                         reads=[(H.b, hh), (ex.b, None), (PB[7], None)], writes=[(H.b, hh)])
                P.op("act", lambda e: e.copy(out=Hb1.ap, in_=Hf), reads=[(H.b, None)], writes=[(Hb1.b, None)])
                SS(3)
                for hh in range(4):
                    P.op("pe", lambda e, hh=hh, M=M, xdt=xdt: e.matmul(out=self.bank(6, 64, c0=256 + hh * 64), lhsT=M.ap[:, hh, :], rhs=xdt.ap[:, hh, :], start=True, stop=True),
                         reads=[(M.b, None), (xdt.b, None)], writes=[(PB[6], None)])
                for hh in range(4):
                    P.op("pe", lambda e, hh=hh, ctv=ctv: e.matmul(out=self.bank(7, 64, p0=64, p1=128, c0=hh * 64), lhsT=ctv[:, 64:128], rhs=Hb1.ap[:, hh * 64:(hh + 1) * 64], start=True, stop=True),
                         reads=[(ctT.b, None), (Hb1.b, None)], writes=[(PB[7], None)])
                SS(2)
                for hh in range(4):
                    P.op("dve", lambda e, hh=hh, ex=ex: e.scalar_tensor_tensor(out=H.ap[:, hh, :], in0=H.ap[:, hh, :], scalar=ex.ap[:, 3, hh:hh + 1], in1=self.bank(3, 64, c0=256 + hh * 64), op0=ALU.mult, op1=ALU.add),
                         reads=[(H.b, hh), (ex.b, None), (PB[3], None)], writes=[(H.b, hh)])
                P.op("act", lambda e: e.copy(out=Hb0.ap, in_=Hf), reads=[(H.b, None)], writes=[(Hb0.b, None)])
                SS(3)
                P.op("dve", lambda e, t1=t1, ex=ex: e.tensor_tensor(out=t1.ap, in0=self.bank(7, 256).rearrange("p (a b) -> p a b", a=4), in1=ex.ap[:, 0, :].unsqueeze(2).to_broadcast([128, 4, 64]), op=ALU.mult),
                     reads=[(PB[7], None), (ex.b, None)], writes=[(t1.b, None)])
                P.op("dve", lambda e, t1=t1: e.tensor_tensor(out=t1.ap.rearrange("p a b -> p (a b)"), in0=self.bank(6, 256, c0=256), in1=t1.ap.rearrange("p a b -> p (a b)"), op=ALU.add),
                     reads=[(t1.b, None), (PB[6], None)], writes=[(t1.b, None)])
                P.op("pool", lambda e, t2=t2, xs_tok=xs_tok, hs=hs: e.tensor_tensor(out=t2.ap, in0=xs_tok.ap, in1=dsk[:, hs].unsqueeze(2).to_broadcast([128, 4, 64]), op=ALU.mult),
                     reads=[(xs_tok.b, None), (small.b, None)], writes=[(t2.b, None)])
                P.op("pool", lambda e, t1=t1, t2=t2: e.tensor_tensor(out=t1.ap, in0=t1.ap, in1=t2.ap, op=ALU.add), reads=[(t1.b, None), (t2.b, None)], writes=[(t1.b, None)])
                P.op("pool", lambda e, t1=t1, zs=zs: e.tensor_tensor(out=t1.ap.rearrange("p a b -> p (a b)"), in0=t1.ap.rearrange("p a b -> p (a b)"), in1=zs.ap, op=ALU.mult), reads=[(t1.b, None), (zs.b, None)], writes=[(t1.b, None)])
                SS(4)
                P.op("act", lambda e, t1=t1, t2=t2, ss=ss: e.activation(out=t2.ap.rearrange("p a b -> p (a b)"), in_=t1.ap.rearrange("p a b -> p (a b)"), func=AF.Square, accum_out=ss.ap[:, 0:1]),
                     reads=[(t1.b, None)], writes=[(t2.b, None), (ss.b, 0)])
                P.op("pool", lambda e, ss=ss: e.tensor_scalar(out=ss.ap[:, 1:2], in0=ss.ap[:, 0:1], scalar1=1.0 / 256.0, scalar2=RMS_EPS, op0=ALU.mult, op1=ALU.add), reads=[(ss.b, 0)], writes=[(ss.b, 1)])
                P.op("pool", lambda e, ss=ss: e.tensor_tensor(out=ss.ap[:, 2:3], in0=ss.ap[:, 1:2], in1=self.mhalf.ap, op=ALU.pow), reads=[(ss.b, 1), (self.mhalf.b, None)], writes=[(ss.b, 2)])
                P.op("dve", lambda e, yn=yn, t1=t1, ss=ss, ng=ng: e.scalar_tensor_tensor(out=yn.ap, in0=t1.ap.rearrange("p a b -> p (a b)"), scalar=ss.ap[:, 2:3], in1=ng.ap, op0=ALU.mult, op1=ALU.mult),
                     reads=[(t1.b, None), (ss.b, 2), (ng.b, None)], writes=[(yn.b, None)])
                SS(5)
                for i in range(2):
                    P.op("pe", lambda e, i=i, yn=yn: e.transpose(out=self.bankb(0, 128, c0=i * 128), in_=yn.ap[:, i * 128:(i + 1) * 128], identity=self.idb.ap),
                         reads=[(yn.b, None), (self.idb.b, None)], writes=[(PB[0], None)])
                P.op("act", lambda e, ynT=ynT: e.copy(out=ynT.ap, in_=self.bankb(0, 256).rearrange("p (a b) -> p a b", a=2)), reads=[(PB[0], None)], writes=[(ynT.b, None)])
                P.dma(ynTd[:, 2 * g:2 * g + 2, t * 128:(t + 1) * 128], ynT.ap, reads=[(ynT.b, None)], writes=[(self.ynT_dep, (g, t))])
                P.active = True
            for i in range(NT + 5):
                for sidx in (5, 4, 3, 2, 1, 0):
                    t = i - sidx
                    if 0 <= t < NT:
                        unit(t, sidx)
        stop_at(11)
        P.barrier()
        self.off = mark
        wout = self.alloc("wout", [16, D], BF16)
        lng = self.bcast_load("lng", d["ln_g"][li * 2:li * 2 + 1, :], D)
        lnb = self.bcast_load("lnb", d["ln_b"][li * 2:li * 2 + 1, :], D)
        stg = [self.alloc("stg%d" % i, [2048], F32) for i in range(3)]
        self.load_w(wout, d["ssm_w_out"][j].rearrange("(c p) n -> p c n", p=128), 16, D, stg)
        yt = [self.alloc("yt%d" % i, [16, 512], BF16) for i in range(2)]
        xres = [self.alloc("xresc%d" % i, [D], F32) for i in range(2)]
        tmps = [self.ln_tmp("s2a", inplace=True), self.ln_tmp("s2b", inplace=True)]
        nb = 0
        for tt in range(max(1, NT // 4)):
            y_ = yt[tt % 2]
            P.dma(y_.ap, ynTd[:, :, tt * 512:(tt + 1) * 512], reads=[(self.ynT_dep, None)], writes=[(y_.b, None)])
            for blk in range(4):
                tok0 = tt * 512 + blk * 128
                xr = xres[nb % 2]
                nb += 1
                P.dma(xr.ap, xin[tok0:tok0 + 128, :], writes=[(xr.b, None)])
                yb = 1 + 2 * (nb % 2)
                for half in range(2):
                    for c in range(16):
                        P.op("pe", lambda e, c=c, half=half, blk=blk, y_=y_, yb=yb: e.matmul(out=self.bank(yb + half), lhsT=y_.ap[:, c, blk * 128:(blk + 1) * 128], rhs=wout.ap[:, c, half * 512:(half + 1) * 512],
                                                                                             start=(c == 0), stop=(c == 15)),
                             reads=[(y_.b, None), (wout.b, None)], writes=[(PB[yb + half], None)])
                self.epilogue(yb, xr.ap, (xr.b, None), lng, lnb, xout, tok0, tmps[nb % 2])

    def sg_phase(self, li, xin, xout):
        P, d = self.P, self.dram
        self.phase()
        PB = self.PB
        win = self.alloc("gwin", [8, 4096], BF16)
        wout = self.alloc("gwout", [16, D], BF16)
        wsT = self.alloc("wsT", [8, 128], BF16)
        bsp = self.alloc("bsp", [8], F32)
        bin_b = self.bcast_load("bin_b", d["sg_b_in"][0:1, :], 4096)
        vg = self.bcast_load("vg", d["sg_ln_g"][0:1, :], 2048)
        vb = self.bcast_load("vb", d["sg_ln_b"][0:1, :], 2048)
        lng = self.bcast_load("lng", d["ln_g"][li * 2:li * 2 + 1, :], D)
        lnb = self.bcast_load("lnb", d["ln_b"][li * 2:li * 2 + 1, :], D)
        mark = self.off
        stg = [self.alloc("stg%d" % i, [2048], F32) for i in range(3)]
        wsf = self.alloc("wsf", [8, 128], F32)
        P.dma(bsp.ap, d["sg_bs"][:, :], writes=[(bsp.b, None)])
        P.dma(wsf.ap, d["sg_w_s"][0].rearrange("g t s -> t g s"), writes=[(wsf.b, None)])
        for g in range(8):
            P.op("pe", lambda e, g=g: e.transpose(out=self.bank(3 + g // 4, 128, c0=(g % 4) * 128), in_=wsf.ap[:, g, :], identity=self.idf.ap),
                 reads=[(wsf.b, None), (self.idf.b, None)], writes=[(PB[3 + g // 4], None)])
        for g in range(8):
            P.op("dve", lambda e, g=g: e.tensor_tensor(out=wsT.ap[:, g, :], in0=self.bank(3 + g // 4, 128, c0=(g % 4) * 128), in1=self.masks.ap[:, 4, :], op=ALU.mult),
                 reads=[(PB[3 + g // 4], None), (self.masks.b, None)], writes=[(wsT.b, g)])
        self.load_w(win, d["sg_w_in"][0].rearrange("(c p) n -> p c n", p=128), 8, 4096, stg)
        self.load_w(wout, d["sg_w_out"][0].rearrange("(c p) n -> p c n", p=128), 16, D, stg)
        P.barrier()
        self.off = mark
        xres = [self.alloc("xres%d" % i, [D], F32) for i in range(2)]
        xbf = self.alloc("xbf", [D], BF16)
        xT = [self.alloc("xT%d" % i, [8, 128], BF16) for i in range(2)]
        u_tok = self.alloc("u_tok", [2048], F32)
        v_tok = self.alloc("v_tok", [2048], F32)
        vo = self.alloc("vo", [2048], F32)
        hA = [self.alloc("hA%d" % i, [512], F32) for i in range(2)]
        hB = [self.alloc("hB%d" % i, [512], F32) for i in range(2)]
        vln = self.alloc("vln", [2048], BF16)
        uv = self.alloc("uv", [2048], BF16)
        uvT = self.alloc("uvT", [16, 128], BF16)
        vst = self.alloc("vst", [24], F32)
        vmv = self.alloc("vmv", [2], F32)
        vsc = self.alloc("vsc", [4], F32)
        tmp = self.ln_tmp("g", inplace=True)
        GC = 2.0 * math.sqrt(2.0 / math.pi)
        for t in range(int(os.environ.get('K_NT', L // 128))):
            cur = xT[t % 2]
            xr = xres[t % 2]
            self.make_xT_block(xin, t * 128, xr.ap, xr.b, None, xbf, cur, None, 0, 0)
            for q in range(8):
                bk = 1 + q % 2
                a, b_ = hA[q % 2], hB[q % 2]
                for k in range(8):
                    P.op("pe", lambda e, k=k, q=q, bk=bk, cur=cur: e.matmul(out=self.bank(bk), lhsT=cur.ap[:, k, :], rhs=win.ap[:, k, q * 512:(q + 1) * 512],
                                                                           start=(k == 0), stop=(k == 7)),
                         reads=[(cur.b, None), (win.b, None)], writes=[(PB[bk], None)])
                P.op("dve", lambda e, q=q, bk=bk, a=a: e.tensor_tensor(out=a.ap, in0=self.bank(bk), in1=bin_b.ap[:, q * 512:(q + 1) * 512], op=ALU.add),
                     reads=[(PB[bk], None), (bin_b.b, None)], writes=[(a.b, None)])
                P.op("act", lambda e, a=a, b_=b_: e.activation(out=b_.ap, in_=a.ap, func=AF.Square), reads=[(a.b, None)], writes=[(b_.b, None)])
                P.op("dve", lambda e, a=a, b_=b_: e.scalar_tensor_tensor(out=b_.ap, in0=b_.ap, scalar=1.0 / 0.044715, in1=a.ap, op0=ALU.add, op1=ALU.mult),
                     reads=[(a.b, None), (b_.b, None)], writes=[(b_.b, None)])
                P.op("act", lambda e, b_=b_: e.activation(out=b_.ap, in_=b_.ap, func=AF.Sigmoid, scale=GC * 0.044715), reads=[(b_.b, None)], writes=[(b_.b, None)])
                dst = u_tok if q < 4 else v_tok
                dv = dst.ap[:, (q % 4) * 512:(q % 4 + 1) * 512]
                P.op("pool", lambda e, a=a, b_=b_, dv=dv: e.tensor_tensor(out=dv, in0=a.ap, in1=b_.ap, op=ALU.mult),
                     reads=[(a.b, None), (b_.b, None)], writes=[(dst.b, q % 4)])
            self.layernorm(v_tok, 2048, vg, vb, vo, vst, vmv, vsc, LN_EPS, out_ap=vln.ap, out_buf=vln)
            for g in range(8):
                bk = 3 + g // 2
                P.op("pe", lambda e, g=g, bk=bk: e.matmul(out=self.bank(bk, 256, c0=(g % 2) * 256), lhsT=wsT.ap[:, g, :], rhs=vln.ap[:, g * 256:(g + 1) * 256], start=True, stop=True),
                     reads=[(wsT.b, None), (vln.b, None)], writes=[(PB[bk], None)])
            for g in range(8):
                bk = 3 + g // 2
                P.op("dve", lambda e, g=g, bk=bk: e.scalar_tensor_tensor(out=uv.ap[:, g * 256:(g + 1) * 256], in0=self.bank(bk, 256, c0=(g % 2) * 256), scalar=bsp.ap[:, g:g + 1],
                                                                          in1=u_tok.ap[:, g * 256:(g + 1) * 256], op0=ALU.add, op1=ALU.mult),
                     reads=[(PB[bk], None), (bsp.b, None), (u_tok.b, None)], writes=[(uv.b, g)])
            for hf in range(2):
                for c in range(8):
                    P.op("pe", lambda e, c=c, hf=hf: e.transpose(out=self.bankb(7, 128, c0=c * 128), in_=uv.ap[:, (hf * 8 + c) * 128:(hf * 8 + c + 1) * 128], identity=self.idb.ap),
                         reads=[(uv.b, None), (self.idb.b, None)], writes=[(PB[7], None)])
                P.op("act", lambda e, hf=hf: e.copy(out=uvT.ap[:, hf * 8:(hf + 1) * 8, :], in_=self.bankb(7, 1024).rearrange("p (c t) -> p c t", c=8)),
                     reads=[(PB[7], None)], writes=[(uvT.b, hf)])
            for half in range(2):
                for c in range(16):
                    P.op("pe", lambda e, c=c, half=half: e.matmul(out=self.bank(1 + half), lhsT=uvT.ap[:, c, :], rhs=wout.ap[:, c, half * 512:(half + 1) * 512],
                                                                  start=(c == 0), stop=(c == 15)),
                         reads=[(uvT.b, None), (wout.b, None)], writes=[(PB[1 + half], None)])
            self.epilogue(1, xr.ap, (xr.b, None), lng, lnb, xout, t * 128, tmp)


    def mla_phase(self, li, xin, xout):
        P, d = self.P, self.dram
        self.phase()
        PB = self.PB
        SCALE = 96.0 ** -0.5
        TWO_PI = 2.0 * math.pi
        C1 = 6.28125
        C2 = TWO_PI - C1
        PI_S = 3.1415925
        qT = self.alloc("qT", [3, L], BF16)
        kvT = self.alloc("kvT", [2, L], BF16)
        krT = self.alloc("krT", [L], BF16)
        cosT = self.alloc("cosT", [L], F32)
        sinT = self.alloc("sinT", [L], F32)
        wq = self.alloc("wq", [3, 16, 128], BF16)
        wqs = self.alloc("wqs", [3, 16, 64], BF16)
        wkn = self.alloc("wkn", [2, 16, 128], BF16)
        wv = self.alloc("wv", [2, 16, 64], BF16)
        wout = self.alloc("mwout", [8, D], BF16)
        lng = self.bcast_load("lng", d["ln_g"][li * 2:li * 2 + 1, :], D)
        lnb = self.bcast_load("lnb", d["ln_b"][li * 2:li * 2 + 1, :], D)
        mark = self.off
        win = self.alloc("mwin", [8, 640], BF16)
        wkr = self.alloc("wkr", [8, 64], BF16)
        wkrs = self.alloc("wkrs", [8, 64], BF16)
        gq = self.bcast_load("gq", d["mla_q_norm_g"][0:1, :], 384)
        gkv = self.bcast_load("gkv", d["mla_kv_norm_g"][0:1, :], 256)
        rc = self.alloc("rc", [2], F32)
        P.dma(rc.ap, d["consts"][:, 768:770], writes=[(rc.b, None)])
        mark2 = self.off
        stq = self.alloc("stq", [3, 1536], F32)
        stkv = self.alloc("stkv", [2, 2048], F32)
        stin = self.alloc("stin", [8, 672], F32)
        P.dma(stq.ap, d["mla_w_q_b"][0].rearrange("(c p) n -> p c n", p=128), writes=[(stq.b, None)])
        P.dma(stkv.ap, d["mla_w_kv_b"][0].rearrange("(c p) n -> p c n", p=128), writes=[(stkv.b, None)])
        P.dma(stin.ap, d["mla_w_in"][0].rearrange("(c p) n -> p c n", p=128), writes=[(stin.b, None)])
        for tz in (wq, wqs, wkn, wkr, wkrs):
            P.op("pool", lambda e, tz=tz: e.memset(tz.ap, 0.0), writes=[(tz.b, None)])
        for c in range(3):
            src = stq.ap[:, c, :].rearrange("p (h d) -> p h d", h=16)
            P.op("dve", lambda e, c=c, src=src: e.tensor_copy(out=wq.ap[:, c, :, 64:128], in_=src[:, :, 0:64]), reads=[(stq.b, None)], writes=[(wq.b, (c, 0))])
            P.op("act", lambda e, c=c, src=src: e.copy(out=wq.ap[:, c, :, 32:64], in_=src[:, :, 64:96]), reads=[(stq.b, None)], writes=[(wq.b, (c, 1))])
            P.op("dve", lambda e, c=c, src=src: e.tensor_copy(out=wqs.ap[:, c, :, 32:48], in_=src[:, :, 80:96]), reads=[(stq.b, None)], writes=[(wqs.b, (c, 0))])
            P.op("act", lambda e, c=c, src=src: e.copy(out=wqs.ap[:, c, :, 48:64], in_=src[:, :, 64:80]), reads=[(stq.b, None)], writes=[(wqs.b, (c, 1))])
        for c in range(2):
            src = stkv.ap[:, c, :].rearrange("p (h d) -> p h d", h=16)
            P.op("dve", lambda e, c=c, src=src: e.tensor_copy(out=wkn.ap[:, c, :, 64:128], in_=src[:, :, 0:64]), reads=[(stkv.b, None)], writes=[(wkn.b, (c, 0))])
            P.op("act", lambda e, c=c, src=src: e.copy(out=wv.ap[:, c, :, :], in_=src[:, :, 64:128]), reads=[(stkv.b, None)], writes=[(wv.b, c)])
        P.op("dve", lambda e: e.tensor_copy(out=win.ap, in_=stin.ap[:, :, 0:640]), reads=[(stin.b, None)], writes=[(win.b, None)])
        P.op("act", lambda e: e.copy(out=wkr.ap[:, :, 32:64], in_=stin.ap[:, :, 640:672]), reads=[(stin.b, None)], writes=[(wkr.b, 1)])
        P.op("dve", lambda e: e.tensor_copy(out=wkrs.ap[:, :, 32:48], in_=stin.ap[:, :, 656:672]), reads=[(stin.b, None)], writes=[(wkrs.b, 1)])
        P.op("act", lambda e: e.copy(out=wkrs.ap[:, :, 48:64], in_=stin.ap[:, :, 640:656]), reads=[(stin.b, None)], writes=[(wkrs.b, 2)])
        P.barrier()
        self.off = mark2
        stg = [self.alloc("stg%d" % i, [2048], F32) for i in range(3)]
        self.load_w(wout, d["mla_w_out"][0].rearrange("(c p) n -> p c n", p=128), 8, D, stg)
        P.barrier()
        self.off = mark2
        stop_at(20)
        posi = self.alloc("posi", [1024], I32)
        ki = self.alloc("ki", [1024], I32)
        pf = self.alloc("pf", [1024], F32)
        ang = self.alloc("ang", [1024], F32)
        kf = self.alloc("kf", [1024], F32)
        rr = self.alloc("rr", [1024], F32)
        r2 = self.alloc("r2", [1024], F32)
        mm = self.alloc("mm", [1024], F32)
        R = slice(0, 64)
        for ch in range(4):
            cs = slice(ch * 1024, (ch + 1) * 1024)
            P.dma(posi.ap[R], d["positions"][0:1, cs].partition_broadcast(64), writes=[(posi.b, None)])
            P.op("dve", lambda e: e.tensor_copy(out=pf.ap[R], in_=posi.ap[R]), reads=[(posi.b, None)], writes=[(pf.b, None)])
            P.op("dve", lambda e: e.tensor_scalar(out=ang.ap[R], in0=pf.ap[R], scalar1=rc.ap[R, 0:1], scalar2=None, op0=ALU.mult), reads=[(pf.b, None), (rc.b, None)], writes=[(ang.b, None)])
            P.op("dve", lambda e: e.tensor_scalar(out=pf.ap[R], in0=ang.ap[R], scalar1=1.0 / TWO_PI, scalar2=None, op0=ALU.mult), reads=[(ang.b, None)], writes=[(pf.b, None)])
            P.op("dve", lambda e: e.tensor_copy(out=ki.ap[R], in_=pf.ap[R]), reads=[(pf.b, None)], writes=[(ki.b, None)])
            P.op("dve", lambda e: e.tensor_copy(out=kf.ap[R], in_=ki.ap[R]), reads=[(ki.b, None)], writes=[(kf.b, None)])
            P.op("dve", lambda e: e.scalar_tensor_tensor(out=rr.ap[R], in0=kf.ap[R], scalar=-C1, in1=ang.ap[R], op0=ALU.mult, op1=ALU.add), reads=[(kf.b, None), (ang.b, None)], writes=[(rr.b, None)])
            P.op("dve", lambda e: e.scalar_tensor_tensor(out=rr.ap[R], in0=kf.ap[R], scalar=-C2, in1=rr.ap[R], op0=ALU.mult, op1=ALU.add), reads=[(kf.b, None), (rr.b, None)], writes=[(rr.b, None)])
            P.op("dve", lambda e: e.tensor_scalar(out=r2.ap[R], in0=rr.ap[R], scalar1=math.pi / 2, scalar2=None, op0=ALU.add), reads=[(rr.b, None)], writes=[(r2.b, None)])
            P.op("dve", lambda e: e.tensor_scalar(out=mm.ap[R], in0=r2.ap[R], scalar1=math.pi, scalar2=TWO_PI, op0=ALU.is_gt, op1=ALU.mult), reads=[(r2.b, None)], writes=[(mm.b, None)])
            P.op("dve", lambda e: e.tensor_tensor(out=r2.ap[R], in0=r2.ap[R], in1=mm.ap[R], op=ALU.subtract), reads=[(r2.b, None), (mm.b, None)], writes=[(r2.b, None)])
            P.op("dve", lambda e: e.tensor_scalar(out=rr.ap[R], in0=rr.ap[R], scalar1=PI_S, scalar2=-PI_S, op0=ALU.min, op1=ALU.max), reads=[(rr.b, None)], writes=[(rr.b, None)])
            P.op("dve", lambda e: e.tensor_scalar(out=r2.ap[R], in0=r2.ap[R], scalar1=PI_S, scalar2=-PI_S, op0=ALU.min, op1=ALU.max), reads=[(r2.b, None)], writes=[(r2.b, None)])
            P.op("act", lambda e: e.activation(out=mm.ap[R], in_=rr.ap[R], func=AF.Sin), reads=[(rr.b, None)], writes=[(mm.b, None)])
            P.op("dve", lambda e, cs=cs: e.tensor_scalar(out=sinT.ap[R, cs], in0=mm.ap[R], scalar1=rc.ap[R, 1:2], scalar2=None, op0=ALU.mult), reads=[(mm.b, None), (rc.b, None)], writes=[(sinT.b, ch)])
            P.op("act", lambda e, cs=cs: e.activation(out=cosT.ap[R, cs], in_=r2.ap[R], func=AF.Sin), reads=[(r2.b, None)], writes=[(cosT.b, ch)])
        stop_at(201)
        xres = [self.alloc("xres%d" % i, [D], F32) for i in range(2)]
        xbf = self.alloc("xbf", [D], BF16)
        xT = [self.alloc("xT%d" % i, [8, 128], BF16) for i in range(2)]
        junk = self.alloc("junk", [384], F32)
        ss = self.alloc("ss", [8], F32)
        qn = self.alloc("qn", [384], BF16)
        kvn = self.alloc("kvn", [256], BF16)
        rt1 = self.alloc("rt1", [512], F32)
        rt2 = self.alloc("rt2", [512], F32)
        NT = int(os.environ.get('K_NT', L // 128))
        for t in range(NT):
            cur = xT[t % 2]
            xr = xres[t % 2]
            ts_ = slice(t * 128, (t + 1) * 128)
            self.make_xT_block(xin, t * 128, xr.ap, xr.b, None, xbf, cur, None, 0, 0)
            for k in range(8):
                P.op("pe", lambda e, k=k, cur=cur: e.matmul(out=self.bank(1, 384), lhsT=cur.ap[:, k, :], rhs=win.ap[:, k, 0:384], start=(k == 0), stop=(k == 7)),
                     reads=[(cur.b, None), (win.b, None)], writes=[(PB[1], None)])
            for k in range(8):
                P.op("pe", lambda e, k=k, cur=cur: e.matmul(out=self.bank(2, 256), lhsT=cur.ap[:, k, :], rhs=win.ap[:, k, 384:640], start=(k == 0), stop=(k == 7)),
                     reads=[(cur.b, None), (win.b, None)], writes=[(PB[2], None)])
            P.op("act", lambda e: e.activation(out=junk.ap, in_=self.bank(1, 384), func=AF.Square, accum_out=ss.ap[:, 0:1]), reads=[(PB[1], None)], writes=[(junk.b, None), (ss.b, 0)])
            P.op("act", lambda e: e.activation(out=junk.ap[:, 0:256], in_=self.bank(2, 256), func=AF.Square, accum_out=ss.ap[:, 1:2]), reads=[(PB[2], None)], writes=[(junk.b, None), (ss.b, 1)])
            stop_at(211)
            P.op("pool", lambda e: e.tensor_scalar(out=ss.ap[:, 2:3], in0=ss.ap[:, 0:1], scalar1=1.0 / 384.0, scalar2=RMS_EPS, op0=ALU.mult, op1=ALU.add), reads=[(ss.b, 0)], writes=[(ss.b, 2)])
            P.op("pool", lambda e: e.tensor_scalar(out=ss.ap[:, 3:4], in0=ss.ap[:, 1:2], scalar1=1.0 / 256.0, scalar2=RMS_EPS, op0=ALU.mult, op1=ALU.add), reads=[(ss.b, 1)], writes=[(ss.b, 3)])
            for i_ in range(2):
                P.op("pool", lambda e, i_=i_: e.tensor_tensor(out=ss.ap[:, 4 + i_:5 + i_], in0=ss.ap[:, 2 + i_:3 + i_], in1=self.mhalf.ap, op=ALU.pow),
                     reads=[(ss.b, 2 + i_), (self.mhalf.b, None)], writes=[(ss.b, 4 + i_)])
            stop_at(212)
            P.op("dve", lambda e: e.scalar_tensor_tensor(out=qn.ap, in0=self.bank(1, 384), scalar=ss.ap[:, 4:5], in1=gq.ap, op0=ALU.mult, op1=ALU.mult),
                 reads=[(PB[1], None), (ss.b, 4), (gq.b, None)], writes=[(qn.b, None)])
            stop_at(2121)
            P.op("dve", lambda e: e.scalar_tensor_tensor(out=kvn.ap, in0=self.bank(2, 256), scalar=ss.ap[:, 5:6], in1=gkv.ap, op0=ALU.mult, op1=ALU.mult),
                 reads=[(PB[2], None), (ss.b, 5), (gkv.b, None)], writes=[(kvn.b, None)])
            stop_at(2122)
            for c in range(3):
                P.op("pe", lambda e, c=c: e.transpose(out=self.bankb(3, 128, c0=c * 128), in_=qn.ap[:, c * 128:(c + 1) * 128], identity=self.idb.ap),
                     reads=[(qn.b, None), (self.idb.b, None)], writes=[(PB[3], None)])
            for c in range(2):
                P.op("pe", lambda e, c=c: e.transpose(out=self.bankb(3, 128, c0=(3 + c) * 128), in_=kvn.ap[:, c * 128:(c + 1) * 128], identity=self.idb.ap),
                     reads=[(kvn.b, None), (self.idb.b, None)], writes=[(PB[3], None)])
            stop_at(2123)
            P.op("act", lambda e, ts_=ts_: e.copy(out=qT.ap[:, :, ts_], in_=self.bankb(3, 384).rearrange("p (c t) -> p c t", c=3)), reads=[(PB[3], None)], writes=[(qT.b, t)])
            stop_at(2124)
            P.op("act", lambda e, ts_=ts_: e.copy(out=kvT.ap[:, :, ts_], in_=self.bankb(3, 256, c0=384).rearrange("p (c t) -> p c t", c=2)), reads=[(PB[3], None)], writes=[(kvT.b, t)])
            stop_at(213)
            for k in range(8):
                P.op("pe", lambda e, k=k, cur=cur: e.matmul(out=self.bank(4, 128, p0=0, p1=64), lhsT=wkr.ap[:, k, :], rhs=cur.ap[:, k, :], start=(k == 0), stop=(k == 7)),
                     reads=[(cur.b, None), (wkr.b, None)], writes=[(PB[4], None)])
            for k in range(8):
                P.op("pe", lambda e, k=k, cur=cur: e.matmul(out=self.bank(4, 128, p0=0, p1=64, c0=128), lhsT=wkrs.ap[:, k, :], rhs=cur.ap[:, k, :], start=(k == 0), stop=(k == 7)),
                     reads=[(cur.b, None), (wkrs.b, None)], writes=[(PB[4], None)])
            stop_at(214)
            P.op("dve", lambda e, ts_=ts_: e.tensor_tensor(out=rt1.ap[32:64, 0:128], in0=self.bank(4, 128, p0=32, p1=64), in1=cosT.ap[32:64, ts_], op=ALU.mult),
                 reads=[(PB[4], None), (cosT.b, None)], writes=[(rt1.b, None)])
            P.op("dve", lambda e, ts_=ts_: e.tensor_tensor(out=rt2.ap[32:64, 0:128], in0=self.bank(4, 128, p0=32, p1=64, c0=128), in1=sinT.ap[32:64, ts_], op=ALU.mult),
                 reads=[(PB[4], None), (sinT.b, None)], writes=[(rt2.b, None)])
            P.op("pool", lambda e, ts_=ts_: e.tensor_tensor(out=krT.ap[32:64, ts_], in0=rt1.ap[32:64, 0:128], in1=rt2.ap[32:64, 0:128], op=ALU.add),
                 reads=[(rt1.b, None), (rt2.b, None)], writes=[(krT.b, t)])
        stop_at(21)
        P.barrier()
        self.off = mark
        QT = [self.alloc("QT%d" % i, [L], BF16) for i in range(2)]
        KT = [self.alloc("KT%d" % i, [L], BF16) for i in range(2)]
        V = [self.alloc("V%d" % i, [32, 65], BF16) for i in range(2)]
        PT = [self.alloc("PT%d" % i, [512], BF16) for i in range(3)]
        oTh = [self.alloc("oTh%d" % i, [L], BF16) for i in range(2)]
        rq1 = self.alloc("rt1b", [512], F32)
        rq2 = self.alloc("rt2b", [512], F32)
        on = [self.alloc("on%d" % i, [64], BF16) for i in range(2)]
        rden = self.alloc("rden", [4], F32)
        for i in range(2):
            P.op("pool", lambda e, i=i: e.memset(QT[i].ap[0:32, :], 0.0), writes=[(QT[i].b, "z")])
            P.op("pool", lambda e, i=i: e.memset(KT[i].ap[0:32, :], 0.0), writes=[(KT[i].b, "z")])
            P.op("pool", lambda e, i=i: e.memset(V[i].ap[:, :, 64:65], 1.0), writes=[(V[i].b, "one")])
        NQT = max(1, NT // 4)
        cnt = 0
        for h in range(int(os.environ.get('K_NH', 16))):
            qt_, kt_, v_, oh = QT[h % 2], KT[h % 2], V[h % 2], oTh[h % 2]
            P.op("pool", lambda e, kt_=kt_: e.tensor_copy(out=kt_.ap[32:64, 0:NT * 128], in_=krT.ap[32:64, 0:NT * 128]), reads=[(krT.b, None)], writes=[(kt_.b, "r")])
            for tt in range(NQT):
                cs = slice(tt * 512, (tt + 1) * 512)
                for c in range(2):
                    P.op("pe", lambda e, c=c, h=h, cs=cs: e.matmul(out=self.bank(1), lhsT=wkn.ap[:, c, h, :], rhs=kvT.ap[:, c, cs], start=(c == 0), stop=(c == 1)),
                         reads=[(wkn.b, None), (kvT.b, None)], writes=[(PB[1], None)])
                P.op("act", lambda e, kt_=kt_, cs=cs: e.copy(out=kt_.ap[64:128, cs], in_=self.bank(1, 512, p0=64, p1=128)), reads=[(PB[1], None)], writes=[(kt_.b, ("n", tt))])
            for tt in range(NQT):
                cs = slice(tt * 512, (tt + 1) * 512)
                for c in range(3):
                    P.op("pe", lambda e, c=c, h=h, cs=cs: e.matmul(out=self.bank(0), lhsT=wq.ap[:, c, h, :], rhs=qT.ap[:, c, cs], start=(c == 0), stop=(c == 2)),
                         reads=[(wq.b, None), (qT.b, None)], writes=[(PB[0], None)])
                for c in range(3):
                    P.op("pe", lambda e, c=c, h=h, cs=cs: e.matmul(out=self.bank(1, 512, p0=0, p1=64), lhsT=wqs.ap[:, c, h, :], rhs=qT.ap[:, c, cs], start=(c == 0), stop=(c == 2)),
                         reads=[(wqs.b, None), (qT.b, None)], writes=[(PB[1], None)])
                P.op("act", lambda e, qt_=qt_, cs=cs: e.copy(out=qt_.ap[64:128, cs], in_=self.bank(0, 512, p0=64, p1=128)), reads=[(PB[0], None)], writes=[(qt_.b, ("n", tt))])
                P.op("dve", lambda e, cs=cs: e.tensor_tensor(out=rq1.ap[32:64, :], in0=self.bank(0, 512, p0=32, p1=64), in1=cosT.ap[32:64, cs], op=ALU.mult),
                     reads=[(PB[0], None), (cosT.b, None)], writes=[(rq1.b, None)])
                P.op("dve", lambda e, cs=cs: e.tensor_tensor(out=rq2.ap[32:64, :], in0=self.bank(1, 512, p0=32, p1=64), in1=sinT.ap[32:64, cs], op=ALU.mult),
                     reads=[(PB[1], None), (sinT.b, None)], writes=[(rq2.b, None)])
                P.op("pool", lambda e, qt_=qt_, cs=cs: e.tensor_tensor(out=qt_.ap[32:64, cs], in0=rq1.ap[32:64, :], in1=rq2.ap[32:64, :], op=ALU.add),
                     reads=[(rq1.b, None), (rq2.b, None)], writes=[(qt_.b, ("r", tt))])
            for kg in range((NQT * 4 + 7) // 8):
                for kb in range(kg * 8, min(kg * 8 + 8, NQT * 4)):
                    for c in range(2):
                        P.op("pe", lambda e, c=c, h=h, kb=kb: e.matmul(out=self.bank(1, 64, c0=(kb % 8) * 64), lhsT=kvT.ap[:, c, kb * 128:(kb + 1) * 128], rhs=wv.ap[:, c, h, :],
                                                                       start=(c == 0), stop=(c == 1)),
                             reads=[(wv.b, None), (kvT.b, None)], writes=[(PB[1], None)])
                nk = min(8, NQT * 4 - kg * 8)
                P.op("dve", lambda e, v_=v_, kg=kg, nk=nk: e.tensor_copy(out=v_.ap[:, kg * 8:kg * 8 + nk, 0:64], in_=self.bank(1, nk * 64).rearrange("p (a b) -> p a b", a=nk)),
                     reads=[(PB[1], None)], writes=[(v_.b, ("v", kg))])
            steps = [(qt, kb) for qt in range(NQT) for kb in range(4 * qt + 4)]
            SB = (2, 3, 1)

            def geom(i):
                qt, kb = steps[i]
                diag = kb >= 4 * qt
                j = kb - 4 * qt if diag else 0
                return qt, kb, diag, j, qt * 512 + 128 * j, 512 - 128 * j, SB[i % 3], PT[i % 3]

            def emit_score(i):
                qt, kb, diag, j, q0, N, sb, pt = geom(i)
                P.op("pe", lambda e, kb=kb, q0=q0, N=N, sb=sb, kt_=kt_, qt_=qt_: e.matmul(out=self.bank(sb, N), lhsT=kt_.ap[:, kb * 128:(kb + 1) * 128], rhs=qt_.ap[:, q0:q0 + N], start=True, stop=True),
                     reads=[(kt_.b, None), (qt_.b, None)], writes=[(PB[sb], None)])
                P.op("act", lambda e, pt=pt, N=N, sb=sb: e.activation(out=pt.ap[:, 0:N], in_=self.bank(sb, N), func=AF.Exp, scale=SCALE),
                     reads=[(PB[sb], None)], writes=[(pt.b, None)])
                if diag:
                    P.op("pool", lambda e, pt=pt: e.memset(pt.ap[64:128, 0:64], 0.0), reads=[(pt.b, None)], writes=[(pt.b, None)])

            def emit_pv(i):
                qt, kb, diag, j, q0, N, sb, pt = geom(i)
                for qb in range(j, 4):
                    P.op("pe", lambda e, pt=pt, kb=kb, qb=qb, j=j, qt=qt, v_=v_: e.matmul(out=self.bank(4 + qb, 65), lhsT=pt.ap[:, (qb - j) * 128:(qb - j + 1) * 128], rhs=v_.ap[:, kb, :],
                                                                                  start=(kb == 0), stop=(kb == 4 * qt + qb)),
                         reads=[(pt.b, None), (v_.b, None)], writes=[(PB[4 + qb], None)])
                if kb == 4 * qt + 3:
                    for qb in range(4):
                        o_ = on[qb % 2]
                        P.op("dve", lambda e, qb=qb: e.reciprocal(out=rden.ap[:, qb:qb + 1], in_=self.bank(4 + qb, 1, c0=64)), reads=[(PB[4 + qb], None)], writes=[(rden.b, qb)])
                        P.op("dve", lambda e, qb=qb, o_=o_: e.tensor_scalar(out=o_.ap, in0=self.bank(4 + qb, 64), scalar1=rden.ap[:, qb:qb + 1], scalar2=None, op0=ALU.mult),
                             reads=[(PB[4 + qb], None), (rden.b, qb)], writes=[(o_.b, None)])
                        P.op("pe", lambda e, qb=qb, o_=o_: e.transpose(out=self.bankb(0, 128, p0=0, p1=64, c0=qb * 128), in_=o_.ap, identity=self.idb.ap),
                             reads=[(o_.b, None), (self.idb.b, None)], writes=[(PB[0], None)])
                    P.op("act", lambda e, qt=qt, oh=oh: e.copy(out=oh.ap[0:64, qt * 512:(qt + 1) * 512], in_=self.bankb(0, 512, p0=0, p1=64)), reads=[(PB[0], None)], writes=[(oh.b, qt)])

            LA = 2
            for i in range(min(LA, len(steps))):
                emit_score(i)
            for i in range(len(steps)):
                if i + LA < len(steps):
                    emit_score(i + LA)
                emit_pv(i)
            P.dma(d["oT"][h * 64:(h + 1) * 64, 0:NT * 128], oh.ap[0:64, 0:NT * 128], reads=[(oh.b, None)], writes=[(self.oT_dep, h)])
        stop_at(22)
        P.barrier()
        self.off = mark
        oTt = [self.alloc("oTt%d" % i, [8, 512], BF16) for i in range(2)]
        xres = [self.alloc("xresc%d" % i, [D], F32) for i in range(2)]
        tmps = [self.ln_tmp("ma", inplace=True), self.ln_tmp("mb", inplace=True)]
        oTv = d["oT"].rearrange("(c p) t -> p c t", p=128)
        nb = 0
        for tt in range(NQT):
            ot = oTt[tt % 2]
            P.dma(ot.ap, oTv[:, :, tt * 512:(tt + 1) * 512], reads=[(self.oT_dep, None)], writes=[(ot.b, None)])
            for blk in range(4):
                tok0 = tt * 512 + blk * 128
                xr = xres[nb % 2]
                nb += 1
                P.dma(xr.ap, xin[tok0:tok0 + 128, :], writes=[(xr.b, None)])
                yb = 1 + 2 * (nb % 2)
                for half in range(2):
                    for c in range(8):
                        P.op("pe", lambda e, c=c, half=half, blk=blk, ot=ot, yb=yb: e.matmul(out=self.bank(yb + half), lhsT=ot.ap[:, c, blk * 128:(blk + 1) * 128], rhs=wout.ap[:, c, half * 512:(half + 1) * 512],
                                                                                             start=(c == 0), stop=(c == 7)),
                             reads=[(ot.b, None), (wout.b, None)], writes=[(PB[yb + half], None)])
                self.epilogue(yb, xr.ap, (xr.b, None), lng, lnb, xout, tok0, tmps[nb % 2])


def make_consts():
    c = np.zeros((128, 1024), np.float32)
    c[:, 0:128] = np.eye(128, dtype=np.float32)
    t = np.arange(128)
    same = (t[:, None] // 64) == (t[None, :] // 64)
    c[:, 128:256] = (same & (t[:, None] <= t[None, :])).astype(np.float32)
    c[:, 256:384] = (same & (t[:, None] > t[None, :])).astype(np.float32)
    c[:, 384:512] = (t[:, None] < 64).astype(np.float32) * np.ones((1, 128), np.float32)
    c[:, 512:640] = (t[:, None] >= 64).astype(np.float32) * np.ones((1, 128), np.float32)
    c[:, 640:768] = (t[:, None] <= t[None, :]).astype(np.float32)
    i = t % 32
    c[:, 768] = (10000.0 ** (-((i % 16).astype(np.float32) * 2.0 / 32.0))).astype(np.float32)
    c[:, 769] = np.where(i < 16, -1.0, 1.0)
    return c


def build(sublayers, x_from_input=True):
    nc = bass.Bass("TRN2", target_bir_lowering=False)
    dram = {}

    def din(name, shape, dt=F32):
        dram[name] = nc.dram_tensor(name, list(shape), dt, kind="ExternalInput").ap()

    din("x", [L, D])
    din("consts", [128, 1024])
    din("ffn_w_in", [DEPTH, D, 2 * FH])
    din("ffn_w_out", [DEPTH, FH, D])
    din("ffn_cw", [DEPTH, 128, 44, 3])
    din("ffn_cb", [DEPTH, 128, 44])
    din("ssm_w_in", [2, D, 6176])
    din("ssm_w_out", [2, 2048, D])
    din("ssm_cw", [2, 128, 32, 4])
    din("ssm_cb", [2, 128, 32])
    din("ssm_small", [2, 96])
    din("ssm_norm_g", [2, 2048])
    din("sg_w_in", [1, D, 4096])
    din("sg_w_out", [1, 2048, D])
    din("sg_b_in", [1, 4096])
    din("sg_ln_g", [1, 2048])
    din("sg_ln_b", [1, 2048])
    din("sg_w_s", [1, 8, 128, 128])
    din("sg_bs", [128, 8])
    din("positions", [1, L], I32)
    din("mla_w_in", [1, D, 672])
    din("mla_w_q_b", [1, 384, 1536])
    din("mla_w_kv_b", [1, 256, 2048])
    din("mla_w_out", [1, D, D])
    din("mla_q_norm_g", [1, 384])
    din("mla_kv_norm_g", [1, 256])
    din("ln_g", [DEPTH * 2, D])
    din("ln_b", [DEPTH * 2, D])
    out = nc.dram_tensor("out", [L, D], F32, kind="ExternalOutput").ap()
    xa = nc.dram_tensor("xa", [L, D], F32, kind="Internal").ap()
    xb = nc.dram_tensor("xb", [L, D], F32, kind="Internal").ap()
    dram["oT"] = nc.dram_tensor("oT", [D, L], BF16, kind="Internal").ap()
    dram["ynT"] = nc.dram_tensor("ynT", [2048, L], BF16, kind="Internal").ap()
    P = Prog(nc)
    with ExitStack() as es:
        arena = es.enter_context(nc.sbuf_tensor("arena", [128, ARENA_ELEMS], BF16))
        ps = es.enter_context(nc.psum_tensor("ps", [128, 4096], F32))
        sems = {e: es.enter_context(nc.semaphore("s_" + e)) for e in ENGS}
        dsems = [es.enter_context(nc.semaphore("d%d" % i)) for i in range(NDSEM)]
        block = es.enter_context(nc.Block())
        kb = KB(nc, P, arena, ps, dram)
        kb.xbuf_dep = {id(xa): Buf("xa"), id(xb): Buf("xb"), id(out): Buf("out"), id(dram["x"]): Buf("xin")}
        kb.oT_dep = Buf("oT")
        kb.ynT_dep = Buf("ynT")
        kb.setup_consts()
        cur = dram["x"]
        n = len(sublayers)
        for si, (kind, li) in enumerate(sublayers):
            dst = out if si == n - 1 else (xa if cur is not xa else xb)
            try:
                if kind == "ffn":
                    kb.ffn_phase(li, cur, dst)
                elif li % 3 == 0:
                    (kb.ssd_phase if os.environ.get("K_OLDSSD") else kb.ssd_phase2)(li, li // 3, cur, dst)
                elif li % 3 == 1:
                    kb.sg_phase(li, cur, dst)
                elif li % 3 == 2:
                    kb.mla_phase(li, cur, dst)
                else:
                    raise NotImplementedError((kind, li))
            except StopBuild:
                break
            cur = dst
        P.barrier()
        P.finalize(block, sems, dsems)
    kb_stats = {e: len(P.ins[e]) for e in ENGS}
    print("instr counts", kb_stats, "dmas", P.ndma)
    return nc


def prep_shared(inputs):
    f = lambda a: np.ascontiguousarray(a, dtype=np.float32)
    sh = {"consts": make_consts()}
    sh["ffn_w_in"] = f(inputs["ffn_w_in"])
    sh["ffn_w_out"] = f(inputs["ffn_w_out"])
    sh["ffn_cw"] = f(np.asarray(inputs["ffn_conv_w"]).reshape(DEPTH, 3, 44, 128).transpose(0, 3, 2, 1))
    sh["ffn_cb"] = f(np.asarray(inputs["ffn_conv_b"]).reshape(DEPTH, 44, 128).transpose(0, 2, 1))
    sh["ssm_w_in"] = f(inputs["ssm_w_in"])
    sh["ssm_w_out"] = f(inputs["ssm_w_out"])
    sh["ssm_cw"] = f(np.asarray(inputs["ssm_conv_w"]).reshape(2, 4, 32, 128).transpose(0, 3, 2, 1))
    sh["ssm_cb"] = f(np.asarray(inputs["ssm_conv_b"]).reshape(2, 32, 128).transpose(0, 2, 1))
    sh["ssm_small"] = f(np.concatenate([np.asarray(inputs["ssm_dt_bias"]), np.asarray(inputs["ssm_a_log"]), np.asarray(inputs["ssm_d"])], axis=1))
    sh["ssm_norm_g"] = f(inputs["ssm_norm_g"])
    sh["sg_w_in"] = f(inputs["sg_w_in"])
    sh["sg_w_out"] = f(inputs["sg_w_out"])
    sh["sg_b_in"] = f(inputs["sg_b_in"])
    sh["sg_ln_g"] = f(inputs["sg_ln_g"])
    sh["sg_ln_b"] = f(inputs["sg_ln_b"])
    sh["sg_w_s"] = f(inputs["sg_w_s"])
    sh["sg_bs"] = f(np.asarray(inputs["sg_b_s"])[0].T)
    for k_ in ("mla_w_in", "mla_w_q_b", "mla_w_kv_b", "mla_w_out", "mla_q_norm_g", "mla_kv_norm_g"):
        sh[k_] = f(inputs[k_])
    sh["ln_g"] = f(np.asarray(inputs["ln_g"]).reshape(DEPTH * 2, D))
    sh["ln_b"] = f(np.asarray(inputs["ln_b"]).reshape(DEPTH * 2, D))
    return sh


def per_core(inputs, c):
    return {"positions": np.ascontiguousarray(np.asarray(inputs["positions"])[c:c + 1].astype(np.int32))}


ALL_SUBLAYERS = [(k, i) for i in range(DEPTH) for k in ("mix", "ffn")]


def kernel(**inputs):
    x = np.asarray(inputs["x"], dtype=np.float32)
    nc = build(ALL_SUBLAYERS)
    sh = prep_shared(inputs)
    in_maps = [dict(sh, x=np.ascontiguousarray(x[c]), **per_core(inputs, c)) for c in range(8)]
    res = run_bass_kernel_spmd(nc, in_maps, core_ids=list(range(8)))
    return np.stack([res.results[c]["out"] for c in range(8)], axis=0).astype(np.float32)
```

```python
import math, os
from contextlib import ExitStack
import numpy as np
import concourse.bass as bass
import concourse.mybir as mybir
from concourse.bass_utils import run_bass_kernel_spmd

F32 = mybir.dt.float32
BF16 = mybir.dt.bfloat16
I32 = mybir.dt.int32
ALU = mybir.AluOpType
AF = mybir.ActivationFunctionType

D = 1024
L = 4096
DEPTH = 4
ALPHA = (2.0 * DEPTH) ** 0.25
LN_EPS = 1e-5
RMS_EPS = 1e-6
FH = 2816
ENGS = ("pe", "act", "dve", "pool", "sp")
NDSEM = 24
ARENA_ELEMS = 106400


class Buf:
    def __init__(self, name):
        self.name = name
        self.st = {}

    def state(self, key):
        s = self.st.get(key)
        if s is None:
            s = {"w": None, "r": {}}
            self.st[key] = s
        return s


class Prog:
    def __init__(self, nc):
        self.nc = nc
        self.ins = {e: [] for e in ENGS}
        self.clock = {e: {f: 0 for f in ENGS} for e in ENGS}
        self.seen_dma = {e: set() for e in ENGS}
        self.ndma = 0
        self.dma_clock = {}
        self.active = True

    def _deps_for(self, reads, writes):
        deps = []
        for (b, k) in reads:
            keys = (k, None) if k is not None else tuple(b.st.keys()) + (None,)
            for kk in set(keys):
                s = b.st.get(kk)
                if s is not None and s["w"] is not None:
                    deps.append((s["w"], "raw"))
        for (b, k) in writes:
            keys = (k, None) if k is not None else tuple(b.st.keys()) + (None,)
            for kk in set(keys):
                s = b.st.get(kk)
                if s is None:
                    continue
                if s["w"] is not None:
                    deps.append((s["w"], "waw"))
                for t in s["r"].values():
                    deps.append((t, "war"))
        return deps

    def _record(self, tok, reads, writes):
        for (b, k) in reads:
            s = b.state(k)
            s["r"][tok[1] if tok[0] == "c" else tok] = tok
        for (b, k) in writes:
            if k is None:
                b.st = {}
            s = b.state(k)
            s["w"] = tok
            s["r"] = {}

    def _filter(self, eng, deps):
        waits = []
        clk = self.clock[eng]
        for (t, kind) in deps:
            if t[0] == "c":
                _, f, idx = t
                if f == eng and eng in ("pe", "sp"):
                    continue
                if clk[f] >= idx:
                    continue
                waits.append(t)
                oc = self.ins[f][idx - 1]["clock"]
                for g in ENGS:
                    if oc[g] > clk[g]:
                        clk[g] = oc[g]
                if clk[f] < idx:
                    clk[f] = idx
            else:
                if t in self.seen_dma[eng]:
                    continue
                self.seen_dma[eng].add(t)
                waits.append(t)
                oc = self.dma_clock[t]
                for g in ENGS:
                    if oc[g] > clk[g]:
                        clk[g] = oc[g]
        return waits

    def op(self, eng, fn, reads=(), writes=()):
        if not self.active:
            return None
        waits = self._filter(eng, self._deps_for(reads, writes))
        idx = len(self.ins[eng]) + 1
        self.ins[eng].append({"fn": fn, "waits": waits, "dma": None, "sig": False, "clock": dict(self.clock[eng])})
        tok = ("c", eng, idx)
        self._record(tok, reads, writes)
        return tok

    def dma(self, out_ap, in_ap, reads=(), writes=(), q="sp", **kw):
        if not self.active:
            return None
        waits = self._filter(q, self._deps_for(reads, writes))
        n = self.ndma
        self.ndma += 1
        tok = ("d", n)
        self.ins[q].append({"fn": (lambda e: e.dma_start(out=out_ap, in_=in_ap, **kw)), "waits": waits, "dma": n,
                            "sig": False, "clock": dict(self.clock[q])})
        self.dma_clock[tok] = dict(self.clock[q])
        self._record(tok, reads, writes)
        return tok

    def barrier(self):
        last = {e: max([i + 1 for i, r in enumerate(self.ins[e]) if r["fn"] is not None and r["dma"] is None] or [0]) for e in ENGS}
        nd = self.ndma
        for e in ENGS:
            deps = [(("c", f, last[f]), "raw") for f in ENGS if (f != e or e not in ("pe", "sp")) and f != "sp" and last[f] > 0]
            deps += [(("d", n), "raw") for n in range(max(0, nd - NDSEM), nd)]
            waits = self._filter(e, deps)
            if waits:
                self.ins[e].append({"fn": None, "waits": waits, "dma": None, "sig": False, "clock": dict(self.clock[e])})

    def finalize(self, block_ctx, sems, dsems):
        for e in ENGS:
            for rec in self.ins[e]:
                for t in rec["waits"]:
                    if t[0] == "c":
                        self.ins[t[1]][t[2] - 1]["sig"] = True
        sigval = {}
        for e in ENGS:
            c = 0
            for i, rec in enumerate(self.ins[e]):
                if rec["sig"]:
                    c += 1
                    sigval[(e, i + 1)] = c
        ndma = self.ndma

        def play(eng_name):
            def body(e):
                for rec in self.ins[eng_name]:
                    for t in rec["waits"]:
                        if t[0] == "c":
                            e.wait_ge(sems[t[1]], sigval[(t[1], t[2])])
                        else:
                            n = t[1]
                            e.wait_ge(dsems[n % NDSEM], 16 * (n // NDSEM + 1))
                    if rec["fn"] is None:
                        continue
                    if rec["dma"] is not None:
                        n = rec["dma"]
                        if n >= NDSEM:
                            e.wait_ge(dsems[n % NDSEM], 16 * (n // NDSEM))
                        rec["fn"](e).then_inc(dsems[n % NDSEM], 16)
                    else:
                        ins = rec["fn"](e)
                        if rec["sig"]:
                            ins.then_inc(sems[eng_name], 1)
                if eng_name == "sp":
                    for j in range(min(NDSEM, ndma)):
                        e.wait_ge(dsems[j], 16 * ((ndma - 1 - j) // NDSEM + 1))
            return body

        block_ctx.tensor(play("pe"))
        block_ctx.scalar(play("act"))
        block_ctx.vector(play("dve"))
        block_ctx.gpsimd(play("pool"))
        block_ctx.sync(play("sp"))


class StopBuild(Exception):
    pass


def stop_at(n):
    if int(os.environ.get('K_STOP', 99)) == n:
        raise StopBuild()


class T:
    def __init__(self, ap, name):
        self.ap = ap
        self.b = Buf(name)

    def __getitem__(self, k):
        return self.ap[k]


class KB:
    def __init__(self, nc, P, arena, ps, dram):
        self.nc, self.P, self.arena, self.ps, self.dram = nc, P, arena, ps, dram
        self.psb = ps[:].bitcast(BF16)
        self.PB = [Buf("psum%d" % i) for i in range(8)]
        self.off = 0
        self.base = 0
        self.cast_rr = 0

    def alloc(self, name, shape, dt, parts=128):
        size = {F32: 4, BF16: 2, I32: 4}[dt]
        n = int(np.prod(shape))
        nb = (n * size + 63) // 64 * 64
        e0 = self.off // 2
        self.off += nb
        assert self.off <= ARENA_ELEMS * 2, ("arena overflow", name, self.off)
        v = self.arena[0:parts, e0:e0 + n * size // 2]
        if dt != BF16:
            v = v.bitcast(dt)
        if len(shape) == 2:
            v = v.rearrange("p (a b) -> p a b", a=shape[0])
        elif len(shape) == 3:
            v = v.rearrange("p (a b c) -> p a b c", a=shape[0], b=shape[1])
        return T(v, name)

    def phase(self):
        self.P.barrier()
        self.off = self.base

    def bank(self, b, n=512, p0=0, p1=128, c0=0):
        return self.ps[p0:p1, b * 512 + c0: b * 512 + c0 + n]

    def bankb(self, b, n=1024, p0=0, p1=128, c0=0):
        return self.psb[p0:p1, b * 1024 + c0: b * 1024 + c0 + n]

    def setup_consts(self):
        P = self.P
        c = self.dram["consts"]
        self.idf = self.alloc("idf", [128], F32)
        self.masks = self.alloc("masks", [5, 128], F32)
        self.idb = self.alloc("idb", [128], BF16)
        self.mhalf = self.alloc("mhalf", [1], F32)
        P.dma(self.idf.ap, c[:, 0:128], writes=[(self.idf.b, None)])
        P.dma(self.masks.ap, c[:, 128:768].rearrange("p (a b) -> p a b", a=5), writes=[(self.masks.b, None)])
        P.op("dve", lambda e: e.tensor_copy(out=self.idb.ap, in_=self.idf.ap), reads=[(self.idf.b, None)], writes=[(self.idb.b, None)])
        P.op("pool", lambda e: e.memset(self.mhalf.ap, -0.5), writes=[(self.mhalf.b, None)])
        self.base = self.off

    def load_w(self, dst, src3, C, N, stg):
        P = self.P
        ncol = max(1, 2048 // N) if N <= 2048 else 1
        nsplit = (N + 2047) // 2048
        i = 0
        for c0 in range(0, C, ncol):
            c1 = min(C, c0 + ncol)
            for s in range(nsplit):
                n0 = s * 2048
                n1 = min(N, n0 + 2048)
                st = stg[self.cast_rr % len(stg)]
                w = (c1 - c0) * (n1 - n0)
                sv = st.ap[:, 0:w].rearrange("p (a b) -> p a b", a=c1 - c0)
                P.dma(sv, src3[:, c0:c1, n0:n1], writes=[(st.b, None)])
                eng = ("act", "dve")[self.cast_rr % 2]
                dv = dst.ap[:, c0:c1, n0:n1]
                if eng == "act":
                    P.op(eng, lambda e, dv=dv, sv=sv: e.copy(out=dv, in_=sv), reads=[(st.b, None)], writes=[(dst.b, None)])
                else:
                    P.op(eng, lambda e, dv=dv, sv=sv: e.tensor_copy(out=dv, in_=sv), reads=[(st.b, None)], writes=[(dst.b, None)])
                self.cast_rr += 1

    def bcast_load(self, name, src_row, n):
        t = self.alloc(name, [n], F32)
        self.P.dma(t.ap, src_row.partition_broadcast(128), writes=[(t.b, None)])
        return t

    def make_xT_block(self, xin, tok0, xres_view, xres_buf, xres_key, xbf, xT, xT_key, col0, tbank):
        P = self.P
        P.dma(xres_view, xin[tok0:tok0 + 128, :], writes=[(xres_buf, xres_key)])
        P.op("act", lambda e: e.copy(out=xbf.ap, in_=xres_view), reads=[(xres_buf, xres_key)], writes=[(xbf.b, None)])
        for c in range(8):
            P.op("pe", lambda e, c=c: e.transpose(out=self.bankb(tbank, 128, c0=c * 128), in_=xbf.ap[:, c * 128:(c + 1) * 128], identity=self.idb.ap),
                 reads=[(xbf.b, None), (self.idb.b, None)], writes=[(self.PB[tbank], None)])
        src = self.bankb(tbank, 1024).rearrange("p (c t) -> p c t", c=8)
        P.op("act", lambda e: e.copy(out=xT.ap[:, 0:8, col0:col0 + 128], in_=src), reads=[(self.PB[tbank], None)], writes=[(xT.b, xT_key)])

    def epilogue(self, ybank, xres_view, xres_dep, lng, lnb, xout, tok0, tmp):
        P = self.P
        r, o, st, mv, sc = tmp["r"], tmp["o"], tmp["st"], tmp["mv"], tmp["sc"]
        y = self.ps[:, ybank * 512: ybank * 512 + 1024]
        P.op("dve", lambda e: e.scalar_tensor_tensor(out=r.ap, in0=xres_view, scalar=ALPHA, in1=y, op0=ALU.mult, op1=ALU.add),
             reads=[xres_dep, (self.PB[ybank], None), (self.PB[ybank + 1], None)], writes=[(r.b, None)])
        self.layernorm(r, 1024, lng, lnb, o, st, mv, sc, LN_EPS)
        P.dma(xout[tok0:tok0 + 128, :], o.ap, reads=[(o.b, None)], writes=[(self.xbuf_dep[id(xout)], tok0)])

    def layernorm(self, r, n, lng, lnb, o, st, mv, sc, eps, out_ap=None, out_buf=None):
        P = self.P
        nchunk = n // 512
        for i in range(nchunk):
            P.op("dve", lambda e, i=i: e.bn_stats(out=st.ap[:, i * 6:(i + 1) * 6], in_=r.ap[:, i * 512:(i + 1) * 512]),
                 reads=[(r.b, None)], writes=[(st.b, i)])
        P.op("dve", lambda e: e.bn_aggr(out=mv.ap, in_=st.ap[:, 0:6 * nchunk]), reads=[(st.b, None)], writes=[(mv.b, None)])
        P.op("pool", lambda e: e.tensor_scalar(out=sc.ap[:, 0:1], in0=mv.ap[:, 1:2], scalar1=1.0, scalar2=eps, op0=ALU.mult, op1=ALU.add),
             reads=[(mv.b, None)], writes=[(sc.b, 0)])
        P.op("pool", lambda e: e.tensor_tensor(out=sc.ap[:, 1:2], in0=sc.ap[:, 0:1], in1=self.mhalf.ap, op=ALU.pow),
             reads=[(sc.b, 0), (self.mhalf.b, None)], writes=[(sc.b, 1)])
        P.op("dve", lambda e: e.tensor_scalar(out=sc.ap[:, 2:3], in0=mv.ap[:, 0:1], scalar1=sc.ap[:, 1:2], scalar2=-1.0, op0=ALU.mult, op1=ALU.mult),
             reads=[(mv.b, None), (sc.b, 1)], writes=[(sc.b, 2)])
        P.op("act", lambda e: e.activation(out=r.ap, in_=r.ap, func=AF.Identity, scale=sc.ap[:, 1:2], bias=sc.ap[:, 2:3]),
             reads=[(r.b, None), (sc.b, 1), (sc.b, 2)], writes=[(r.b, None)])
        P.op("pool", lambda e: e.tensor_tensor(out=o.ap, in0=r.ap, in1=lng.ap, op=ALU.mult), reads=[(r.b, None), (lng.b, None)], writes=[(o.b, None)])
        oo = o.ap if out_ap is None else out_ap
        ob = o if out_buf is None else out_buf
        P.op("dve", lambda e: e.tensor_tensor(out=oo, in0=o.ap, in1=lnb.ap, op=ALU.add), reads=[(o.b, None), (lnb.b, None)], writes=[(ob.b, None)])

    def ln_tmp(self, tag, inplace=False):
        r = self.alloc("r" + tag, [1024], F32)
        return {"r": r, "o": r if inplace else self.alloc("o" + tag, [1024], F32),
                "st": self.alloc("st" + tag, [24], F32), "mv": self.alloc("mv" + tag, [2], F32), "sc": self.alloc("sc" + tag, [4], F32)}

    def ffn_phase(self, li, xin, xout):
        P, d = self.P, self.dram
        self.phase()
        TT, PAD = 256, 2
        w1 = self.alloc("w1", [8, 2 * FH], BF16)
        w2 = self.alloc("w2", [22, D], BF16)
        cw = self.alloc("cw", [44, 3], F32)
        cb = self.alloc("cb", [44], F32)
        lng = self.bcast_load("lng", d["ln_g"][li * 2 + 1:li * 2 + 2, :], D)
        lnb = self.bcast_load("lnb", d["ln_b"][li * 2 + 1:li * 2 + 2, :], D)
        mark = self.off
        stg = [self.alloc("stg%d" % i, [2048], F32) for i in range(3)]
        P.dma(cw.ap, d["ffn_cw"][li], writes=[(cw.b, None)])
        P.dma(cb.ap, d["ffn_cb"][li], writes=[(cb.b, None)])
        self.load_w(w1, d["ffn_w_in"][li].rearrange("(c p) n -> p c n", p=128), 8, 2 * FH, stg)
        self.load_w(w2, d["ffn_w_out"][li].rearrange("(c p) n -> p c n", p=128), 22, D, stg)
        P.barrier()
        self.off = mark
        xres = [self.alloc("xres%d" % i, [2, D], F32) for i in range(2)]
        xbf = [self.alloc("xbf%d" % i, [D], BF16) for i in range(2)]
        xT = [self.alloc("xT%d" % i, [8, TT + PAD], BF16) for i in range(2)]
        acc = [self.alloc("acc%d" % i, [TT], F32) for i in range(4)]
        sg = [self.alloc("sg%d" % i, [TT], F32) for i in range(2)]
        pTs = [self.alloc("pT%d" % i, [22, TT], BF16) for i in range(2)]
        tmp = self.ln_tmp("f", inplace=True)
        P.op("pool", lambda e: e.memset(xT[0].ap[:, :, 0:PAD], 0.0), writes=[(xT[0].b, "pad")])
        TB, UP, DB = 0, (1, 2, 3, 4), 5
        NTF = int(os.environ.get('K_NTF', L // TT))
        jjb = [0]

        def load_tile(t):
            cur = xT[t % 2]
            for blk in range(2):
                self.make_xT_block(xin, t * TT + blk * 128, xres[t % 2].ap[:, blk, :], xres[t % 2].b, blk, xbf[blk], cur, blk, PAD + blk * 128, TB)
            if t > 0:
                prv = xT[(t - 1) % 2]
                P.op("pool", lambda e, cur=cur, prv=prv: e.tensor_copy(out=cur.ap[:, :, 0:PAD], in_=prv.ap[:, :, TT:TT + PAD]),
                     reads=[(prv.b, 1)], writes=[(cur.b, "pad")])

        def down_epi(t):
            pT = pTs[t % 2]
            for blk in range(2):
                for half in range(2):
                    for c in range(22):
                        P.op("pe", lambda e, c=c, half=half, blk=blk, pT=pT: e.matmul(out=self.bank(DB + half), lhsT=pT.ap[:, c, blk * 128:(blk + 1) * 128],
                                                                                      rhs=w2.ap[:, c, half * 512:(half + 1) * 512], start=(c == 0), stop=(c == 21)),
                             reads=[(pT.b, c), (w2.b, None)], writes=[(self.PB[DB + half], None)])
                self.epilogue(DB, xres[t % 2].ap[:, blk, :], (xres[t % 2].b, blk), lng, lnb, xout, t * TT + blk * 128, tmp)

        def up_chunk(t, c):
            cur = xT[t % 2]
            pT = pTs[t % 2]
            for which in range(2):
                j = which * 22 + c
                bk = UP[jjb[0] % 4]
                a = acc[jjb[0] % 4]
                jjb[0] += 1
                for k in range(8):
                    P.op("pe", lambda e, k=k, j=j, bk=bk, cur=cur: e.matmul(out=self.bank(bk, TT + PAD), lhsT=w1.ap[:, k, j * 128:(j + 1) * 128],
                                                                           rhs=cur.ap[:, k, :], start=(k == 0), stop=(k == 7)),
                         reads=[(w1.b, None), (cur.b, None)], writes=[(self.PB[bk], None)])
                P.op("act", lambda e, j=j, bk=bk, a=a: e.activation(out=a.ap, in_=self.bank(bk, TT, c0=2), func=AF.Identity,
                                                                     scale=cw.ap[:, j, 2:3], bias=cb.ap[:, j:j + 1]),
                     reads=[(self.PB[bk], None), (cw.b, None), (cb.b, None)], writes=[(a.b, None)])
                for kk in (1, 0):
                    P.op("dve", lambda e, j=j, bk=bk, a=a, kk=kk: e.scalar_tensor_tensor(out=a.ap, in0=self.bank(bk, TT, c0=kk), scalar=cw.ap[:, j, kk:kk + 1],
                                                                                        in1=a.ap, op0=ALU.mult, op1=ALU.add),
                         reads=[(self.PB[bk], None), (cw.b, None), (a.b, None)], writes=[(a.b, None)])
                s_ = sg[c % 2]
                if which == 0:
                    P.op("act", lambda e, a=a, s_=s_: e.activation(out=s_.ap, in_=a.ap, func=AF.Silu), reads=[(a.b, None)], writes=[(s_.b, None)])
                else:
                    P.op("pool", lambda e, a=a, s_=s_, c=c, pT=pT: e.tensor_tensor(out=pT.ap[:, c, :], in0=s_.ap, in1=a.ap, op=ALU.mult),
                         reads=[(a.b, None), (s_.b, None)], writes=[(pT.b, c)])

        load_tile(0)
        for t in range(NTF):
            for c in range(22):
                if c == 6 and t > 0:
                    down_epi(t - 1)
                if c == 14 and t + 1 < NTF:
                    load_tile(t + 1)
                up_chunk(t, c)
        down_epi(NTF - 1)

    def ssd_phase(self, li, j, xin, xout):
        P, d = self.P, self.dram
        self.phase()
        PAD = 3
        PB = self.PB
        win = self.alloc("win", [8, 6176], BF16)
        wout = self.alloc("wout", [16, D], BF16)
        cw = self.alloc("scw", [32, 4], F32)
        cb = self.alloc("scb", [32], F32)
        small = self.bcast_load("ssmall", d["ssm_small"][j:j + 1, :], 96)
        ng = self.bcast_load("sng", d["ssm_norm_g"][j:j + 1, :], 2048)
        lng = self.bcast_load("lng", d["ln_g"][li * 2:li * 2 + 1, :], D)
        lnb = self.bcast_load("lnb", d["ln_b"][li * 2:li * 2 + 1, :], D)
        mark = self.off
        stg = [self.alloc("stg%d" % i, [2048], F32) for i in range(3)]
        P.dma(cw.ap, d["ssm_cw"][j], writes=[(cw.b, None)])
        P.dma(cb.ap, d["ssm_cb"][j], writes=[(cb.b, None)])
        self.load_w(win, d["ssm_w_in"][j].rearrange("(c p) n -> p c n", p=128), 8, 6176, stg)
        self.load_w(wout, d["ssm_w_out"][j].rearrange("(c p) n -> p c n", p=128), 16, D, stg)
        P.barrier()
        self.off = mark
        A_b = self.alloc("A_b", [32], F32)
        H = self.alloc("H", [2048], F32)
        Hb0 = self.alloc("Hb0", [2048], BF16)
        Hb1 = self.alloc("Hb1", [2048], BF16)
        xres = [self.alloc("xres%d" % i, [D], F32) for i in range(2)]
        xbf = self.alloc("xbf", [D], BF16)
        xT = [self.alloc("xT%d" % i, [8, 128 + PAD], BF16) for i in range(2)]
        acc = [self.alloc("acc%d" % i, [128], F32) for i in range(4)]
        xsf = [self.alloc("xsf%d" % i, [128], F32) for i in range(2)]
        BT = [self.alloc("BT%d" % i, [128], BF16) for i in range(2)]
        CT = [self.alloc("CT%d" % i, [128], BF16) for i in range(2)]
        dtt = self.alloc("dtt", [32], F32)
        dte = self.alloc("dte", [32], F32)
        dt_ = self.alloc("dt_", [32], F32)
        dtA = self.alloc("dtA", [32], F32)
        ex = self.alloc("ex", [4, 32], F32)
        dtw = self.alloc("dtw", [32], F32)
        Lg = self.alloc("Lg", [4, 128], F32)
        E = self.alloc("E", [4, 128], F32)
        CBm = self.alloc("CBm", [128], F32)
        M = self.alloc("M", [4, 128], BF16)
        xs_tok = self.alloc("xs_tok", [4, 64], F32)
        Btok = self.alloc("Btok", [128], BF16)
        zs = self.alloc("zs", [256], F32)
        xdt = self.alloc("xdt", [4, 64], BF16)
        xw = self.alloc("xw", [2, 4, 64], BF16)
        dtw2 = self.alloc("dtw2", [2, 32], F32)
        t1 = self.alloc("t1", [4, 64], F32)
        t2 = self.alloc("t2", [4, 64], F32)
        ss = self.alloc("ss", [4], F32)
        hcd = self.alloc("hcd", [4, 64], F32)
        yn = self.alloc("yn", [256], BF16)
        ynT = self.alloc("ynT", [16, 128], BF16)
        tmp = self.ln_tmp("s", inplace=True)
        dtb, alog, dsk = small.ap[:, 0:32], small.ap[:, 32:64], small.ap[:, 64:96]
        Mle, Mgt = self.masks.ap[:, 0, :], self.masks.ap[:, 1, :]
        P.op("act", lambda e: e.activation(out=A_b.ap, in_=alog, func=AF.Exp), reads=[(small.b, None)], writes=[(A_b.b, None)])
        P.op("dve", lambda e: e.tensor_scalar(out=A_b.ap, in0=A_b.ap, scalar1=-1.0, scalar2=None, op0=ALU.mult), reads=[(A_b.b, None)], writes=[(A_b.b, None)])
        P.op("pool", lambda e: e.memset(H.ap, 0.0), writes=[(H.b, None)])
        P.op("pool", lambda e: e.memset(Hb0.ap, 0.0), writes=[(Hb0.b, None)])
        P.op("pool", lambda e: e.memset(xT[0].ap[:, :, 0:PAD], 0.0), writes=[(xT[0].b, "pad")])
        cc = 0
        for t in range(int(os.environ.get('K_NT', L // 128))):
            cur = xT[t % 2]
            xr = xres[t % 2]
            self.make_xT_block(xin, t * 128, xr.ap, xr.b, None, xbf, cur, "blk", PAD, 0)
            if t > 0:
                prv = xT[(t - 1) % 2]
                P.op("pool", lambda e, cur=cur, prv=prv: e.tensor_copy(out=cur.ap[:, :, 0:PAD], in_=prv.ap[:, :, 128:128 + PAD]),
                     reads=[(prv.b, "blk")], writes=[(cur.b, "pad")])
            for k in range(8):
                P.op("pe", lambda e, k=k, cur=cur: e.matmul(out=self.bank(3, 32), lhsT=cur.ap[:, k, PAD:PAD + 128], rhs=win.ap[:, k, 6144:6176],
                                                          start=(k == 0), stop=(k == 7)),
                     reads=[(cur.b, None), (win.b, None)], writes=[(PB[3], None)])
            P.op("dve", lambda e: e.tensor_tensor(out=dtt.ap, in0=self.bank(3, 32), in1=dtb, op=ALU.add), reads=[(PB[3], None), (small.b, None)], writes=[(dtt.b, None)])
            P.op("act", lambda e: e.activation(out=dte.ap, in_=dtt.ap, func=AF.Exp), reads=[(dtt.b, None)], writes=[(dte.b, None)])
            P.op("act", lambda e: e.activation(out=dt_.ap, in_=dte.ap, func=AF.Ln, bias=1.0), reads=[(dte.b, None)], writes=[(dt_.b, None)])
            P.op("dve", lambda e: e.tensor_tensor(out=dtA.ap, in0=dt_.ap, in1=A_b.ap, op=ALU.mult), reads=[(dt_.b, None), (A_b.b, None)], writes=[(dtA.b, None)])
            for m in range(4):
                P.op("pe", lambda e, m=m: e.matmul(out=self.bank(3, 32, c0=64 + m * 32), lhsT=self.masks.ap[:, m, :], rhs=dtA.ap, start=True, stop=True),
                     reads=[(self.masks.b, None), (dtA.b, None)], writes=[(PB[3], None)])
            P.op("act", lambda e: e.activation(out=ex.ap, in_=self.bank(3, 128, c0=64).rearrange("p (a b) -> p a b", a=4), func=AF.Exp),
                 reads=[(PB[3], None)], writes=[(ex.b, None)])
            P.op("dve", lambda e: e.tensor_tensor(out=dtw.ap, in0=dt_.ap, in1=ex.ap[:, 1, :], op=ALU.mult), reads=[(dt_.b, None), (ex.b, None)], writes=[(dtw.b, None)])
            stop_at(1)
            for c_ in range(2):
                P.op("dve", lambda e, c_=c_: e.tensor_scalar(out=dtw2.ap[:, c_, :], in0=dtw.ap, scalar1=self.masks.ap[:, 2 + c_, 0:1], scalar2=None, op0=ALU.mult),
                     reads=[(dtw.b, None), (self.masks.b, None)], writes=[(dtw2.b, c_)])
            for g in range(8):
                hs = slice(4 * g, 4 * g + 4)
                bt, ct = BT[g % 2], CT[g % 2]
                chunks = [(2048 + (2 * g) * 128, xsf[0]), (2048 + (2 * g + 1) * 128, xsf[1]), (4096 + g * 128, bt), (5120 + g * 128, ct)]
                for (col, dst) in chunks:
                    jc = (col - 2048) // 128
                    bk = (1, 2)[cc % 2]
                    co = 0
                    a = acc[cc % 4]
                    cc += 1
                    for k in range(8):
                        P.op("pe", lambda e, k=k, col=col, bk=bk, cur=cur, co=co: e.matmul(out=self.bank(bk, 128 + PAD, c0=co), lhsT=win.ap[:, k, col:col + 128], rhs=cur.ap[:, k, :],
                                                                                   start=(k == 0), stop=(k == 7)),
                             reads=[(win.b, None), (cur.b, None)], writes=[(PB[bk], None)])
                    P.op("act", lambda e, jc=jc, bk=bk, a=a, co=co: e.activation(out=a.ap, in_=self.bank(bk, 128, c0=co + 3), func=AF.Identity,
                                                                           scale=cw.ap[:, jc, 3:4], bias=cb.ap[:, jc:jc + 1]),
                         reads=[(PB[bk], None), (cw.b, None), (cb.b, None)], writes=[(a.b, None)])
                    for kk in (2, 1, 0):
                        P.op("dve", lambda e, jc=jc, bk=bk, a=a, kk=kk, co=co: e.scalar_tensor_tensor(out=a.ap, in0=self.bank(bk, 128, c0=co + kk), scalar=cw.ap[:, jc, kk:kk + 1],
                                                                                              in1=a.ap, op0=ALU.mult, op1=ALU.add),
                             reads=[(PB[bk], None), (cw.b, None), (a.b, None)], writes=[(a.b, None)])
                    P.op("act", lambda e, a=a, dst=dst: e.activation(out=dst.ap, in_=a.ap, func=AF.Silu), reads=[(a.b, None)], writes=[(dst.b, None)])
                stop_at(2)
                for i in range(2):
                    P.op("pe", lambda e, i=i: e.transpose(out=self.bank(5, 128, c0=128 + i * 128), in_=xsf[i].ap, identity=self.idf.ap),
                         reads=[(xsf[i].b, None), (self.idf.b, None)], writes=[(PB[5], None)])
                P.op("act", lambda e: e.copy(out=xs_tok.ap.rearrange("p a b -> p (a b)"), in_=self.bank(5, 256, c0=128)), reads=[(PB[5], None)], writes=[(xs_tok.b, None)])
                P.op("pe", lambda e, bt=bt: e.transpose(out=self.bankb(5, 128, c0=768), in_=bt.ap, identity=self.idb.ap),
                     reads=[(bt.b, None), (self.idb.b, None)], writes=[(PB[5], None)])
                P.op("dve", lambda e: e.tensor_copy(out=Btok.ap, in_=self.bankb(5, 128, c0=768)), reads=[(PB[5], None)], writes=[(Btok.b, None)])
                stop_at(3)
                for k in range(8):
                    P.op("pe", lambda e, k=k, g=g, cur=cur: e.matmul(out=self.bank(6, 256), lhsT=cur.ap[:, k, PAD:PAD + 128], rhs=win.ap[:, k, g * 256:(g + 1) * 256],
                                                                   start=(k == 0), stop=(k == 7)),
                         reads=[(cur.b, None), (win.b, None)], writes=[(PB[6], None)])
                P.op("act", lambda e: e.activation(out=zs.ap, in_=self.bank(6, 256), func=AF.Silu), reads=[(PB[6], None)], writes=[(zs.b, None)])
                stop_at(4)
                P.op("pool", lambda e, hs=hs: e.tensor_tensor(out=xdt.ap, in0=xs_tok.ap, in1=dt_.ap[:, hs].unsqueeze(2).to_broadcast([128, 4, 64]), op=ALU.mult),
                     reads=[(xs_tok.b, None), (dt_.b, None)], writes=[(xdt.b, None)])
                for c_ in range(2):
                    P.op("pool", lambda e, hs=hs, c_=c_: e.tensor_tensor(out=xw.ap[:, c_], in0=xs_tok.ap, in1=dtw2.ap[:, c_, hs].unsqueeze(2).to_broadcast([128, 4, 64]), op=ALU.mult),
                         reads=[(xs_tok.b, None), (dtw2.b, None)], writes=[(xw.b, c_)])
                stop_at(5)
                P.op("pool", lambda e, hs=hs: e.tensor_tensor(out=Lg.ap, in0=self.masks.ap[:, 1:2, :].to_broadcast([128, 4, 128]),
                                                              in1=dtA.ap[:, hs].unsqueeze(2).to_broadcast([128, 4, 128]), op=ALU.mult),
                     reads=[(self.masks.b, None), (dtA.b, None)], writes=[(Lg.b, None)])
                for hh in range(4):
                    P.op("pe", lambda e, hh=hh: e.matmul(out=self.bank(4, 128, c0=hh * 128), lhsT=Lg.ap[:, hh, :], rhs=Mle, start=True, stop=True),
                         reads=[(Lg.b, None), (self.masks.b, None)], writes=[(PB[4], None)])
                P.op("act", lambda e: e.activation(out=E.ap.rearrange("p a b -> p (a b)"), in_=self.bank(4, 512), func=AF.Exp), reads=[(PB[4], None)], writes=[(E.b, None)])
                stop_at(6)
                P.op("pe", lambda e, bt=bt, ct=ct: e.matmul(out=self.bank(5, 128), lhsT=bt.ap, rhs=ct.ap, start=True, stop=True),
                     reads=[(bt.b, None), (ct.b, None)], writes=[(PB[5], None)])
                P.op("dve", lambda e: e.tensor_tensor(out=CBm.ap, in0=self.bank(5, 128), in1=Mle, op=ALU.mult), reads=[(PB[5], None), (self.masks.b, None)], writes=[(CBm.b, None)])
                P.op("dve", lambda e: e.tensor_tensor(out=M.ap, in0=E.ap, in1=CBm.ap.unsqueeze(1).to_broadcast([128, 4, 128]), op=ALU.mult),
                     reads=[(E.b, None), (CBm.b, None)], writes=[(M.b, None)])
                for hh in range(4):
                    P.op("pe", lambda e, hh=hh: e.matmul(out=self.bank(6, 64, c0=256 + hh * 64), lhsT=M.ap[:, hh, :], rhs=xdt.ap[:, hh, :], start=True, stop=True),
                         reads=[(M.b, None), (xdt.b, None)], writes=[(PB[6], None)])
                stop_at(7)
                P.op("pe", lambda e: e.matmul(out=self.bank(7, 256, c0=256), lhsT=Btok.ap, rhs=xw.ap[:, 0].rearrange("p a b -> p (a b)"), start=True, stop=True),
                     reads=[(Btok.b, None), (xw.b, None)], writes=[(PB[7], None)])
                P.op("pe", lambda e: e.matmul(out=self.bank(3, 256, c0=256), lhsT=Btok.ap, rhs=xw.ap[:, 1].rearrange("p a b -> p (a b)"), start=True, stop=True),
                     reads=[(Btok.b, None), (xw.b, None)], writes=[(PB[3], None)])
                stop_at(8)
                for hh in range(4):
                    h = 4 * g + hh
                    P.op("pe", lambda e, hh=hh, h=h, ct=ct: e.matmul(out=self.bank(7, 64, p0=0, p1=64, c0=hh * 64), lhsT=ct.ap[:, 0:64], rhs=Hb0.ap[:, h * 64:(h + 1) * 64],
                                                                      start=True, stop=True),
                         reads=[(ct.b, None), (Hb0.b, g)], writes=[(PB[7], None)])
                stop_at(81)
                Hg = H.ap[:, g * 256:(g + 1) * 256].rearrange("p (a b) -> p a b", a=4)
                Hgf = H.ap[:, g * 256:(g + 1) * 256]
                P.op("dve", lambda e, hs=hs, Hg=Hg: e.tensor_tensor(out=hcd.ap, in0=Hg, in1=ex.ap[:, 2, hs].unsqueeze(2).to_broadcast([128, 4, 64]), op=ALU.mult),
                     reads=[(H.b, g), (ex.b, None)], writes=[(hcd.b, None)])
                stop_at(811)
                P.op("dve", lambda e, Hgf=Hgf: e.scalar_tensor_tensor(out=Hgf, in0=self.bank(7, 256, c0=256), scalar=1.0, in1=hcd.ap.rearrange("p a b -> p (a b)"), op0=ALU.mult, op1=ALU.add),
                     reads=[(hcd.b, None), (PB[7], None)], writes=[(H.b, g)])
                stop_at(812)
                P.op("act", lambda e, g=g, Hgf=Hgf: e.copy(out=Hb1.ap[:, g * 256:(g + 1) * 256], in_=Hgf), reads=[(H.b, g)], writes=[(Hb1.b, g)])
                stop_at(82)
                for hh in range(4):
                    h = 4 * g + hh
                    P.op("pe", lambda e, hh=hh, h=h, ct=ct: e.matmul(out=self.bank(7, 64, p0=64, p1=128, c0=hh * 64), lhsT=ct.ap[:, 64:128], rhs=Hb1.ap[:, h * 64:(h + 1) * 64],
                                                                      start=True, stop=True),
                         reads=[(ct.b, None), (Hb1.b, g)], writes=[(PB[7], None)])
                P.op("dve", lambda e, hs=hs, Hg=Hg: e.tensor_tensor(out=hcd.ap, in0=Hg, in1=ex.ap[:, 3, hs].unsqueeze(2).to_broadcast([128, 4, 64]), op=ALU.mult),
                     reads=[(H.b, g), (ex.b, None)], writes=[(hcd.b, None)])
                P.op("dve", lambda e, Hgf=Hgf: e.scalar_tensor_tensor(out=Hgf, in0=self.bank(3, 256, c0=256), scalar=1.0, in1=hcd.ap.rearrange("p a b -> p (a b)"), op0=ALU.mult, op1=ALU.add),
                     reads=[(hcd.b, None), (PB[3], None)], writes=[(H.b, g)])
                P.op("act", lambda e, g=g, Hgf=Hgf: e.copy(out=Hb0.ap[:, g * 256:(g + 1) * 256], in_=Hgf), reads=[(H.b, g)], writes=[(Hb0.b, g)])
                stop_at(9)
                P.op("dve", lambda e, hs=hs: e.tensor_tensor(out=t1.ap, in0=self.bank(7, 256).rearrange("p (a b) -> p a b", a=4),
                                                             in1=ex.ap[:, 0, hs].unsqueeze(2).to_broadcast([128, 4, 64]), op=ALU.mult),
                     reads=[(PB[7], None), (PB[7], None), (ex.b, None)], writes=[(t1.b, None)])
                P.op("dve", lambda e: e.tensor_tensor(out=t1.ap.rearrange("p a b -> p (a b)"), in0=t1.ap.rearrange("p a b -> p (a b)"), in1=self.bank(6, 256, c0=256), op=ALU.add),
                     reads=[(t1.b, None), (PB[6], None)], writes=[(t1.b, None)])
                P.op("pool", lambda e, hs=hs: e.tensor_tensor(out=t2.ap, in0=xs_tok.ap, in1=dsk[:, hs].unsqueeze(2).to_broadcast([128, 4, 64]), op=ALU.mult),
                     reads=[(xs_tok.b, None), (small.b, None)], writes=[(t2.b, None)])
                P.op("pool", lambda e: e.tensor_tensor(out=t1.ap, in0=t1.ap, in1=t2.ap, op=ALU.add), reads=[(t1.b, None), (t2.b, None)], writes=[(t1.b, None)])
                P.op("pool", lambda e: e.tensor_tensor(out=t1.ap.rearrange("p a b -> p (a b)"), in0=t1.ap.rearrange("p a b -> p (a b)"), in1=zs.ap, op=ALU.mult),
                     reads=[(t1.b, None), (zs.b, None)], writes=[(t1.b, None)])
                P.op("act", lambda e: e.activation(out=t2.ap.rearrange("p a b -> p (a b)"), in_=t1.ap.rearrange("p a b -> p (a b)"), func=AF.Square, accum_out=ss.ap[:, 0:1]),
                     reads=[(t1.b, None)], writes=[(t2.b, None), (ss.b, 0)])
                P.op("pool", lambda e: e.tensor_scalar(out=ss.ap[:, 1:2], in0=ss.ap[:, 0:1], scalar1=1.0 / 256.0, scalar2=RMS_EPS, op0=ALU.mult, op1=ALU.add),
                     reads=[(ss.b, 0)], writes=[(ss.b, 1)])
                P.op("pool", lambda e: e.tensor_tensor(out=ss.ap[:, 2:3], in0=ss.ap[:, 1:2], in1=self.mhalf.ap, op=ALU.pow),
                     reads=[(ss.b, 1), (self.mhalf.b, None)], writes=[(ss.b, 2)])
                P.op("dve", lambda e, g=g: e.scalar_tensor_tensor(out=yn.ap, in0=t1.ap.rearrange("p a b -> p (a b)"), scalar=ss.ap[:, 2:3], in1=ng.ap[:, g * 256:(g + 1) * 256],
                                                                  op0=ALU.mult, op1=ALU.mult),
                     reads=[(t1.b, None), (ss.b, 2), (ng.b, None)], writes=[(yn.b, None)])
                for i in range(2):
                    P.op("pe", lambda e, i=i: e.transpose(out=self.bankb(0, 128, c0=i * 128), in_=yn.ap[:, i * 128:(i + 1) * 128], identity=self.idb.ap),
                         reads=[(yn.b, None), (self.idb.b, None)], writes=[(PB[0], None)])
                P.op("act", lambda e, g=g: e.copy(out=ynT.ap[:, 2 * g:2 * g + 2, :], in_=self.bankb(0, 256).rearrange("p (a b) -> p a b", a=2)),
                     reads=[(PB[0], None)], writes=[(ynT.b, g)])
            for half in range(2):
                for c in range(16):
                    P.op("pe", lambda e, c=c, half=half: e.matmul(out=self.bank(1 + half), lhsT=ynT.ap[:, c, :], rhs=wout.ap[:, c, half * 512:(half + 1) * 512],
                                                                  start=(c == 0), stop=(c == 15)),
                         reads=[(ynT.b, None), (wout.b, None)], writes=[(PB[1 + half], None)])
            self.epilogue(1, xr.ap, (xr.b, None), lng, lnb, xout, t * 128, tmp)


    def ssd_phase2(self, li, j, xin, xout):
        P, d = self.P, self.dram
        self.phase()
        PB = self.PB
        NT = int(os.environ.get('K_NT', L // 128))
        NG = int(os.environ.get('K_NG', 8))
        NB = 5
        xTall = self.alloc("xTall", [8, 3 + L], BF16)
        cw = self.alloc("scw", [32, 4], F32)
        cb = self.alloc("scb", [32], F32)
        small = self.bcast_load("ssmall", d["ssm_small"][j:j + 1, :], 96)
        A_b = self.alloc("A_b", [32], F32)
        P.dma(cw.ap, d["ssm_cw"][j], writes=[(cw.b, None)])
        P.dma(cb.ap, d["ssm_cb"][j], writes=[(cb.b, None)])
        dtb, alog, dsk = small.ap[:, 0:32], small.ap[:, 32:64], small.ap[:, 64:96]
        Mle = self.masks.ap[:, 0, :]
        P.op("act", lambda e: e.activation(out=A_b.ap, in_=alog, func=AF.Exp), reads=[(small.b, None)], writes=[(A_b.b, None)])
        P.op("dve", lambda e: e.tensor_scalar(out=A_b.ap, in0=A_b.ap, scalar1=-1.0, scalar2=None, op0=ALU.mult), reads=[(A_b.b, None)], writes=[(A_b.b, None)])
        mark = self.off
        xld = [self.alloc("xld%d" % i, [D], F32) for i in range(2)]
        xbf = [self.alloc("xbf%d" % i, [D], BF16) for i in range(2)]
        P.op("pool", lambda e: e.memset(xTall.ap[:, :, 0:3], 0.0), writes=[(xTall.b, "pad")])
        for t in range(NT):
            self.make_xT_block(xin, t * 128, xld[t % 2].ap, xld[t % 2].b, None, xbf[t % 2], xTall, t, 3 + t * 128, (0, 5)[t % 2])
        P.barrier()
        self.off = mark
        wst = self.alloc("wst", [8, 772], F32)
        wg = [self.alloc("wg%d" % i, [8, 772], BF16) for i in range(2)]
        ngg = [self.alloc("ngg%d" % i, [256], F32) for i in range(2)]
        H = self.alloc("H", [4, 64], F32)
        Hb0 = self.alloc("Hb0", [256], BF16)
        Hb1 = self.alloc("Hb1", [256], BF16)
        acc = [self.alloc("acc%d" % i, [384], F32) for i in range(4)]
        XS0 = [self.alloc("XS0_%d" % i, [384], F32) for i in range(2)]
        XS1 = [self.alloc("XS1_%d" % i, [384], F32) for i in range(2)]
        BTs = [self.alloc("BTs_%d" % i, [384], BF16) for i in range(2)]
        CTs = [self.alloc("CTs_%d" % i, [384], BF16) for i in range(2)]

        def U(name, shape, dt):
            return [self.alloc("%s_%d" % (name, i), shape, dt) for i in range(NB)]
        dttu, dteu, dtu, dtAu, dtwu = U("dtt", [4], F32), U("dte", [4], F32), U("dt_", [4], F32), U("dtA", [4], F32), U("dtw", [4], F32)
        exu, dtw2u = U("ex", [4, 4], F32), U("dtw2", [2, 4], F32)
        Lgu, Eu, CBmu, Mu = U("Lg", [4, 128], F32), U("E", [4, 128], F32), U("CBm", [128], F32), U("M", [4, 128], BF16)
        xstu, Btoku, zsu = U("xs_tok", [4, 64], F32), U("Btok", [128], BF16), U("zs", [256], F32)
        xdtu, xwu = U("xdt", [4, 64], BF16), U("xw", [2, 4, 64], BF16)
        t1u, t2u, hcdu = U("t1", [4, 64], F32), U("t2", [4, 64], F32), U("hcd", [4, 64], F32)
        ssu, ynu, ynTu = U("ss", [4], F32), U("yn", [256], BF16), U("ynTu", [2, 128], BF16)
        wsrc = d["ssm_w_in"][j].rearrange("(c p) n -> p c n", p=128)
        ynTd = d["ynT"].rearrange("(c p) t -> p c t", p=128)

        def load_group(g):
            segs = [(2048 + g * 256, 256, 0), (4096 + g * 128, 128, 256), (5120 + g * 128, 128, 384), (g * 256, 256, 512), (6144 + 4 * g, 4, 768)]
            for (c0, w, o) in segs:
                P.dma(wst.ap[:, :, o:o + w], wsrc[:, :, c0:c0 + w], writes=[(wst.b, o)])
            wgt = wg[g % 2]
            P.op("pool", lambda e, wgt=wgt: e.tensor_copy(out=wgt.ap[:, 0:4, :], in_=wst.ap[:, 0:4, :]), reads=[(wst.b, None)], writes=[(wgt.b, 0)])
            P.op("act", lambda e, wgt=wgt: e.copy(out=wgt.ap[:, 4:8, :], in_=wst.ap[:, 4:8, :]), reads=[(wst.b, None)], writes=[(wgt.b, 1)])
            P.dma(ngg[g % 2].ap, d["ssm_norm_g"][j:j + 1, g * 256:(g + 1) * 256].partition_broadcast(128), writes=[(ngg[g % 2].b, None)])

        load_group(0)
        ccb = [0]
        for g in range(NG):
            if g + 1 < NG:
                load_group(g + 1)
            W = wg[g % 2]
            ng = ngg[g % 2]
            hs = slice(4 * g, 4 * g + 4)
            P.op("pool", lambda e: e.memset(H.ap, 0.0), writes=[(H.b, None)])
            P.op("pool", lambda e: e.memset(Hb0.ap, 0.0), writes=[(Hb0.b, None)])
            Hf = H.ap.rearrange("p a b -> p (a b)")
            def unit(t, sidx):
                def SS(k):
                    P.active = (k == sidx)
                u = t % NB
                xc = slice(t * 128, t * 128 + 131)
                xk = slice(3 + t * 128, 3 + (t + 1) * 128)
                dtt, dte, dt_, dtA, dtw, ex, dtw2 = dttu[u], dteu[u], dtu[u], dtAu[u], dtwu[u], exu[u], dtw2u[u]
                Lg, E, CBm, M = Lgu[u], Eu[u], CBmu[u], Mu[u]
                xs_tok, Btok, zs, xdt, xw = xstu[u], Btoku[u], zsu[u], xdtu[u], xwu[u]
                t1, t2, hcd, ss, yn, ynT = t1u[u], t2u[u], hcdu[u], ssu[u], ynu[u], ynTu[u]
                s3 = (t // 3) % 2
                o3 = (t % 3) * 128
                btT, ctT = BTs[s3], CTs[s3]
                btv, ctv = btT.ap[:, o3:o3 + 128], ctT.ap[:, o3:o3 + 128]
                SS(0)
                for k in range(8):
                    P.op("pe", lambda e, k=k, xk=xk, W=W: e.matmul(out=self.bank(0, 4, c0=256), lhsT=xTall.ap[:, k, xk], rhs=W.ap[:, k, 768:772], start=(k == 0), stop=(k == 7)),
                         reads=[(xTall.b, None), (W.b, None)], writes=[(PB[0], None)])
                P.op("dve", lambda e, dtt=dtt, hs=hs: e.tensor_tensor(out=dtt.ap, in0=self.bank(0, 4, c0=256), in1=dtb[:, hs], op=ALU.add), reads=[(PB[0], None), (small.b, None)], writes=[(dtt.b, None)])
                P.op("act", lambda e, dtt=dtt, dte=dte: e.activation(out=dte.ap, in_=dtt.ap, func=AF.Exp), reads=[(dtt.b, None)], writes=[(dte.b, None)])
                P.op("act", lambda e, dte=dte, dt_=dt_: e.activation(out=dt_.ap, in_=dte.ap, func=AF.Ln, bias=1.0), reads=[(dte.b, None)], writes=[(dt_.b, None)])
                P.op("dve", lambda e, dt_=dt_, dtA=dtA, hs=hs: e.tensor_tensor(out=dtA.ap, in0=dt_.ap, in1=A_b.ap[:, hs], op=ALU.mult), reads=[(dt_.b, None), (A_b.b, None)], writes=[(dtA.b, None)])
                SS(1)
                for m in range(4):
                    P.op("pe", lambda e, m=m, dtA=dtA: e.matmul(out=self.bank(0, 4, c0=320 + m * 4), lhsT=self.masks.ap[:, m, :], rhs=dtA.ap, start=True, stop=True),
                         reads=[(self.masks.b, None), (dtA.b, None)], writes=[(PB[0], None)])
                P.op("act", lambda e, ex=ex: e.activation(out=ex.ap, in_=self.bank(0, 16, c0=320).rearrange("p (a b) -> p a b", a=4), func=AF.Exp), reads=[(PB[0], None)], writes=[(ex.b, None)])
                P.op("dve", lambda e, dtw=dtw, dt_=dt_, ex=ex: e.tensor_tensor(out=dtw.ap, in0=dt_.ap, in1=ex.ap[:, 1, :], op=ALU.mult), reads=[(dt_.b, None), (ex.b, None)], writes=[(dtw.b, None)])
                for c_ in range(2):
                    P.op("dve", lambda e, c_=c_, dtw=dtw, dtw2=dtw2: e.tensor_scalar(out=dtw2.ap[:, c_, :], in0=dtw.ap, scalar1=self.masks.ap[:, 2 + c_, 0:1], scalar2=None, op0=ALU.mult),
                         reads=[(dtw.b, None), (self.masks.b, None)], writes=[(dtw2.b, c_)])
                SS(0)
                if t % 3 == 0:
                    Wd = min(3, NT - t) * 128
                    xcs = slice(t * 128, t * 128 + Wd + 3)
                    chunks = [(0, 2 * g, XS0[s3]), (128, 2 * g + 1, XS1[s3]), (256, 16 + g, btT), (384, 24 + g, ctT)]
                    for (col, jc, dst) in chunks:
                        bk = (1, 2)[ccb[0] % 2]
                        a = acc[ccb[0] % 4]
                        ccb[0] += 1
                        for k in range(8):
                            P.op("pe", lambda e, k=k, col=col, bk=bk, xcs=xcs, Wd=Wd, W=W: e.matmul(out=self.bank(bk, Wd + 3), lhsT=W.ap[:, k, col:col + 128], rhs=xTall.ap[:, k, xcs], start=(k == 0), stop=(k == 7)),
                                 reads=[(W.b, None), (xTall.b, None)], writes=[(PB[bk], None)])
                        P.op("act", lambda e, jc=jc, bk=bk, a=a, Wd=Wd: e.activation(out=a.ap[:, 0:Wd], in_=self.bank(bk, Wd, c0=3), func=AF.Identity, scale=cw.ap[:, jc, 3:4], bias=cb.ap[:, jc:jc + 1]),
                             reads=[(PB[bk], None), (cw.b, None), (cb.b, None)], writes=[(a.b, None)])
                        for kk in (2, 1, 0):
                            P.op("dve", lambda e, jc=jc, bk=bk, a=a, kk=kk, Wd=Wd: e.scalar_tensor_tensor(out=a.ap[:, 0:Wd], in0=self.bank(bk, Wd, c0=kk), scalar=cw.ap[:, jc, kk:kk + 1], in1=a.ap[:, 0:Wd], op0=ALU.mult, op1=ALU.add),
                                 reads=[(PB[bk], None), (cw.b, None), (a.b, None)], writes=[(a.b, None)])
                        P.op("act", lambda e, a=a, dst=dst, Wd=Wd: e.activation(out=dst.ap[:, 0:Wd], in_=a.ap[:, 0:Wd], func=AF.Silu), reads=[(a.b, None)], writes=[(dst.b, None)])
                SS(1)
                for i, xf in enumerate((XS0[s3], XS1[s3])):
                    P.op("pe", lambda e, i=i, xf=xf, o3=o3: e.transpose(out=self.bank(5, 128, c0=128 + i * 128), in_=xf.ap[:, o3:o3 + 128], identity=self.idf.ap),
                         reads=[(xf.b, None), (self.idf.b, None)], writes=[(PB[5], None)])
                P.op("act", lambda e, xs_tok=xs_tok: e.copy(out=xs_tok.ap.rearrange("p a b -> p (a b)"), in_=self.bank(5, 256, c0=128)), reads=[(PB[5], None)], writes=[(xs_tok.b, None)])
                P.op("pe", lambda e, btv=btv: e.transpose(out=self.bankb(5, 128, c0=768), in_=btv, identity=self.idb.ap), reads=[(btT.b, None), (self.idb.b, None)], writes=[(PB[5], None)])
                P.op("dve", lambda e, Btok=Btok: e.tensor_copy(out=Btok.ap, in_=self.bankb(5, 128, c0=768)), reads=[(PB[5], None)], writes=[(Btok.b, None)])
                for k in range(8):
                    P.op("pe", lambda e, k=k, xk=xk, W=W: e.matmul(out=self.bank(6, 256), lhsT=xTall.ap[:, k, xk], rhs=W.ap[:, k, 512:768], start=(k == 0), stop=(k == 7)),
                         reads=[(xTall.b, None), (W.b, None)], writes=[(PB[6], None)])
                P.op("act", lambda e, zs=zs: e.activation(out=zs.ap, in_=self.bank(6, 256), func=AF.Silu), reads=[(PB[6], None)], writes=[(zs.b, None)])
                P.op("pool", lambda e, xdt=xdt, xs_tok=xs_tok, dt_=dt_: e.tensor_tensor(out=xdt.ap, in0=xs_tok.ap, in1=dt_.ap.unsqueeze(2).to_broadcast([128, 4, 64]), op=ALU.mult),
                     reads=[(xs_tok.b, None), (dt_.b, None)], writes=[(xdt.b, None)])
                for c_ in range(2):
                    P.op("pool", lambda e, c_=c_, xw=xw, xs_tok=xs_tok, dtw2=dtw2: e.tensor_tensor(out=xw.ap[:, c_], in0=xs_tok.ap, in1=dtw2.ap[:, c_, :].unsqueeze(2).to_broadcast([128, 4, 64]), op=ALU.mult),
                         reads=[(xs_tok.b, None), (dtw2.b, None)], writes=[(xw.b, c_)])
                SS(0)
                P.op("pool", lambda e, Lg=Lg, dtA=dtA: e.tensor_tensor(out=Lg.ap, in0=self.masks.ap[:, 1:2, :].to_broadcast([128, 4, 128]), in1=dtA.ap.unsqueeze(2).to_broadcast([128, 4, 128]), op=ALU.mult),
                     reads=[(self.masks.b, None), (dtA.b, None)], writes=[(Lg.b, None)])
                SS(1)
                for hh in range(4):
                    P.op("pe", lambda e, hh=hh, Lg=Lg: e.matmul(out=self.bank(4, 128, c0=hh * 128), lhsT=Lg.ap[:, hh, :], rhs=Mle, start=True, stop=True),
                         reads=[(Lg.b, None), (self.masks.b, None)], writes=[(PB[4], None)])
                P.op("act", lambda e, E=E: e.activation(out=E.ap.rearrange("p a b -> p (a b)"), in_=self.bank(4, 512), func=AF.Exp), reads=[(PB[4], None)], writes=[(E.b, None)])
                P.op("pe", lambda e, btv=btv, ctv=ctv: e.matmul(out=self.bank(5, 128), lhsT=btv, rhs=ctv, start=True, stop=True), reads=[(btT.b, None), (ctT.b, None)], writes=[(PB[5], None)])
                P.op("dve", lambda e, CBm=CBm: e.tensor_tensor(out=CBm.ap, in0=self.bank(5, 128), in1=Mle, op=ALU.mult), reads=[(PB[5], None), (self.masks.b, None)], writes=[(CBm.b, None)])
                P.op("dve", lambda e, M=M, E=E, CBm=CBm: e.tensor_tensor(out=M.ap, in0=E.ap, in1=CBm.ap.unsqueeze(1).to_broadcast([128, 4, 128]), op=ALU.mult), reads=[(E.b, None), (CBm.b, None)], writes=[(M.b, None)])
                SS(2)
                P.op("pe", lambda e, Btok=Btok, xw=xw: e.matmul(out=self.bank(7, 256, c0=256), lhsT=Btok.ap, rhs=xw.ap[:, 0].rearrange("p a b -> p (a b)"), start=True, stop=True),
                     reads=[(Btok.b, None), (xw.b, None)], writes=[(PB[7], None)])
                P.op("pe", lambda e, Btok=Btok, xw=xw: e.matmul(out=self.bank(3, 256, c0=256), lhsT=Btok.ap, rhs=xw.ap[:, 1].rearrange("p a b -> p (a b)"), start=True, stop=True),
                     reads=[(Btok.b, None), (xw.b, None)], writes=[(PB[3], None)])
                for hh in range(4):
                    P.op("pe", lambda e, hh=hh, ctv=ctv: e.matmul(out=self.bank(7, 64, p0=0, p1=64, c0=hh * 64), lhsT=ctv[:, 0:64], rhs=Hb0.ap[:, hh * 64:(hh + 1) * 64], start=True, stop=True),
                         reads=[(ctT.b, None), (Hb0.b, None)], writes=[(PB[7], None)])
                for hh in range(4):
                    P.op("dve", lambda e, hh=hh, ex=ex: e.scalar_tensor_tensor(out=H.ap[:, hh, :], in0=H.ap[:, hh, :], scalar=ex.ap[:, 2, hh:hh + 1], in1=self.bank(7, 64, c0=256 + hh * 64), op0=ALU.mult, op1=ALU.add),
                         reads=[(H.b, hh), (ex.b, None), (PB[7], None)], writes=[(H.b, hh)])
                P.op("act", lambda e: e.copy(out=Hb1.ap, in_=Hf), reads=[(H.b, None)], writes=[(Hb1.b, None)])
                SS(3)
                for hh in range(4):
                    P.op("pe", lambda e, hh=hh, M=M, xdt=xdt: e.matmul(out=self.bank(6, 64, c0=256 + hh * 64), lhsT=M.ap[:, hh, :], rhs=xdt.ap[:, hh, :], start=True, stop=True),
                         reads=[(M.b, None), (xdt.b, None)], writes=[(PB[6], None)])
                for hh in range(4):
                    P.op("pe", lambda e, hh=hh, ctv=ctv: e.matmul(out=self.bank(7, 64, p0=64, p1=128, c0=hh * 64), lhsT=ctv[:, 64:128], rhs=Hb1.ap[:, hh * 64:(hh + 1) * 64], start=True, stop=True),
                         reads=[(ctT.b, None), (Hb1.b, None)], writes=[(PB[7], None)])
                SS(2)
                for hh in range(4):
                    P.op("dve", lambda e, hh=hh, ex=ex: e.scalar_tensor_tensor(out=H.ap[:, hh, :], in0=H.ap[:, hh, :], scalar=ex.ap[:, 3, hh:hh + 1], in1=self.bank(3, 64, c0=256 + hh * 64), op0=ALU.mult, op1=ALU.add),
                         reads=[(H.b, hh), (ex.b, None), (PB[3], None)], writes=[(H.b, hh)])
                P.op("act", lambda e: e.copy(out=Hb0.ap, in_=Hf), reads=[(H.b, None)], writes=[(Hb0.b, None)])
                SS(3)
                P.op("dve", lambda e, t1=t1, ex=ex: e.tensor_tensor(out=t1.ap, in0=self.bank(7, 256).rearrange("p (a b) -> p a b", a=4), in1=ex.ap[:, 0, :].unsqueeze(2).to_broadcast([128, 4, 64]), op=ALU.mult),
                     reads=[(PB[7], None), (ex.b, None)], writes=[(t1.b, None)])
                P.op("dve", lambda e, t1=t1: e.tensor_tensor(out=t1.ap.rearrange("p a b -> p (a b)"), in0=self.bank(6, 256, c0=256), in1=t1.ap.rearrange("p a b -> p (a b)"), op=ALU.add),
                     reads=[(t1.b, None), (PB[6], None)], writes=[(t1.b, None)])
                P.op("pool", lambda e, t2=t2, xs_tok=xs_tok, hs=hs: e.tensor_tensor(out=t2.ap, in0=xs_tok.ap, in1=dsk[:, hs].unsqueeze(2).to_broadcast([128, 4, 64]), op=ALU.mult),
                     reads=[(xs_tok.b, None), (small.b, None)], writes=[(t2.b, None)])
                P.op("pool", lambda e, t1=t1, t2=t2: e.tensor_tensor(out=t1.ap, in0=t1.ap, in1=t2.ap, op=ALU.add), reads=[(t1.b, None), (t2.b, None)], writes=[(t1.b, None)])
                P.op("pool", lambda e, t1=t1, zs=zs: e.tensor_tensor(out=t1.ap.rearrange("p a b -> p (a b)"), in0=t1.ap.rearrange("p a b -> p (a b)"), in1=zs.ap, op=ALU.mult), reads=[(t1.b, None), (zs.b, None)], writes=[(t1.b, None)])
                SS(4)
                P.op("act", lambda e, t1=t1, t2=t2, ss=ss: e.activation(out=t2.ap.rearrange("p a b -> p (a b)"), in_=t1.ap.rearrange("p a b -> p (a b)"), func=AF.Square, accum_out=ss.ap[:, 0:1]),
                     reads=[(t1.b, None)], writes=[(t2.b, None), (ss.b, 0)])
                P.op("pool", lambda e, ss=ss: e.tensor_scalar(out=ss.ap[:, 1:2], in0=ss.ap[:, 0:1], scalar1=1.0 / 256.0, scalar2=RMS_EPS, op0=ALU.mult, op1=ALU.add), reads=[(ss.b, 0)], writes=[(ss.b, 1)])
                P.op("pool", lambda e, ss=ss: e.tensor_tensor(out=ss.ap[:, 2:3], in0=ss.ap[:, 1:2], in1=self.mhalf.ap, op=ALU.pow), reads=[(ss.b, 1), (self.mhalf.b, None)], writes=[(ss.b, 2)])
                P.op("dve", lambda e, yn=yn, t1=t1, ss=ss, ng=ng: e.scalar_tensor_tensor(out=yn.ap, in0=t1.ap.rearrange("p a b -> p (a b)"), scalar=ss.ap[:, 2:3], in1=ng.ap, op0=ALU.mult, op1=ALU.mult),
                     reads=[(t1.b, None), (ss.b, 2), (ng.b, None)], writes=[(yn.b, None)])
                SS(5)
                for i in range(2):
                    P.op("pe", lambda e, i=i, yn=yn: e.transpose(out=self.bankb(0, 128, c0=i * 128), in_=yn.ap[:, i * 128:(i + 1) * 128], identity=self.idb.ap),
                         reads=[(yn.b, None), (self.idb.b, None)], writes=[(PB[0], None)])
                P.op("act", lambda e, ynT=ynT: e.copy(out=ynT.ap, in_=self.bankb(0, 256).rearrange("p (a b) -> p a b", a=2)), reads=[(PB[0], None)], writes=[(ynT.b, None)])
                P.dma(ynTd[:, 2 * g:2 * g + 2, t * 128:(t + 1) * 128], ynT.ap, reads=[(ynT.b, None)], writes=[(self.ynT_dep, (g, t))])
                P.active = True
            for i in range(NT + 5):
                for sidx in (5, 4, 3, 2, 1, 0):
                    t = i - sidx
                    if 0 <= t < NT:
                        unit(t, sidx)
        stop_at(11)
        P.barrier()
        self.off = mark
        wout = self.alloc("wout", [16, D], BF16)
        lng = self.bcast_load("lng", d["ln_g"][li * 2:li * 2 + 1, :], D)
        lnb = self.bcast_load("lnb", d["ln_b"][li * 2:li * 2 + 1, :], D)
        stg = [self.alloc("stg%d" % i, [2048], F32) for i in range(3)]
        self.load_w(wout, d["ssm_w_out"][j].rearrange("(c p) n -> p c n", p=128), 16, D, stg)
        yt = [self.alloc("yt%d" % i, [16, 512], BF16) for i in range(2)]
        xres = [self.alloc("xresc%d" % i, [D], F32) for i in range(3)]
        tmps = [self.ln_tmp("s2a", inplace=True), self.ln_tmp("s2b", inplace=True)]
        nb = 0
        ntt = max(1, NT // 4)

        def yload(tt):
            P.dma(yt[tt % 2].ap, ynTd[:, :, tt * 512:(tt + 1) * 512], reads=[(self.ynT_dep, None)], writes=[(yt[tt % 2].b, None)])

        def xload(i):
            P.dma(xres[i % 3].ap, xin[i * 128:(i + 1) * 128, :], writes=[(xres[i % 3].b, None)])
        yload(0)
        xload(0)
        for tt in range(ntt):
            y_ = yt[tt % 2]
            if tt + 1 < ntt:
                yload(tt + 1)
            for blk in range(4):
                tok0 = tt * 512 + blk * 128
                if tt * 4 + blk + 1 < ntt * 4:
                    xload(tt * 4 + blk + 1)
                xr = xres[(tt * 4 + blk) % 3]
                nb += 1
                yb = 1 + 2 * (nb % 2)
                for half in range(2):
                    for c in range(16):
                        P.op("pe", lambda e, c=c, half=half, blk=blk, y_=y_, yb=yb: e.matmul(out=self.bank(yb + half), lhsT=y_.ap[:, c, blk * 128:(blk + 1) * 128], rhs=wout.ap[:, c, half * 512:(half + 1) * 512],
                                                                                             start=(c == 0), stop=(c == 15)),
                             reads=[(y_.b, None), (wout.b, None)], writes=[(PB[yb + half], None)])
                self.epilogue(yb, xr.ap, (xr.b, None), lng, lnb, xout, tok0, tmps[nb % 2])

    def sg_phase(self, li, xin, xout):
        P, d = self.P, self.dram
        self.phase()
        PB = self.PB
        win = self.alloc("gwin", [8, 4096], BF16)
        wout = self.alloc("gwout", [16, D], BF16)
        wsT = self.alloc("wsT", [8, 128], BF16)
        bsp = self.alloc("bsp", [8], F32)
        bin_b = self.bcast_load("bin_b", d["sg_b_in"][0:1, :], 4096)
        vg = self.bcast_load("vg", d["sg_ln_g"][0:1, :], 2048)
        vb = self.bcast_load("vb", d["sg_ln_b"][0:1, :], 2048)
        lng = self.bcast_load("lng", d["ln_g"][li * 2:li * 2 + 1, :], D)
        lnb = self.bcast_load("lnb", d["ln_b"][li * 2:li * 2 + 1, :], D)
        mark = self.off
        stg = [self.alloc("stg%d" % i, [2048], F32) for i in range(3)]
        wsf = self.alloc("wsf", [8, 128], F32)
        P.dma(bsp.ap, d["sg_bs"][:, :], writes=[(bsp.b, None)])
        P.dma(wsf.ap, d["sg_w_s"][0].rearrange("g t s -> t g s"), writes=[(wsf.b, None)])
        for g in range(8):
            P.op("pe", lambda e, g=g: e.transpose(out=self.bank(3 + g // 4, 128, c0=(g % 4) * 128), in_=wsf.ap[:, g, :], identity=self.idf.ap),
                 reads=[(wsf.b, None), (self.idf.b, None)], writes=[(PB[3 + g // 4], None)])
        for g in range(8):
            P.op("dve", lambda e, g=g: e.tensor_tensor(out=wsT.ap[:, g, :], in0=self.bank(3 + g // 4, 128, c0=(g % 4) * 128), in1=self.masks.ap[:, 4, :], op=ALU.mult),
                 reads=[(PB[3 + g // 4], None), (self.masks.b, None)], writes=[(wsT.b, g)])
        self.load_w(win, d["sg_w_in"][0].rearrange("(c p) n -> p c n", p=128), 8, 4096, stg)
        self.load_w(wout, d["sg_w_out"][0].rearrange("(c p) n -> p c n", p=128), 16, D, stg)
        P.barrier()
        self.off = mark
        xres = [self.alloc("xres%d" % i, [D], F32) for i in range(2)]
        xbf = self.alloc("xbf", [D], BF16)
        xT = [self.alloc("xT%d" % i, [8, 128], BF16) for i in range(2)]
        u_tok = self.alloc("u_tok", [2048], F32)
        v_tok = self.alloc("v_tok", [2048], F32)
        vo = self.alloc("vo", [2048], F32)
        hA = [self.alloc("hA%d" % i, [512], F32) for i in range(2)]
        hB = [self.alloc("hB%d" % i, [512], F32) for i in range(2)]
        vln = self.alloc("vln", [2048], BF16)
        uv = self.alloc("uv", [2048], BF16)
        uvT = self.alloc("uvT", [16, 128], BF16)
        vst = self.alloc("vst", [24], F32)
        vmv = self.alloc("vmv", [2], F32)
        vsc = self.alloc("vsc", [4], F32)
        tmp = self.ln_tmp("g", inplace=True)
        GC = 2.0 * math.sqrt(2.0 / math.pi)
        for t in range(int(os.environ.get('K_NT', L // 128))):
            cur = xT[t % 2]
            xr = xres[t % 2]
            if t == 0:
                self.make_xT_block(xin, 0, xr.ap, xr.b, None, xbf, cur, None, 0, 0)
            for q in range(8):
                bk = 1 + q % 2
                a, b_ = hA[q % 2], hB[q % 2]
                for k in range(8):
                    P.op("pe", lambda e, k=k, q=q, bk=bk, cur=cur: e.matmul(out=self.bank(bk), lhsT=cur.ap[:, k, :], rhs=win.ap[:, k, q * 512:(q + 1) * 512],
                                                                           start=(k == 0), stop=(k == 7)),
                         reads=[(cur.b, None), (win.b, None)], writes=[(PB[bk], None)])
                P.op("dve", lambda e, q=q, bk=bk, a=a: e.tensor_tensor(out=a.ap, in0=self.bank(bk), in1=bin_b.ap[:, q * 512:(q + 1) * 512], op=ALU.add),
                     reads=[(PB[bk], None), (bin_b.b, None)], writes=[(a.b, None)])
                P.op("act", lambda e, a=a, b_=b_: e.activation(out=b_.ap, in_=a.ap, func=AF.Square), reads=[(a.b, None)], writes=[(b_.b, None)])
                P.op("dve", lambda e, a=a, b_=b_: e.scalar_tensor_tensor(out=b_.ap, in0=b_.ap, scalar=1.0 / 0.044715, in1=a.ap, op0=ALU.add, op1=ALU.mult),
                     reads=[(a.b, None), (b_.b, None)], writes=[(b_.b, None)])
                P.op("act", lambda e, b_=b_: e.activation(out=b_.ap, in_=b_.ap, func=AF.Sigmoid, scale=GC * 0.044715), reads=[(b_.b, None)], writes=[(b_.b, None)])
                dst = u_tok if q < 4 else v_tok
                dv = dst.ap[:, (q % 4) * 512:(q % 4 + 1) * 512]
                P.op("pool", lambda e, a=a, b_=b_, dv=dv: e.tensor_tensor(out=dv, in0=a.ap, in1=b_.ap, op=ALU.mult),
                     reads=[(a.b, None), (b_.b, None)], writes=[(dst.b, q % 4)])
            if t + 1 < int(os.environ.get('K_NT', L // 128)):
                self.make_xT_block(xin, (t + 1) * 128, xres[(t + 1) % 2].ap, xres[(t + 1) % 2].b, None, xbf, xT[(t + 1) % 2], None, 0, 0)
            self.layernorm(v_tok, 2048, vg, vb, vo, vst, vmv, vsc, LN_EPS, out_ap=vln.ap, out_buf=vln)
            for g in range(8):
                bk = 3 + g // 2
                P.op("pe", lambda e, g=g, bk=bk: e.matmul(out=self.bank(bk, 256, c0=(g % 2) * 256), lhsT=wsT.ap[:, g, :], rhs=vln.ap[:, g * 256:(g + 1) * 256], start=True, stop=True),
                     reads=[(wsT.b, None), (vln.b, None)], writes=[(PB[bk], None)])
            for g in range(8):
                bk = 3 + g // 2
                P.op("dve", lambda e, g=g, bk=bk: e.scalar_tensor_tensor(out=uv.ap[:, g * 256:(g + 1) * 256], in0=self.bank(bk, 256, c0=(g % 2) * 256), scalar=bsp.ap[:, g:g + 1],
                                                                          in1=u_tok.ap[:, g * 256:(g + 1) * 256], op0=ALU.add, op1=ALU.mult),
                     reads=[(PB[bk], None), (bsp.b, None), (u_tok.b, None)], writes=[(uv.b, g)])
            for hf in range(2):
                for c in range(8):
                    P.op("pe", lambda e, c=c, hf=hf: e.transpose(out=self.bankb(7, 128, c0=c * 128), in_=uv.ap[:, (hf * 8 + c) * 128:(hf * 8 + c + 1) * 128], identity=self.idb.ap),
                         reads=[(uv.b, None), (self.idb.b, None)], writes=[(PB[7], None)])
                P.op("act", lambda e, hf=hf: e.copy(out=uvT.ap[:, hf * 8:(hf + 1) * 8, :], in_=self.bankb(7, 1024).rearrange("p (c t) -> p c t", c=8)),
                     reads=[(PB[7], None)], writes=[(uvT.b, hf)])
            for half in range(2):
                for c in range(16):
                    P.op("pe", lambda e, c=c, half=half: e.matmul(out=self.bank(1 + half), lhsT=uvT.ap[:, c, :], rhs=wout.ap[:, c, half * 512:(half + 1) * 512],
                                                                  start=(c == 0), stop=(c == 15)),
                         reads=[(uvT.b, None), (wout.b, None)], writes=[(PB[1 + half], None)])
            self.epilogue(1, xr.ap, (xr.b, None), lng, lnb, xout, t * 128, tmp)


    def mla_phase(self, li, xin, xout):
        P, d = self.P, self.dram
        self.phase()
        PB = self.PB
        SCALE = 96.0 ** -0.5
        TWO_PI = 2.0 * math.pi
        C1 = 6.28125
        C2 = TWO_PI - C1
        PI_S = 3.1415925
        qT = self.alloc("qT", [3, L], BF16)
        kvT = self.alloc("kvT", [2, L], BF16)
        krT = self.alloc("krT", [L], BF16)
        cosT = self.alloc("cosT", [L], F32)
        sinT = self.alloc("sinT", [L], F32)
        wq = self.alloc("wq", [3, 16, 128], BF16)
        wqs = self.alloc("wqs", [3, 16, 64], BF16)
        wkn = self.alloc("wkn", [2, 16, 128], BF16)
        wv = self.alloc("wv", [2, 16, 64], BF16)
        wout = self.alloc("mwout", [8, D], BF16)
        lng = self.bcast_load("lng", d["ln_g"][li * 2:li * 2 + 1, :], D)
        lnb = self.bcast_load("lnb", d["ln_b"][li * 2:li * 2 + 1, :], D)
        mark = self.off
        win = self.alloc("mwin", [8, 640], BF16)
        wkr = self.alloc("wkr", [8, 64], BF16)
        wkrs = self.alloc("wkrs", [8, 64], BF16)
        gq = self.bcast_load("gq", d["mla_q_norm_g"][0:1, :], 384)
        gkv = self.bcast_load("gkv", d["mla_kv_norm_g"][0:1, :], 256)
        rc = self.alloc("rc", [2], F32)
        P.dma(rc.ap, d["consts"][:, 768:770], writes=[(rc.b, None)])
        mark2 = self.off
        stq = self.alloc("stq", [3, 1536], F32)
        stkv = self.alloc("stkv", [2, 2048], F32)
        stin = self.alloc("stin", [8, 672], F32)
        P.dma(stq.ap, d["mla_w_q_b"][0].rearrange("(c p) n -> p c n", p=128), writes=[(stq.b, None)])
        P.dma(stkv.ap, d["mla_w_kv_b"][0].rearrange("(c p) n -> p c n", p=128), writes=[(stkv.b, None)])
        P.dma(stin.ap, d["mla_w_in"][0].rearrange("(c p) n -> p c n", p=128), writes=[(stin.b, None)])
        for tz in (wq, wqs, wkn, wkr, wkrs):
            P.op("pool", lambda e, tz=tz: e.memset(tz.ap, 0.0), writes=[(tz.b, None)])
        for c in range(3):
            src = stq.ap[:, c, :].rearrange("p (h d) -> p h d", h=16)
            P.op("dve", lambda e, c=c, src=src: e.tensor_copy(out=wq.ap[:, c, :, 64:128], in_=src[:, :, 0:64]), reads=[(stq.b, None)], writes=[(wq.b, (c, 0))])
            P.op("act", lambda e, c=c, src=src: e.copy(out=wq.ap[:, c, :, 32:64], in_=src[:, :, 64:96]), reads=[(stq.b, None)], writes=[(wq.b, (c, 1))])
            P.op("dve", lambda e, c=c, src=src: e.tensor_copy(out=wqs.ap[:, c, :, 32:48], in_=src[:, :, 80:96]), reads=[(stq.b, None)], writes=[(wqs.b, (c, 0))])
            P.op("act", lambda e, c=c, src=src: e.copy(out=wqs.ap[:, c, :, 48:64], in_=src[:, :, 64:80]), reads=[(stq.b, None)], writes=[(wqs.b, (c, 1))])
        for c in range(2):
            src = stkv.ap[:, c, :].rearrange("p (h d) -> p h d", h=16)
            P.op("dve", lambda e, c=c, src=src: e.tensor_copy(out=wkn.ap[:, c, :, 64:128], in_=src[:, :, 0:64]), reads=[(stkv.b, None)], writes=[(wkn.b, (c, 0))])
            P.op("act", lambda e, c=c, src=src: e.copy(out=wv.ap[:, c, :, :], in_=src[:, :, 64:128]), reads=[(stkv.b, None)], writes=[(wv.b, c)])
        P.op("dve", lambda e: e.tensor_copy(out=win.ap, in_=stin.ap[:, :, 0:640]), reads=[(stin.b, None)], writes=[(win.b, None)])
        P.op("act", lambda e: e.copy(out=wkr.ap[:, :, 32:64], in_=stin.ap[:, :, 640:672]), reads=[(stin.b, None)], writes=[(wkr.b, 1)])
        P.op("dve", lambda e: e.tensor_copy(out=wkrs.ap[:, :, 32:48], in_=stin.ap[:, :, 656:672]), reads=[(stin.b, None)], writes=[(wkrs.b, 1)])
        P.op("act", lambda e: e.copy(out=wkrs.ap[:, :, 48:64], in_=stin.ap[:, :, 640:656]), reads=[(stin.b, None)], writes=[(wkrs.b, 2)])
        P.barrier()
        self.off = mark2
        stg = [self.alloc("stg%d" % i, [2048], F32) for i in range(3)]
        self.load_w(wout, d["mla_w_out"][0].rearrange("(c p) n -> p c n", p=128), 8, D, stg)
        P.barrier()
        self.off = mark2
        stop_at(20)
        posi = self.alloc("posi", [1024], I32)
        ki = self.alloc("ki", [1024], I32)
        pf = self.alloc("pf", [1024], F32)
        ang = self.alloc("ang", [1024], F32)
        kf = self.alloc("kf", [1024], F32)
        rr = self.alloc("rr", [1024], F32)
        r2 = self.alloc("r2", [1024], F32)
        mm = self.alloc("mm", [1024], F32)
        R = slice(0, 64)
        for ch in range(4):
            cs = slice(ch * 1024, (ch + 1) * 1024)
            P.dma(posi.ap[R], d["positions"][0:1, cs].partition_broadcast(64), writes=[(posi.b, None)])
            P.op("dve", lambda e: e.tensor_copy(out=pf.ap[R], in_=posi.ap[R]), reads=[(posi.b, None)], writes=[(pf.b, None)])
            P.op("dve", lambda e: e.tensor_scalar(out=ang.ap[R], in0=pf.ap[R], scalar1=rc.ap[R, 0:1], scalar2=None, op0=ALU.mult), reads=[(pf.b, None), (rc.b, None)], writes=[(ang.b, None)])
            P.op("dve", lambda e: e.tensor_scalar(out=pf.ap[R], in0=ang.ap[R], scalar1=1.0 / TWO_PI, scalar2=None, op0=ALU.mult), reads=[(ang.b, None)], writes=[(pf.b, None)])
            P.op("dve", lambda e: e.tensor_copy(out=ki.ap[R], in_=pf.ap[R]), reads=[(pf.b, None)], writes=[(ki.b, None)])
            P.op("dve", lambda e: e.tensor_copy(out=kf.ap[R], in_=ki.ap[R]), reads=[(ki.b, None)], writes=[(kf.b, None)])
            P.op("dve", lambda e: e.scalar_tensor_tensor(out=rr.ap[R], in0=kf.ap[R], scalar=-C1, in1=ang.ap[R], op0=ALU.mult, op1=ALU.add), reads=[(kf.b, None), (ang.b, None)], writes=[(rr.b, None)])
            P.op("dve", lambda e: e.scalar_tensor_tensor(out=rr.ap[R], in0=kf.ap[R], scalar=-C2, in1=rr.ap[R], op0=ALU.mult, op1=ALU.add), reads=[(kf.b, None), (rr.b, None)], writes=[(rr.b, None)])
            P.op("dve", lambda e: e.tensor_scalar(out=r2.ap[R], in0=rr.ap[R], scalar1=math.pi / 2, scalar2=None, op0=ALU.add), reads=[(rr.b, None)], writes=[(r2.b, None)])
            P.op("dve", lambda e: e.tensor_scalar(out=mm.ap[R], in0=r2.ap[R], scalar1=math.pi, scalar2=TWO_PI, op0=ALU.is_gt, op1=ALU.mult), reads=[(r2.b, None)], writes=[(mm.b, None)])
            P.op("dve", lambda e: e.tensor_tensor(out=r2.ap[R], in0=r2.ap[R], in1=mm.ap[R], op=ALU.subtract), reads=[(r2.b, None), (mm.b, None)], writes=[(r2.b, None)])
            P.op("dve", lambda e: e.tensor_scalar(out=rr.ap[R], in0=rr.ap[R], scalar1=PI_S, scalar2=-PI_S, op0=ALU.min, op1=ALU.max), reads=[(rr.b, None)], writes=[(rr.b, None)])
            P.op("dve", lambda e: e.tensor_scalar(out=r2.ap[R], in0=r2.ap[R], scalar1=PI_S, scalar2=-PI_S, op0=ALU.min, op1=ALU.max), reads=[(r2.b, None)], writes=[(r2.b, None)])
            P.op("act", lambda e: e.activation(out=mm.ap[R], in_=rr.ap[R], func=AF.Sin), reads=[(rr.b, None)], writes=[(mm.b, None)])
            P.op("dve", lambda e, cs=cs: e.tensor_scalar(out=sinT.ap[R, cs], in0=mm.ap[R], scalar1=rc.ap[R, 1:2], scalar2=None, op0=ALU.mult), reads=[(mm.b, None), (rc.b, None)], writes=[(sinT.b, ch)])
            P.op("act", lambda e, cs=cs: e.activation(out=cosT.ap[R, cs], in_=r2.ap[R], func=AF.Sin), reads=[(r2.b, None)], writes=[(cosT.b, ch)])
        stop_at(201)
        xres = [self.alloc("xres%d" % i, [D], F32) for i in range(2)]
        xbf = self.alloc("xbf", [D], BF16)
        xT = [self.alloc("xT%d" % i, [8, 128], BF16) for i in range(2)]
        junk = self.alloc("junk", [384], F32)
        ss = self.alloc("ss", [8], F32)
        qn = self.alloc("qn", [384], BF16)
        kvn = self.alloc("kvn", [256], BF16)
        rt1 = self.alloc("rt1", [512], F32)
        rt2 = self.alloc("rt2", [512], F32)
        NT = int(os.environ.get('K_NT', L // 128))
        for t in range(NT):
            cur = xT[t % 2]
            xr = xres[t % 2]
            ts_ = slice(t * 128, (t + 1) * 128)
            self.make_xT_block(xin, t * 128, xr.ap, xr.b, None, xbf, cur, None, 0, 0)
            for k in range(8):
                P.op("pe", lambda e, k=k, cur=cur: e.matmul(out=self.bank(1, 384), lhsT=cur.ap[:, k, :], rhs=win.ap[:, k, 0:384], start=(k == 0), stop=(k == 7)),
                     reads=[(cur.b, None), (win.b, None)], writes=[(PB[1], None)])
            for k in range(8):
                P.op("pe", lambda e, k=k, cur=cur: e.matmul(out=self.bank(2, 256), lhsT=cur.ap[:, k, :], rhs=win.ap[:, k, 384:640], start=(k == 0), stop=(k == 7)),
                     reads=[(cur.b, None), (win.b, None)], writes=[(PB[2], None)])
            P.op("act", lambda e: e.activation(out=junk.ap, in_=self.bank(1, 384), func=AF.Square, accum_out=ss.ap[:, 0:1]), reads=[(PB[1], None)], writes=[(junk.b, None), (ss.b, 0)])
            P.op("act", lambda e: e.activation(out=junk.ap[:, 0:256], in_=self.bank(2, 256), func=AF.Square, accum_out=ss.ap[:, 1:2]), reads=[(PB[2], None)], writes=[(junk.b, None), (ss.b, 1)])
            stop_at(211)
            P.op("pool", lambda e: e.tensor_scalar(out=ss.ap[:, 2:3], in0=ss.ap[:, 0:1], scalar1=1.0 / 384.0, scalar2=RMS_EPS, op0=ALU.mult, op1=ALU.add), reads=[(ss.b, 0)], writes=[(ss.b, 2)])
            P.op("pool", lambda e: e.tensor_scalar(out=ss.ap[:, 3:4], in0=ss.ap[:, 1:2], scalar1=1.0 / 256.0, scalar2=RMS_EPS, op0=ALU.mult, op1=ALU.add), reads=[(ss.b, 1)], writes=[(ss.b, 3)])
            for i_ in range(2):
                P.op("pool", lambda e, i_=i_: e.tensor_tensor(out=ss.ap[:, 4 + i_:5 + i_], in0=ss.ap[:, 2 + i_:3 + i_], in1=self.mhalf.ap, op=ALU.pow),
                     reads=[(ss.b, 2 + i_), (self.mhalf.b, None)], writes=[(ss.b, 4 + i_)])
            stop_at(212)
            P.op("dve", lambda e: e.scalar_tensor_tensor(out=qn.ap, in0=self.bank(1, 384), scalar=ss.ap[:, 4:5], in1=gq.ap, op0=ALU.mult, op1=ALU.mult),
                 reads=[(PB[1], None), (ss.b, 4), (gq.b, None)], writes=[(qn.b, None)])
            stop_at(2121)
            P.op("dve", lambda e: e.scalar_tensor_tensor(out=kvn.ap, in0=self.bank(2, 256), scalar=ss.ap[:, 5:6], in1=gkv.ap, op0=ALU.mult, op1=ALU.mult),
                 reads=[(PB[2], None), (ss.b, 5), (gkv.b, None)], writes=[(kvn.b, None)])
            stop_at(2122)
            for c in range(3):
                P.op("pe", lambda e, c=c: e.transpose(out=self.bankb(3, 128, c0=c * 128), in_=qn.ap[:, c * 128:(c + 1) * 128], identity=self.idb.ap),
                     reads=[(qn.b, None), (self.idb.b, None)], writes=[(PB[3], None)])
            for c in range(2):
                P.op("pe", lambda e, c=c: e.transpose(out=self.bankb(3, 128, c0=(3 + c) * 128), in_=kvn.ap[:, c * 128:(c + 1) * 128], identity=self.idb.ap),
                     reads=[(kvn.b, None), (self.idb.b, None)], writes=[(PB[3], None)])
            stop_at(2123)
            P.op("act", lambda e, ts_=ts_: e.copy(out=qT.ap[:, :, ts_], in_=self.bankb(3, 384).rearrange("p (c t) -> p c t", c=3)), reads=[(PB[3], None)], writes=[(qT.b, t)])
            stop_at(2124)
            P.op("act", lambda e, ts_=ts_: e.copy(out=kvT.ap[:, :, ts_], in_=self.bankb(3, 256, c0=384).rearrange("p (c t) -> p c t", c=2)), reads=[(PB[3], None)], writes=[(kvT.b, t)])
            stop_at(213)
            for k in range(8):
                P.op("pe", lambda e, k=k, cur=cur: e.matmul(out=self.bank(4, 128, p0=0, p1=64), lhsT=wkr.ap[:, k, :], rhs=cur.ap[:, k, :], start=(k == 0), stop=(k == 7)),
                     reads=[(cur.b, None), (wkr.b, None)], writes=[(PB[4], None)])
            for k in range(8):
                P.op("pe", lambda e, k=k, cur=cur: e.matmul(out=self.bank(4, 128, p0=0, p1=64, c0=128), lhsT=wkrs.ap[:, k, :], rhs=cur.ap[:, k, :], start=(k == 0), stop=(k == 7)),
                     reads=[(cur.b, None), (wkrs.b, None)], writes=[(PB[4], None)])
            stop_at(214)
            P.op("dve", lambda e, ts_=ts_: e.tensor_tensor(out=rt1.ap[32:64, 0:128], in0=self.bank(4, 128, p0=32, p1=64), in1=cosT.ap[32:64, ts_], op=ALU.mult),
                 reads=[(PB[4], None), (cosT.b, None)], writes=[(rt1.b, None)])
            P.op("dve", lambda e, ts_=ts_: e.tensor_tensor(out=rt2.ap[32:64, 0:128], in0=self.bank(4, 128, p0=32, p1=64, c0=128), in1=sinT.ap[32:64, ts_], op=ALU.mult),
                 reads=[(PB[4], None), (sinT.b, None)], writes=[(rt2.b, None)])
            P.op("pool", lambda e, ts_=ts_: e.tensor_tensor(out=krT.ap[32:64, ts_], in0=rt1.ap[32:64, 0:128], in1=rt2.ap[32:64, 0:128], op=ALU.add),
                 reads=[(rt1.b, None), (rt2.b, None)], writes=[(krT.b, t)])
        stop_at(21)
        P.barrier()
        self.off = mark
        QT = [self.alloc("QT%d" % i, [L], BF16) for i in range(2)]
        KT = [self.alloc("KT%d" % i, [L], BF16) for i in range(2)]
        V = [self.alloc("V%d" % i, [32, 65], BF16) for i in range(2)]
        PT = [self.alloc("PT%d" % i, [512], BF16) for i in range(3)]
        oTh = [self.alloc("oTh%d" % i, [L], BF16) for i in range(2)]
        rq1 = self.alloc("rt1b", [512], F32)
        rq2 = self.alloc("rt2b", [512], F32)
        on = [self.alloc("on%d" % i, [64], BF16) for i in range(2)]
        rden = self.alloc("rden", [4], F32)
        for i in range(2):
            P.op("pool", lambda e, i=i: e.memset(QT[i].ap[0:32, :], 0.0), writes=[(QT[i].b, "z")])
            P.op("pool", lambda e, i=i: e.memset(KT[i].ap[0:32, :], 0.0), writes=[(KT[i].b, "z")])
            P.op("pool", lambda e, i=i: e.memset(V[i].ap[:, :, 64:65], 1.0), writes=[(V[i].b, "one")])
        NQT = max(1, NT // 4)
        cnt = 0
        for h in range(int(os.environ.get('K_NH', 16))):
            qt_, kt_, v_, oh = QT[h % 2], KT[h % 2], V[h % 2], oTh[h % 2]
            P.op("pool", lambda e, kt_=kt_: e.tensor_copy(out=kt_.ap[32:64, 0:NT * 128], in_=krT.ap[32:64, 0:NT * 128]), reads=[(krT.b, None)], writes=[(kt_.b, "r")])
            for tt in range(NQT):
                cs = slice(tt * 512, (tt + 1) * 512)
                for c in range(2):
                    P.op("pe", lambda e, c=c, h=h, cs=cs: e.matmul(out=self.bank(1), lhsT=wkn.ap[:, c, h, :], rhs=kvT.ap[:, c, cs], start=(c == 0), stop=(c == 1)),
                         reads=[(wkn.b, None), (kvT.b, None)], writes=[(PB[1], None)])
                P.op("act", lambda e, kt_=kt_, cs=cs: e.copy(out=kt_.ap[64:128, cs], in_=self.bank(1, 512, p0=64, p1=128)), reads=[(PB[1], None)], writes=[(kt_.b, ("n", tt))])
            for tt in range(NQT):
                cs = slice(tt * 512, (tt + 1) * 512)
                for c in range(3):
                    P.op("pe", lambda e, c=c, h=h, cs=cs: e.matmul(out=self.bank(0), lhsT=wq.ap[:, c, h, :], rhs=qT.ap[:, c, cs], start=(c == 0), stop=(c == 2)),
                         reads=[(wq.b, None), (qT.b, None)], writes=[(PB[0], None)])
                for c in range(3):
                    P.op("pe", lambda e, c=c, h=h, cs=cs: e.matmul(out=self.bank(1, 512, p0=0, p1=64), lhsT=wqs.ap[:, c, h, :], rhs=qT.ap[:, c, cs], start=(c == 0), stop=(c == 2)),
                         reads=[(wqs.b, None), (qT.b, None)], writes=[(PB[1], None)])
                P.op("act", lambda e, qt_=qt_, cs=cs: e.copy(out=qt_.ap[64:128, cs], in_=self.bank(0, 512, p0=64, p1=128)), reads=[(PB[0], None)], writes=[(qt_.b, ("n", tt))])
                P.op("dve", lambda e, cs=cs: e.tensor_tensor(out=rq1.ap[32:64, :], in0=self.bank(0, 512, p0=32, p1=64), in1=cosT.ap[32:64, cs], op=ALU.mult),
                     reads=[(PB[0], None), (cosT.b, None)], writes=[(rq1.b, None)])
                P.op("dve", lambda e, cs=cs: e.tensor_tensor(out=rq2.ap[32:64, :], in0=self.bank(1, 512, p0=32, p1=64), in1=sinT.ap[32:64, cs], op=ALU.mult),
                     reads=[(PB[1], None), (sinT.b, None)], writes=[(rq2.b, None)])
                P.op("pool", lambda e, qt_=qt_, cs=cs: e.tensor_tensor(out=qt_.ap[32:64, cs], in0=rq1.ap[32:64, :], in1=rq2.ap[32:64, :], op=ALU.add),
                     reads=[(rq1.b, None), (rq2.b, None)], writes=[(qt_.b, ("r", tt))])
            for kg in range((NQT * 4 + 7) // 8):
                for kb in range(kg * 8, min(kg * 8 + 8, NQT * 4)):
                    for c in range(2):
                        P.op("pe", lambda e, c=c, h=h, kb=kb: e.matmul(out=self.bank(1, 64, c0=(kb % 8) * 64), lhsT=kvT.ap[:, c, kb * 128:(kb + 1) * 128], rhs=wv.ap[:, c, h, :],
                                                                       start=(c == 0), stop=(c == 1)),
                             reads=[(wv.b, None), (kvT.b, None)], writes=[(PB[1], None)])
                nk = min(8, NQT * 4 - kg * 8)
                P.op("dve", lambda e, v_=v_, kg=kg, nk=nk: e.tensor_copy(out=v_.ap[:, kg * 8:kg * 8 + nk, 0:64], in_=self.bank(1, nk * 64).rearrange("p (a b) -> p a b", a=nk)),
                     reads=[(PB[1], None)], writes=[(v_.b, ("v", kg))])
            steps = [(qt, kb) for qt in range(NQT) for kb in range(4 * qt + 4)]
            SB = (2, 3, 1)

            def geom(i):
                qt, kb = steps[i]
                diag = kb >= 4 * qt
                j = kb - 4 * qt if diag else 0
                return qt, kb, diag, j, qt * 512 + 128 * j, 512 - 128 * j, SB[i % 3], PT[i % 3]

            def emit_score(i):
                qt, kb, diag, j, q0, N, sb, pt = geom(i)
                P.op("pe", lambda e, kb=kb, q0=q0, N=N, sb=sb, kt_=kt_, qt_=qt_: e.matmul(out=self.bank(sb, N), lhsT=kt_.ap[:, kb * 128:(kb + 1) * 128], rhs=qt_.ap[:, q0:q0 + N], start=True, stop=True),
                     reads=[(kt_.b, None), (qt_.b, None)], writes=[(PB[sb], None)])
                P.op("act", lambda e, pt=pt, N=N, sb=sb: e.activation(out=pt.ap[:, 0:N], in_=self.bank(sb, N), func=AF.Exp, scale=SCALE),
                     reads=[(PB[sb], None)], writes=[(pt.b, None)])
                if diag:
                    P.op("pool", lambda e, pt=pt: e.memset(pt.ap[64:128, 0:64], 0.0), reads=[(pt.b, None)], writes=[(pt.b, None)])

            def emit_pv(i):
                qt, kb, diag, j, q0, N, sb, pt = geom(i)
                for qb in range(j, 4):
                    P.op("pe", lambda e, pt=pt, kb=kb, qb=qb, j=j, qt=qt, v_=v_: e.matmul(out=self.bank(4 + qb, 65), lhsT=pt.ap[:, (qb - j) * 128:(qb - j + 1) * 128], rhs=v_.ap[:, kb, :],
                                                                                  start=(kb == 0), stop=(kb == 4 * qt + qb)),
                         reads=[(pt.b, None), (v_.b, None)], writes=[(PB[4 + qb], None)])
                if kb == 4 * qt + 3:
                    for qb in range(4):
                        o_ = on[qb % 2]
                        P.op("dve", lambda e, qb=qb: e.reciprocal(out=rden.ap[:, qb:qb + 1], in_=self.bank(4 + qb, 1, c0=64)), reads=[(PB[4 + qb], None)], writes=[(rden.b, qb)])
                        P.op("dve", lambda e, qb=qb, o_=o_: e.tensor_scalar(out=o_.ap, in0=self.bank(4 + qb, 64), scalar1=rden.ap[:, qb:qb + 1], scalar2=None, op0=ALU.mult),
                             reads=[(PB[4 + qb], None), (rden.b, qb)], writes=[(o_.b, None)])
                        P.op("pe", lambda e, qb=qb, o_=o_: e.transpose(out=self.bankb(0, 128, p0=0, p1=64, c0=qb * 128), in_=o_.ap, identity=self.idb.ap),
                             reads=[(o_.b, None), (self.idb.b, None)], writes=[(PB[0], None)])
                    P.op("act", lambda e, qt=qt, oh=oh: e.copy(out=oh.ap[0:64, qt * 512:(qt + 1) * 512], in_=self.bankb(0, 512, p0=0, p1=64)), reads=[(PB[0], None)], writes=[(oh.b, qt)])

            LA = 2
            for i in range(min(LA, len(steps))):
                emit_score(i)
            for i in range(len(steps)):
                if i + LA < len(steps):
                    emit_score(i + LA)
                emit_pv(i)
            P.dma(d["oT"][h * 64:(h + 1) * 64, 0:NT * 128], oh.ap[0:64, 0:NT * 128], reads=[(oh.b, None)], writes=[(self.oT_dep, h)])
        stop_at(22)
        P.barrier()
        self.off = mark
        oTt = [self.alloc("oTt%d" % i, [8, 512], BF16) for i in range(2)]
        xres = [self.alloc("xresc%d" % i, [D], F32) for i in range(3)]
        tmps = [self.ln_tmp("ma", inplace=True), self.ln_tmp("mb", inplace=True)]
        oTv = d["oT"].rearrange("(c p) t -> p c t", p=128)
        nb = 0

        def oload(tt):
            P.dma(oTt[tt % 2].ap, oTv[:, :, tt * 512:(tt + 1) * 512], reads=[(self.oT_dep, None)], writes=[(oTt[tt % 2].b, None)])

        def xload(i):
            P.dma(xres[i % 3].ap, xin[i * 128:(i + 1) * 128, :], writes=[(xres[i % 3].b, None)])
        oload(0)
        xload(0)
        for tt in range(NQT):
            ot = oTt[tt % 2]
            if tt + 1 < NQT:
                oload(tt + 1)
            for blk in range(4):
                tok0 = tt * 512 + blk * 128
                if tt * 4 + blk + 1 < NQT * 4:
                    xload(tt * 4 + blk + 1)
                xr = xres[(tt * 4 + blk) % 3]
                nb += 1
                yb = 1 + 2 * (nb % 2)
                for half in range(2):
                    for c in range(8):
                        P.op("pe", lambda e, c=c, half=half, blk=blk, ot=ot, yb=yb: e.matmul(out=self.bank(yb + half), lhsT=ot.ap[:, c, blk * 128:(blk + 1) * 128], rhs=wout.ap[:, c, half * 512:(half + 1) * 512],
                                                                                             start=(c == 0), stop=(c == 7)),
                             reads=[(ot.b, None), (wout.b, None)], writes=[(PB[yb + half], None)])
                self.epilogue(yb, xr.ap, (xr.b, None), lng, lnb, xout, tok0, tmps[nb % 2])


def make_consts():
    c = np.zeros((128, 1024), np.float32)
    c[:, 0:128] = np.eye(128, dtype=np.float32)
    t = np.arange(128)
    same = (t[:, None] // 64) == (t[None, :] // 64)
    c[:, 128:256] = (same & (t[:, None] <= t[None, :])).astype(np.float32)
    c[:, 256:384] = (same & (t[:, None] > t[None, :])).astype(np.float32)
    c[:, 384:512] = (t[:, None] < 64).astype(np.float32) * np.ones((1, 128), np.float32)
    c[:, 512:640] = (t[:, None] >= 64).astype(np.float32) * np.ones((1, 128), np.float32)
    c[:, 640:768] = (t[:, None] <= t[None, :]).astype(np.float32)
    i = t % 32
    c[:, 768] = (10000.0 ** (-((i % 16).astype(np.float32) * 2.0 / 32.0))).astype(np.float32)
    c[:, 769] = np.where(i < 16, -1.0, 1.0)
    return c


def build(sublayers, x_from_input=True):
    nc = bass.Bass("TRN2", target_bir_lowering=False)
    dram = {}

    def din(name, shape, dt=F32):
        dram[name] = nc.dram_tensor(name, list(shape), dt, kind="ExternalInput").ap()

    din("x", [L, D])
    din("consts", [128, 1024])
    din("ffn_w_in", [DEPTH, D, 2 * FH])
    din("ffn_w_out", [DEPTH, FH, D])
    din("ffn_cw", [DEPTH, 128, 44, 3])
    din("ffn_cb", [DEPTH, 128, 44])
    din("ssm_w_in", [2, D, 6176])
    din("ssm_w_out", [2, 2048, D])
    din("ssm_cw", [2, 128, 32, 4])
    din("ssm_cb", [2, 128, 32])
    din("ssm_small", [2, 96])
    din("ssm_norm_g", [2, 2048])
    din("sg_w_in", [1, D, 4096])
    din("sg_w_out", [1, 2048, D])
    din("sg_b_in", [1, 4096])
    din("sg_ln_g", [1, 2048])
    din("sg_ln_b", [1, 2048])
    din("sg_w_s", [1, 8, 128, 128])
    din("sg_bs", [128, 8])
    din("positions", [1, L], I32)
    din("mla_w_in", [1, D, 672])
    din("mla_w_q_b", [1, 384, 1536])
    din("mla_w_kv_b", [1, 256, 2048])
    din("mla_w_out", [1, D, D])
    din("mla_q_norm_g", [1, 384])
    din("mla_kv_norm_g", [1, 256])
    din("ln_g", [DEPTH * 2, D])
    din("ln_b", [DEPTH * 2, D])
    out = nc.dram_tensor("out", [L, D], F32, kind="ExternalOutput").ap()
    xa = nc.dram_tensor("xa", [L, D], F32, kind="Internal").ap()
    xb = nc.dram_tensor("xb", [L, D], F32, kind="Internal").ap()
    dram["oT"] = nc.dram_tensor("oT", [D, L], BF16, kind="Internal").ap()
    dram["ynT"] = nc.dram_tensor("ynT", [2048, L], BF16, kind="Internal").ap()
    P = Prog(nc)
    with ExitStack() as es:
        arena = es.enter_context(nc.sbuf_tensor("arena", [128, ARENA_ELEMS], BF16))
        ps = es.enter_context(nc.psum_tensor("ps", [128, 4096], F32))
        sems = {e: es.enter_context(nc.semaphore("s_" + e)) for e in ENGS}
        dsems = [es.enter_context(nc.semaphore("d%d" % i)) for i in range(NDSEM)]
        block = es.enter_context(nc.Block())
        kb = KB(nc, P, arena, ps, dram)
        kb.xbuf_dep = {id(xa): Buf("xa"), id(xb): Buf("xb"), id(out): Buf("out"), id(dram["x"]): Buf("xin")}
        kb.oT_dep = Buf("oT")
        kb.ynT_dep = Buf("ynT")
        kb.setup_consts()
        cur = dram["x"]
        n = len(sublayers)
        for si, (kind, li) in enumerate(sublayers):
            dst = out if si == n - 1 else (xa if cur is not xa else xb)
            try:
                if kind == "ffn":
                    kb.ffn_phase(li, cur, dst)
                elif li % 3 == 0:
                    (kb.ssd_phase if os.environ.get("K_OLDSSD") else kb.ssd_phase2)(li, li // 3, cur, dst)
                elif li % 3 == 1:
                    kb.sg_phase(li, cur, dst)
                elif li % 3 == 2:
                    kb.mla_phase(li, cur, dst)
                else:
                    raise NotImplementedError((kind, li))
            except StopBuild:
                break
            cur = dst
        P.barrier()
        P.finalize(block, sems, dsems)
    kb_stats = {e: len(P.ins[e]) for e in ENGS}
    print("instr counts", kb_stats, "dmas", P.ndma)
    return nc


def prep_shared(inputs):
    f = lambda a: np.ascontiguousarray(a, dtype=np.float32)
    sh = {"consts": make_consts()}
    sh["ffn_w_in"] = f(inputs["ffn_w_in"])
    sh["ffn_w_out"] = f(inputs["ffn_w_out"])
    sh["ffn_cw"] = f(np.asarray(inputs["ffn_conv_w"]).reshape(DEPTH, 3, 44, 128).transpose(0, 3, 2, 1))
    sh["ffn_cb"] = f(np.asarray(inputs["ffn_conv_b"]).reshape(DEPTH, 44, 128).transpose(0, 2, 1))
    sh["ssm_w_in"] = f(inputs["ssm_w_in"])
    sh["ssm_w_out"] = f(inputs["ssm_w_out"])
    sh["ssm_cw"] = f(np.asarray(inputs["ssm_conv_w"]).reshape(2, 4, 32, 128).transpose(0, 3, 2, 1))
    sh["ssm_cb"] = f(np.asarray(inputs["ssm_conv_b"]).reshape(2, 32, 128).transpose(0, 2, 1))
    sh["ssm_small"] = f(np.concatenate([np.asarray(inputs["ssm_dt_bias"]), np.asarray(inputs["ssm_a_log"]), np.asarray(inputs["ssm_d"])], axis=1))
    sh["ssm_norm_g"] = f(inputs["ssm_norm_g"])
    sh["sg_w_in"] = f(inputs["sg_w_in"])
    sh["sg_w_out"] = f(inputs["sg_w_out"])
    sh["sg_b_in"] = f(inputs["sg_b_in"])
    sh["sg_ln_g"] = f(inputs["sg_ln_g"])
    sh["sg_ln_b"] = f(inputs["sg_ln_b"])
    sh["sg_w_s"] = f(inputs["sg_w_s"])
    sh["sg_bs"] = f(np.asarray(inputs["sg_b_s"])[0].T)
    for k_ in ("mla_w_in", "mla_w_q_b", "mla_w_kv_b", "mla_w_out", "mla_q_norm_g", "mla_kv_norm_g"):
        sh[k_] = f(inputs[k_])
    sh["ln_g"] = f(np.asarray(inputs["ln_g"]).reshape(DEPTH * 2, D))
    sh["ln_b"] = f(np.asarray(inputs["ln_b"]).reshape(DEPTH * 2, D))
    return sh


def per_core(inputs, c):
    return {"positions": np.ascontiguousarray(np.asarray(inputs["positions"])[c:c + 1].astype(np.int32))}


ALL_SUBLAYERS = [(k, i) for i in range(DEPTH) for k in ("mix", "ffn")]


def kernel(**inputs):
    x = np.asarray(inputs["x"], dtype=np.float32)
    nc = build(ALL_SUBLAYERS)
    sh = prep_shared(inputs)
    in_maps = [dict(sh, x=np.ascontiguousarray(x[c]), **per_core(inputs, c)) for c in range(8)]
    res = run_bass_kernel_spmd(nc, in_maps, core_ids=list(range(8)))
    return np.stack([res.results[c]["out"] for c in range(8)], axis=0).astype(np.float32)
```

```python
import math, os
from contextlib import ExitStack
import numpy as np
import concourse.bass as bass
import concourse.mybir as mybir
from concourse.bass_utils import run_bass_kernel_spmd

F32 = mybir.dt.float32
BF16 = mybir.dt.bfloat16
I32 = mybir.dt.int32
ALU = mybir.AluOpType
AF = mybir.ActivationFunctionType

D = 1024
L = 4096
DEPTH = 4
ALPHA = (2.0 * DEPTH) ** 0.25
LN_EPS = 1e-5
RMS_EPS = 1e-6
FH = 2816
ENGS = ("pe", "act", "dve", "pool", "sp")
NDSEM = 24
ARENA_ELEMS = 106400


class Buf:
    def __init__(self, name):
        self.name = name
        self.st = {}

    def state(self, key):
        s = self.st.get(key)
        if s is None:
            s = {"w": None, "r": {}}
            self.st[key] = s
        return s


class Prog:
    def __init__(self, nc):
        self.nc = nc
        self.ins = {e: [] for e in ENGS}
        self.clock = {e: {f: 0 for f in ENGS} for e in ENGS}
        self.seen_dma = {e: set() for e in ENGS}
        self.ndma = 0
        self.dma_clock = {}
        self.active = True

    def _deps_for(self, reads, writes):
        deps = []
        for (b, k) in reads:
            keys = (k, None) if k is not None else tuple(b.st.keys()) + (None,)
            for kk in set(keys):
                s = b.st.get(kk)
                if s is not None and s["w"] is not None:
                    deps.append((s["w"], "raw"))
        for (b, k) in writes:
            keys = (k, None) if k is not None else tuple(b.st.keys()) + (None,)
            for kk in set(keys):
                s = b.st.get(kk)
                if s is None:
                    continue
                if s["w"] is not None:
                    deps.append((s["w"], "waw"))
                for t in s["r"].values():
                    deps.append((t, "war"))
        return deps

    def _record(self, tok, reads, writes):
        for (b, k) in reads:
            s = b.state(k)
            s["r"][tok[1] if tok[0] == "c" else tok] = tok
        for (b, k) in writes:
            if k is None:
                b.st = {}
            s = b.state(k)
            s["w"] = tok
            s["r"] = {}

    def _filter(self, eng, deps):
        waits = []
        clk = self.clock[eng]
        for (t, kind) in deps:
            if t[0] == "c":
                _, f, idx = t
                if f == eng and eng in ("pe", "sp"):
                    continue
                if clk[f] >= idx:
                    continue
                waits.append(t)
                oc = self.ins[f][idx - 1]["clock"]
                for g in ENGS:
                    if oc[g] > clk[g]:
                        clk[g] = oc[g]
                if clk[f] < idx:
                    clk[f] = idx
            else:
                if t in self.seen_dma[eng]:
                    continue
                self.seen_dma[eng].add(t)
                waits.append(t)
                oc = self.dma_clock[t]
                for g in ENGS:
                    if oc[g] > clk[g]:
                        clk[g] = oc[g]
        return waits

    def op(self, eng, fn, reads=(), writes=()):
        if not self.active:
            return None
        waits = self._filter(eng, self._deps_for(reads, writes))
        idx = len(self.ins[eng]) + 1
        self.ins[eng].append({"fn": fn, "waits": waits, "dma": None, "sig": False, "clock": dict(self.clock[eng])})
        tok = ("c", eng, idx)
        self._record(tok, reads, writes)
        return tok

    def dma(self, out_ap, in_ap, reads=(), writes=(), q="sp", **kw):
        if not self.active:
            return None
        waits = self._filter(q, self._deps_for(reads, writes))
        n = self.ndma
        self.ndma += 1
        tok = ("d", n)
        self.ins[q].append({"fn": (lambda e: e.dma_start(out=out_ap, in_=in_ap, **kw)), "waits": waits, "dma": n,
                            "sig": False, "clock": dict(self.clock[q])})
        self.dma_clock[tok] = dict(self.clock[q])
        self._record(tok, reads, writes)
        return tok

    def barrier(self):
        last = {e: max([i + 1 for i, r in enumerate(self.ins[e]) if r["fn"] is not None and r["dma"] is None] or [0]) for e in ENGS}
        nd = self.ndma
        for e in ENGS:
            deps = [(("c", f, last[f]), "raw") for f in ENGS if (f != e or e not in ("pe", "sp")) and f != "sp" and last[f] > 0]
            deps += [(("d", n), "raw") for n in range(max(0, nd - NDSEM), nd)]
            waits = self._filter(e, deps)
            if waits:
                self.ins[e].append({"fn": None, "waits": waits, "dma": None, "sig": False, "clock": dict(self.clock[e])})

    def finalize(self, block_ctx, sems, dsems):
        for e in ENGS:
            for rec in self.ins[e]:
                for t in rec["waits"]:
                    if t[0] == "c":
                        self.ins[t[1]][t[2] - 1]["sig"] = True
        sigval = {}
        for e in ENGS:
            c = 0
            for i, rec in enumerate(self.ins[e]):
                if rec["sig"]:
                    c += 1
                    sigval[(e, i + 1)] = c
        ndma = self.ndma

        def play(eng_name):
            def body(e):
                for rec in self.ins[eng_name]:
                    for t in rec["waits"]:
                        if t[0] == "c":
                            e.wait_ge(sems[t[1]], sigval[(t[1], t[2])])
                        else:
                            n = t[1]
                            e.wait_ge(dsems[n % NDSEM], 16 * (n // NDSEM + 1))
                    if rec["fn"] is None:
                        continue
                    if rec["dma"] is not None:
                        n = rec["dma"]
                        if n >= NDSEM:
                            e.wait_ge(dsems[n % NDSEM], 16 * (n // NDSEM))
                        rec["fn"](e).then_inc(dsems[n % NDSEM], 16)
                    else:
                        ins = rec["fn"](e)
                        if rec["sig"]:
                            ins.then_inc(sems[eng_name], 1)
                if eng_name == "sp":
                    for j in range(min(NDSEM, ndma)):
                        e.wait_ge(dsems[j], 16 * ((ndma - 1 - j) // NDSEM + 1))
            return body

        block_ctx.tensor(play("pe"))
        block_ctx.scalar(play("act"))
        block_ctx.vector(play("dve"))
        block_ctx.gpsimd(play("pool"))
        block_ctx.sync(play("sp"))


class StopBuild(Exception):
    pass


def stop_at(n):
    if int(os.environ.get('K_STOP', 99)) == n:
        raise StopBuild()


class T:
    def __init__(self, ap, name):
        self.ap = ap
        self.b = Buf(name)

    def __getitem__(self, k):
        return self.ap[k]


class KB:
    def __init__(self, nc, P, arena, ps, dram):
        self.nc, self.P, self.arena, self.ps, self.dram = nc, P, arena, ps, dram
        self.psb = ps[:].bitcast(BF16)
        self.PB = [Buf("psum%d" % i) for i in range(8)]
        self.off = 0
        self.base = 0
        self.cast_rr = 0

    def alloc(self, name, shape, dt, parts=128):
        size = {F32: 4, BF16: 2, I32: 4}[dt]
        n = int(np.prod(shape))
        nb = (n * size + 63) // 64 * 64
        e0 = self.off // 2
        self.off += nb
        assert self.off <= ARENA_ELEMS * 2, ("arena overflow", name, self.off)
        v = self.arena[0:parts, e0:e0 + n * size // 2]
        if dt != BF16:
            v = v.bitcast(dt)
        if len(shape) == 2:
            v = v.rearrange("p (a b) -> p a b", a=shape[0])
        elif len(shape) == 3:
            v = v.rearrange("p (a b c) -> p a b c", a=shape[0], b=shape[1])
        return T(v, name)

    def phase(self):
        self.P.barrier()
        self.off = self.base

    def bank(self, b, n=512, p0=0, p1=128, c0=0):
        return self.ps[p0:p1, b * 512 + c0: b * 512 + c0 + n]

    def bankb(self, b, n=1024, p0=0, p1=128, c0=0):
        return self.psb[p0:p1, b * 1024 + c0: b * 1024 + c0 + n]

    def setup_consts(self):
        P = self.P
        c = self.dram["consts"]
        self.idf = self.alloc("idf", [128], F32)
        self.masks = self.alloc("masks", [5, 128], F32)
        self.idb = self.alloc("idb", [128], BF16)
        self.mhalf = self.alloc("mhalf", [1], F32)
        P.dma(self.idf.ap, c[:, 0:128], writes=[(self.idf.b, None)])
        P.dma(self.masks.ap, c[:, 128:768].rearrange("p (a b) -> p a b", a=5), writes=[(self.masks.b, None)])
        P.op("dve", lambda e: e.tensor_copy(out=self.idb.ap, in_=self.idf.ap), reads=[(self.idf.b, None)], writes=[(self.idb.b, None)])
        P.op("pool", lambda e: e.memset(self.mhalf.ap, -0.5), writes=[(self.mhalf.b, None)])
        self.base = self.off

    def load_w(self, dst, src3, C, N, stg):
        P = self.P
        ncol = max(1, 2048 // N) if N <= 2048 else 1
        nsplit = (N + 2047) // 2048
        i = 0
        for c0 in range(0, C, ncol):
            c1 = min(C, c0 + ncol)
            for s in range(nsplit):
                n0 = s * 2048
                n1 = min(N, n0 + 2048)
                st = stg[self.cast_rr % len(stg)]
                w = (c1 - c0) * (n1 - n0)
                sv = st.ap[:, 0:w].rearrange("p (a b) -> p a b", a=c1 - c0)
                P.dma(sv, src3[:, c0:c1, n0:n1], writes=[(st.b, None)])
                eng = ("act", "dve")[self.cast_rr % 2]
                dv = dst.ap[:, c0:c1, n0:n1]
                if eng == "act":
                    P.op(eng, lambda e, dv=dv, sv=sv: e.copy(out=dv, in_=sv), reads=[(st.b, None)], writes=[(dst.b, None)])
                else:
                    P.op(eng, lambda e, dv=dv, sv=sv: e.tensor_copy(out=dv, in_=sv), reads=[(st.b, None)], writes=[(dst.b, None)])
                self.cast_rr += 1

    def bcast_load(self, name, src_row, n):
        t = self.alloc(name, [n], F32)
        self.P.dma(t.ap, src_row.partition_broadcast(128), writes=[(t.b, None)])
        return t

    def make_xT_block(self, xin, tok0, xres_view, xres_buf, xres_key, xbf, xT, xT_key, col0, tbank):
        P = self.P
        P.dma(xres_view, xin[tok0:tok0 + 128, :], writes=[(xres_buf, xres_key)])
        P.op("act", lambda e: e.copy(out=xbf.ap, in_=xres_view), reads=[(xres_buf, xres_key)], writes=[(xbf.b, None)])
        for c in range(8):
            P.op("pe", lambda e, c=c: e.transpose(out=self.bankb(tbank, 128, c0=c * 128), in_=xbf.ap[:, c * 128:(c + 1) * 128], identity=self.idb.ap),
                 reads=[(xbf.b, None), (self.idb.b, None)], writes=[(self.PB[tbank], None)])
        src = self.bankb(tbank, 1024).rearrange("p (c t) -> p c t", c=8)
        P.op("act", lambda e: e.copy(out=xT.ap[:, 0:8, col0:col0 + 128], in_=src), reads=[(self.PB[tbank], None)], writes=[(xT.b, xT_key)])

    def epilogue(self, ybank, xres_view, xres_dep, lng, lnb, xout, tok0, tmp):
        P = self.P
        r, o, st, mv, sc = tmp["r"], tmp["o"], tmp["st"], tmp["mv"], tmp["sc"]
        y = self.ps[:, ybank * 512: ybank * 512 + 1024]
        P.op("dve", lambda e: e.scalar_tensor_tensor(out=r.ap, in0=xres_view, scalar=ALPHA, in1=y, op0=ALU.mult, op1=ALU.add),
             reads=[xres_dep, (self.PB[ybank], None), (self.PB[ybank + 1], None)], writes=[(r.b, None)])
        self.layernorm(r, 1024, lng, lnb, o, st, mv, sc, LN_EPS)
        P.dma(xout[tok0:tok0 + 128, :], o.ap, reads=[(o.b, None)], writes=[(self.xbuf_dep[id(xout)], tok0)])

    def layernorm(self, r, n, lng, lnb, o, st, mv, sc, eps, out_ap=None, out_buf=None):
        P = self.P
        nchunk = n // 512
        for i in range(nchunk):
            P.op("dve", lambda e, i=i: e.bn_stats(out=st.ap[:, i * 6:(i + 1) * 6], in_=r.ap[:, i * 512:(i + 1) * 512]),
                 reads=[(r.b, None)], writes=[(st.b, i)])
        P.op("dve", lambda e: e.bn_aggr(out=mv.ap, in_=st.ap[:, 0:6 * nchunk]), reads=[(st.b, None)], writes=[(mv.b, None)])
        P.op("pool", lambda e: e.tensor_scalar(out=sc.ap[:, 0:1], in0=mv.ap[:, 1:2], scalar1=1.0, scalar2=eps, op0=ALU.mult, op1=ALU.add),
             reads=[(mv.b, None)], writes=[(sc.b, 0)])
        P.op("pool", lambda e: e.tensor_tensor(out=sc.ap[:, 1:2], in0=sc.ap[:, 0:1], in1=self.mhalf.ap, op=ALU.pow),
             reads=[(sc.b, 0), (self.mhalf.b, None)], writes=[(sc.b, 1)])
        P.op("dve", lambda e: e.tensor_scalar(out=sc.ap[:, 2:3], in0=mv.ap[:, 0:1], scalar1=sc.ap[:, 1:2], scalar2=-1.0, op0=ALU.mult, op1=ALU.mult),
             reads=[(mv.b, None), (sc.b, 1)], writes=[(sc.b, 2)])
        P.op("act", lambda e: e.activation(out=r.ap, in_=r.ap, func=AF.Identity, scale=sc.ap[:, 1:2], bias=sc.ap[:, 2:3]),
             reads=[(r.b, None), (sc.b, 1), (sc.b, 2)], writes=[(r.b, None)])
        P.op("pool", lambda e: e.tensor_tensor(out=o.ap, in0=r.ap, in1=lng.ap, op=ALU.mult), reads=[(r.b, None), (lng.b, None)], writes=[(o.b, None)])
        oo = o.ap if out_ap is None else out_ap
        ob = o if out_buf is None else out_buf
        P.op("dve", lambda e: e.tensor_tensor(out=oo, in0=o.ap, in1=lnb.ap, op=ALU.add), reads=[(o.b, None), (lnb.b, None)], writes=[(ob.b, None)])

    def ln_tmp(self, tag, inplace=False):
        r = self.alloc("r" + tag, [1024], F32)
        return {"r": r, "o": r if inplace else self.alloc("o" + tag, [1024], F32),
                "st": self.alloc("st" + tag, [24], F32), "mv": self.alloc("mv" + tag, [2], F32), "sc": self.alloc("sc" + tag, [4], F32)}

    def ffn_phase(self, li, xin, xout):
        P, d = self.P, self.dram
        self.phase()
        TT, PAD = 256, 2
        w1 = self.alloc("w1", [8, 2 * FH], BF16)
        w2 = self.alloc("w2", [22, D], BF16)
        cw = self.alloc("cw", [44, 3], F32)
        cb = self.alloc("cb", [44], F32)
        lng = self.bcast_load("lng", d["ln_g"][li * 2 + 1:li * 2 + 2, :], D)
        lnb = self.bcast_load("lnb", d["ln_b"][li * 2 + 1:li * 2 + 2, :], D)
        mark = self.off
        stg = [self.alloc("stg%d" % i, [2048], F32) for i in range(6)]
        P.dma(cw.ap, d["ffn_cw"][li], writes=[(cw.b, None)])
        P.dma(cb.ap, d["ffn_cb"][li], writes=[(cb.b, None)])
        self.load_w(w1, d["ffn_w_in"][li].rearrange("(c p) n -> p c n", p=128), 8, 2 * FH, stg)
        self.load_w(w2, d["ffn_w_out"][li].rearrange("(c p) n -> p c n", p=128), 22, D, stg)
        P.barrier()
        self.off = mark
        xres = [self.alloc("xres%d" % i, [2, D], F32) for i in range(2)]
        xbf = [self.alloc("xbf%d" % i, [D], BF16) for i in range(2)]
        xT = [self.alloc("xT%d" % i, [8, TT + PAD], BF16) for i in range(2)]
        acc = [self.alloc("acc%d" % i, [TT], F32) for i in range(4)]
        sg = [self.alloc("sg%d" % i, [TT], F32) for i in range(2)]
        pTs = [self.alloc("pT%d" % i, [22, TT], BF16) for i in range(2)]
        tmp = self.ln_tmp("f", inplace=True)
        P.op("pool", lambda e: e.memset(xT[0].ap[:, :, 0:PAD], 0.0), writes=[(xT[0].b, "pad")])
        TB, UP, DB = 0, (1, 2, 3, 4), 5
        NTF = int(os.environ.get('K_NTF', L // TT))
        jjb = [0]

        def load_tile(t):
            cur = xT[t % 2]
            for blk in range(2):
                self.make_xT_block(xin, t * TT + blk * 128, xres[t % 2].ap[:, blk, :], xres[t % 2].b, blk, xbf[blk], cur, blk, PAD + blk * 128, TB)
            if t > 0:
                prv = xT[(t - 1) % 2]
                P.op("pool", lambda e, cur=cur, prv=prv: e.tensor_copy(out=cur.ap[:, :, 0:PAD], in_=prv.ap[:, :, TT:TT + PAD]),
                     reads=[(prv.b, 1)], writes=[(cur.b, "pad")])

        def down_epi(t):
            pT = pTs[t % 2]
            for blk in range(2):
                for half in range(2):
                    for c in range(22):
                        P.op("pe", lambda e, c=c, half=half, blk=blk, pT=pT: e.matmul(out=self.bank(DB + half), lhsT=pT.ap[:, c, blk * 128:(blk + 1) * 128],
                                                                                      rhs=w2.ap[:, c, half * 512:(half + 1) * 512], start=(c == 0), stop=(c == 21)),
                             reads=[(pT.b, c), (w2.b, None)], writes=[(self.PB[DB + half], None)])
                self.epilogue(DB, xres[t % 2].ap[:, blk, :], (xres[t % 2].b, blk), lng, lnb, xout, t * TT + blk * 128, tmp)

        def up_chunk(t, c):
            cur = xT[t % 2]
            pT = pTs[t % 2]
            for which in range(2):
                j = which * 22 + c
                bk = UP[jjb[0] % 4]
                a = acc[jjb[0] % 4]
                jjb[0] += 1
                for k in range(8):
                    P.op("pe", lambda e, k=k, j=j, bk=bk, cur=cur: e.matmul(out=self.bank(bk, TT + PAD), lhsT=w1.ap[:, k, j * 128:(j + 1) * 128],
                                                                           rhs=cur.ap[:, k, :], start=(k == 0), stop=(k == 7)),
                         reads=[(w1.b, None), (cur.b, None)], writes=[(self.PB[bk], None)])
                P.op("act", lambda e, j=j, bk=bk, a=a: e.activation(out=a.ap, in_=self.bank(bk, TT, c0=2), func=AF.Identity,
                                                                     scale=cw.ap[:, j, 2:3], bias=cb.ap[:, j:j + 1]),
                     reads=[(self.PB[bk], None), (cw.b, None), (cb.b, None)], writes=[(a.b, None)])
                for kk in (1, 0):
                    P.op("dve", lambda e, j=j, bk=bk, a=a, kk=kk: e.scalar_tensor_tensor(out=a.ap, in0=self.bank(bk, TT, c0=kk), scalar=cw.ap[:, j, kk:kk + 1],
                                                                                        in1=a.ap, op0=ALU.mult, op1=ALU.add),
                         reads=[(self.PB[bk], None), (cw.b, None), (a.b, None)], writes=[(a.b, None)])
                s_ = sg[c % 2]
                if which == 0:
                    P.op("act", lambda e, a=a, s_=s_: e.activation(out=s_.ap, in_=a.ap, func=AF.Silu), reads=[(a.b, None)], writes=[(s_.b, None)])
                else:
                    P.op("pool", lambda e, a=a, s_=s_, c=c, pT=pT: e.tensor_tensor(out=pT.ap[:, c, :], in0=s_.ap, in1=a.ap, op=ALU.mult),
                         reads=[(a.b, None), (s_.b, None)], writes=[(pT.b, c)])

        load_tile(0)
        for t in range(NTF):
            for c in range(22):
                if c == 6 and t > 0:
                    down_epi(t - 1)
                if c == 14 and t + 1 < NTF:
                    load_tile(t + 1)
                up_chunk(t, c)
        down_epi(NTF - 1)

    def ssd_phase(self, li, j, xin, xout):
        P, d = self.P, self.dram
        self.phase()
        PAD = 3
        PB = self.PB
        win = self.alloc("win", [8, 6176], BF16)
        wout = self.alloc("wout", [16, D], BF16)
        cw = self.alloc("scw", [32, 4], F32)
        cb = self.alloc("scb", [32], F32)
        small = self.bcast_load("ssmall", d["ssm_small"][j:j + 1, :], 96)
        ng = self.bcast_load("sng", d["ssm_norm_g"][j:j + 1, :], 2048)
        lng = self.bcast_load("lng", d["ln_g"][li * 2:li * 2 + 1, :], D)
        lnb = self.bcast_load("lnb", d["ln_b"][li * 2:li * 2 + 1, :], D)
        mark = self.off
        stg = [self.alloc("stg%d" % i, [2048], F32) for i in range(3)]
        P.dma(cw.ap, d["ssm_cw"][j], writes=[(cw.b, None)])
        P.dma(cb.ap, d["ssm_cb"][j], writes=[(cb.b, None)])
        self.load_w(win, d["ssm_w_in"][j].rearrange("(c p) n -> p c n", p=128), 8, 6176, stg)
        self.load_w(wout, d["ssm_w_out"][j].rearrange("(c p) n -> p c n", p=128), 16, D, stg)
        P.barrier()
        self.off = mark
        A_b = self.alloc("A_b", [32], F32)
        H = self.alloc("H", [2048], F32)
        Hb0 = self.alloc("Hb0", [2048], BF16)
        Hb1 = self.alloc("Hb1", [2048], BF16)
        xres = [self.alloc("xres%d" % i, [D], F32) for i in range(2)]
        xbf = self.alloc("xbf", [D], BF16)
        xT = [self.alloc("xT%d" % i, [8, 128 + PAD], BF16) for i in range(2)]
        acc = [self.alloc("acc%d" % i, [128], F32) for i in range(4)]
        xsf = [self.alloc("xsf%d" % i, [128], F32) for i in range(2)]
        BT = [self.alloc("BT%d" % i, [128], BF16) for i in range(2)]
        CT = [self.alloc("CT%d" % i, [128], BF16) for i in range(2)]
        dtt = self.alloc("dtt", [32], F32)
        dte = self.alloc("dte", [32], F32)
        dt_ = self.alloc("dt_", [32], F32)
        dtA = self.alloc("dtA", [32], F32)
        ex = self.alloc("ex", [4, 32], F32)
        dtw = self.alloc("dtw", [32], F32)
        Lg = self.alloc("Lg", [4, 128], F32)
        E = self.alloc("E", [4, 128], F32)
        CBm = self.alloc("CBm", [128], F32)
        M = self.alloc("M", [4, 128], BF16)
        xs_tok = self.alloc("xs_tok", [4, 64], F32)
        Btok = self.alloc("Btok", [128], BF16)
        zs = self.alloc("zs", [256], F32)
        xdt = self.alloc("xdt", [4, 64], BF16)
        xw = self.alloc("xw", [2, 4, 64], BF16)
        dtw2 = self.alloc("dtw2", [2, 32], F32)
        t1 = self.alloc("t1", [4, 64], F32)
        t2 = self.alloc("t2", [4, 64], F32)
        ss = self.alloc("ss", [4], F32)
        hcd = self.alloc("hcd", [4, 64], F32)
        yn = self.alloc("yn", [256], BF16)
        ynT = self.alloc("ynT", [16, 128], BF16)
        tmp = self.ln_tmp("s", inplace=True)
        dtb, alog, dsk = small.ap[:, 0:32], small.ap[:, 32:64], small.ap[:, 64:96]
        Mle, Mgt = self.masks.ap[:, 0, :], self.masks.ap[:, 1, :]
        P.op("act", lambda e: e.activation(out=A_b.ap, in_=alog, func=AF.Exp), reads=[(small.b, None)], writes=[(A_b.b, None)])
        P.op("dve", lambda e: e.tensor_scalar(out=A_b.ap, in0=A_b.ap, scalar1=-1.0, scalar2=None, op0=ALU.mult), reads=[(A_b.b, None)], writes=[(A_b.b, None)])
        P.op("pool", lambda e: e.memset(H.ap, 0.0), writes=[(H.b, None)])
        P.op("pool", lambda e: e.memset(Hb0.ap, 0.0), writes=[(Hb0.b, None)])
        P.op("pool", lambda e: e.memset(xT[0].ap[:, :, 0:PAD], 0.0), writes=[(xT[0].b, "pad")])
        cc = 0
        for t in range(int(os.environ.get('K_NT', L // 128))):
            cur = xT[t % 2]
            xr = xres[t % 2]
            self.make_xT_block(xin, t * 128, xr.ap, xr.b, None, xbf, cur, "blk", PAD, 0)
            if t > 0:
                prv = xT[(t - 1) % 2]
                P.op("pool", lambda e, cur=cur, prv=prv: e.tensor_copy(out=cur.ap[:, :, 0:PAD], in_=prv.ap[:, :, 128:128 + PAD]),
                     reads=[(prv.b, "blk")], writes=[(cur.b, "pad")])
            for k in range(8):
                P.op("pe", lambda e, k=k, cur=cur: e.matmul(out=self.bank(3, 32), lhsT=cur.ap[:, k, PAD:PAD + 128], rhs=win.ap[:, k, 6144:6176],
                                                          start=(k == 0), stop=(k == 7)),
                     reads=[(cur.b, None), (win.b, None)], writes=[(PB[3], None)])
            P.op("dve", lambda e: e.tensor_tensor(out=dtt.ap, in0=self.bank(3, 32), in1=dtb, op=ALU.add), reads=[(PB[3], None), (small.b, None)], writes=[(dtt.b, None)])
            P.op("act", lambda e: e.activation(out=dte.ap, in_=dtt.ap, func=AF.Exp), reads=[(dtt.b, None)], writes=[(dte.b, None)])
            P.op("act", lambda e: e.activation(out=dt_.ap, in_=dte.ap, func=AF.Ln, bias=1.0), reads=[(dte.b, None)], writes=[(dt_.b, None)])
            P.op("dve", lambda e: e.tensor_tensor(out=dtA.ap, in0=dt_.ap, in1=A_b.ap, op=ALU.mult), reads=[(dt_.b, None), (A_b.b, None)], writes=[(dtA.b, None)])
            for m in range(4):
                P.op("pe", lambda e, m=m: e.matmul(out=self.bank(3, 32, c0=64 + m * 32), lhsT=self.masks.ap[:, m, :], rhs=dtA.ap, start=True, stop=True),
                     reads=[(self.masks.b, None), (dtA.b, None)], writes=[(PB[3], None)])
            P.op("act", lambda e: e.activation(out=ex.ap, in_=self.bank(3, 128, c0=64).rearrange("p (a b) -> p a b", a=4), func=AF.Exp),
                 reads=[(PB[3], None)], writes=[(ex.b, None)])
            P.op("dve", lambda e: e.tensor_tensor(out=dtw.ap, in0=dt_.ap, in1=ex.ap[:, 1, :], op=ALU.mult), reads=[(dt_.b, None), (ex.b, None)], writes=[(dtw.b, None)])
            stop_at(1)
            for c_ in range(2):
                P.op("dve", lambda e, c_=c_: e.tensor_scalar(out=dtw2.ap[:, c_, :], in0=dtw.ap, scalar1=self.masks.ap[:, 2 + c_, 0:1], scalar2=None, op0=ALU.mult),
                     reads=[(dtw.b, None), (self.masks.b, None)], writes=[(dtw2.b, c_)])
            for g in range(8):
                hs = slice(4 * g, 4 * g + 4)
                bt, ct = BT[g % 2], CT[g % 2]
                chunks = [(2048 + (2 * g) * 128, xsf[0]), (2048 + (2 * g + 1) * 128, xsf[1]), (4096 + g * 128, bt), (5120 + g * 128, ct)]
                for (col, dst) in chunks:
                    jc = (col - 2048) // 128
                    bk = (1, 2)[cc % 2]
                    co = 0
                    a = acc[cc % 4]
                    cc += 1
                    for k in range(8):
                        P.op("pe", lambda e, k=k, col=col, bk=bk, cur=cur, co=co: e.matmul(out=self.bank(bk, 128 + PAD, c0=co), lhsT=win.ap[:, k, col:col + 128], rhs=cur.ap[:, k, :],
                                                                                   start=(k == 0), stop=(k == 7)),
                             reads=[(win.b, None), (cur.b, None)], writes=[(PB[bk], None)])
                    P.op("act", lambda e, jc=jc, bk=bk, a=a, co=co: e.activation(out=a.ap, in_=self.bank(bk, 128, c0=co + 3), func=AF.Identity,
                                                                           scale=cw.ap[:, jc, 3:4], bias=cb.ap[:, jc:jc + 1]),
                         reads=[(PB[bk], None), (cw.b, None), (cb.b, None)], writes=[(a.b, None)])
                    for kk in (2, 1, 0):
                        P.op("dve", lambda e, jc=jc, bk=bk, a=a, kk=kk, co=co: e.scalar_tensor_tensor(out=a.ap, in0=self.bank(bk, 128, c0=co + kk), scalar=cw.ap[:, jc, kk:kk + 1],
                                                                                              in1=a.ap, op0=ALU.mult, op1=ALU.add),
                             reads=[(PB[bk], None), (cw.b, None), (a.b, None)], writes=[(a.b, None)])
                    P.op("act", lambda e, a=a, dst=dst: e.activation(out=dst.ap, in_=a.ap, func=AF.Silu), reads=[(a.b, None)], writes=[(dst.b, None)])
                stop_at(2)
                for i in range(2):
                    P.op("pe", lambda e, i=i: e.transpose(out=self.bank(5, 128, c0=128 + i * 128), in_=xsf[i].ap, identity=self.idf.ap),
                         reads=[(xsf[i].b, None), (self.idf.b, None)], writes=[(PB[5], None)])
                P.op("act", lambda e: e.copy(out=xs_tok.ap.rearrange("p a b -> p (a b)"), in_=self.bank(5, 256, c0=128)), reads=[(PB[5], None)], writes=[(xs_tok.b, None)])
                P.op("pe", lambda e, bt=bt: e.transpose(out=self.bankb(5, 128, c0=768), in_=bt.ap, identity=self.idb.ap),
                     reads=[(bt.b, None), (self.idb.b, None)], writes=[(PB[5], None)])
                P.op("dve", lambda e: e.tensor_copy(out=Btok.ap, in_=self.bankb(5, 128, c0=768)), reads=[(PB[5], None)], writes=[(Btok.b, None)])
                stop_at(3)
                for k in range(8):
                    P.op("pe", lambda e, k=k, g=g, cur=cur: e.matmul(out=self.bank(6, 256), lhsT=cur.ap[:, k, PAD:PAD + 128], rhs=win.ap[:, k, g * 256:(g + 1) * 256],
                                                                   start=(k == 0), stop=(k == 7)),
                         reads=[(cur.b, None), (win.b, None)], writes=[(PB[6], None)])
                P.op("act", lambda e: e.activation(out=zs.ap, in_=self.bank(6, 256), func=AF.Silu), reads=[(PB[6], None)], writes=[(zs.b, None)])
                stop_at(4)
                P.op("pool", lambda e, hs=hs: e.tensor_tensor(out=xdt.ap, in0=xs_tok.ap, in1=dt_.ap[:, hs].unsqueeze(2).to_broadcast([128, 4, 64]), op=ALU.mult),
                     reads=[(xs_tok.b, None), (dt_.b, None)], writes=[(xdt.b, None)])
                for c_ in range(2):
                    P.op("pool", lambda e, hs=hs, c_=c_: e.tensor_tensor(out=xw.ap[:, c_], in0=xs_tok.ap, in1=dtw2.ap[:, c_, hs].unsqueeze(2).to_broadcast([128, 4, 64]), op=ALU.mult),
                         reads=[(xs_tok.b, None), (dtw2.b, None)], writes=[(xw.b, c_)])
                stop_at(5)
                P.op("pool", lambda e, hs=hs: e.tensor_tensor(out=Lg.ap, in0=self.masks.ap[:, 1:2, :].to_broadcast([128, 4, 128]),
                                                              in1=dtA.ap[:, hs].unsqueeze(2).to_broadcast([128, 4, 128]), op=ALU.mult),
                     reads=[(self.masks.b, None), (dtA.b, None)], writes=[(Lg.b, None)])
                for hh in range(4):
                    P.op("pe", lambda e, hh=hh: e.matmul(out=self.bank(4, 128, c0=hh * 128), lhsT=Lg.ap[:, hh, :], rhs=Mle, start=True, stop=True),
                         reads=[(Lg.b, None), (self.masks.b, None)], writes=[(PB[4], None)])
                P.op("act", lambda e: e.activation(out=E.ap.rearrange("p a b -> p (a b)"), in_=self.bank(4, 512), func=AF.Exp), reads=[(PB[4], None)], writes=[(E.b, None)])
                stop_at(6)
                P.op("pe", lambda e, bt=bt, ct=ct: e.matmul(out=self.bank(5, 128), lhsT=bt.ap, rhs=ct.ap, start=True, stop=True),
                     reads=[(bt.b, None), (ct.b, None)], writes=[(PB[5], None)])
                P.op("dve", lambda e: e.tensor_tensor(out=CBm.ap, in0=self.bank(5, 128), in1=Mle, op=ALU.mult), reads=[(PB[5], None), (self.masks.b, None)], writes=[(CBm.b, None)])
                P.op("dve", lambda e: e.tensor_tensor(out=M.ap, in0=E.ap, in1=CBm.ap.unsqueeze(1).to_broadcast([128, 4, 128]), op=ALU.mult),
                     reads=[(E.b, None), (CBm.b, None)], writes=[(M.b, None)])
                for hh in range(4):
                    P.op("pe", lambda e, hh=hh: e.matmul(out=self.bank(6, 64, c0=256 + hh * 64), lhsT=M.ap[:, hh, :], rhs=xdt.ap[:, hh, :], start=True, stop=True),
                         reads=[(M.b, None), (xdt.b, None)], writes=[(PB[6], None)])
                stop_at(7)
                P.op("pe", lambda e: e.matmul(out=self.bank(7, 256, c0=256), lhsT=Btok.ap, rhs=xw.ap[:, 0].rearrange("p a b -> p (a b)"), start=True, stop=True),
                     reads=[(Btok.b, None), (xw.b, None)], writes=[(PB[7], None)])
                P.op("pe", lambda e: e.matmul(out=self.bank(3, 256, c0=256), lhsT=Btok.ap, rhs=xw.ap[:, 1].rearrange("p a b -> p (a b)"), start=True, stop=True),
                     reads=[(Btok.b, None), (xw.b, None)], writes=[(PB[3], None)])
                stop_at(8)
                for hh in range(4):
                    h = 4 * g + hh
                    P.op("pe", lambda e, hh=hh, h=h, ct=ct: e.matmul(out=self.bank(7, 64, p0=0, p1=64, c0=hh * 64), lhsT=ct.ap[:, 0:64], rhs=Hb0.ap[:, h * 64:(h + 1) * 64],
                                                                      start=True, stop=True),
                         reads=[(ct.b, None), (Hb0.b, g)], writes=[(PB[7], None)])
                stop_at(81)
                Hg = H.ap[:, g * 256:(g + 1) * 256].rearrange("p (a b) -> p a b", a=4)
                Hgf = H.ap[:, g * 256:(g + 1) * 256]
                P.op("dve", lambda e, hs=hs, Hg=Hg: e.tensor_tensor(out=hcd.ap, in0=Hg, in1=ex.ap[:, 2, hs].unsqueeze(2).to_broadcast([128, 4, 64]), op=ALU.mult),
                     reads=[(H.b, g), (ex.b, None)], writes=[(hcd.b, None)])
                stop_at(811)
                P.op("dve", lambda e, Hgf=Hgf: e.scalar_tensor_tensor(out=Hgf, in0=self.bank(7, 256, c0=256), scalar=1.0, in1=hcd.ap.rearrange("p a b -> p (a b)"), op0=ALU.mult, op1=ALU.add),
                     reads=[(hcd.b, None), (PB[7], None)], writes=[(H.b, g)])
                stop_at(812)
                P.op("act", lambda e, g=g, Hgf=Hgf: e.copy(out=Hb1.ap[:, g * 256:(g + 1) * 256], in_=Hgf), reads=[(H.b, g)], writes=[(Hb1.b, g)])
                stop_at(82)
                for hh in range(4):
                    h = 4 * g + hh
                    P.op("pe", lambda e, hh=hh, h=h, ct=ct: e.matmul(out=self.bank(7, 64, p0=64, p1=128, c0=hh * 64), lhsT=ct.ap[:, 64:128], rhs=Hb1.ap[:, h * 64:(h + 1) * 64],
                                                                      start=True, stop=True),
                         reads=[(ct.b, None), (Hb1.b, g)], writes=[(PB[7], None)])
                P.op("dve", lambda e, hs=hs, Hg=Hg: e.tensor_tensor(out=hcd.ap, in0=Hg, in1=ex.ap[:, 3, hs].unsqueeze(2).to_broadcast([128, 4, 64]), op=ALU.mult),
                     reads=[(H.b, g), (ex.b, None)], writes=[(hcd.b, None)])
                P.op("dve", lambda e, Hgf=Hgf: e.scalar_tensor_tensor(out=Hgf, in0=self.bank(3, 256, c0=256), scalar=1.0, in1=hcd.ap.rearrange("p a b -> p (a b)"), op0=ALU.mult, op1=ALU.add),
                     reads=[(hcd.b, None), (PB[3], None)], writes=[(H.b, g)])
                P.op("act", lambda e, g=g, Hgf=Hgf: e.copy(out=Hb0.ap[:, g * 256:(g + 1) * 256], in_=Hgf), reads=[(H.b, g)], writes=[(Hb0.b, g)])
                stop_at(9)
                P.op("dve", lambda e, hs=hs: e.tensor_tensor(out=t1.ap, in0=self.bank(7, 256).rearrange("p (a b) -> p a b", a=4),
                                                             in1=ex.ap[:, 0, hs].unsqueeze(2).to_broadcast([128, 4, 64]), op=ALU.mult),
                     reads=[(PB[7], None), (PB[7], None), (ex.b, None)], writes=[(t1.b, None)])
                P.op("dve", lambda e: e.tensor_tensor(out=t1.ap.rearrange("p a b -> p (a b)"), in0=t1.ap.rearrange("p a b -> p (a b)"), in1=self.bank(6, 256, c0=256), op=ALU.add),
                     reads=[(t1.b, None), (PB[6], None)], writes=[(t1.b, None)])
                P.op("pool", lambda e, hs=hs: e.tensor_tensor(out=t2.ap, in0=xs_tok.ap, in1=dsk[:, hs].unsqueeze(2).to_broadcast([128, 4, 64]), op=ALU.mult),
                     reads=[(xs_tok.b, None), (small.b, None)], writes=[(t2.b, None)])
                P.op("pool", lambda e: e.tensor_tensor(out=t1.ap, in0=t1.ap, in1=t2.ap, op=ALU.add), reads=[(t1.b, None), (t2.b, None)], writes=[(t1.b, None)])
                P.op("pool", lambda e: e.tensor_tensor(out=t1.ap.rearrange("p a b -> p (a b)"), in0=t1.ap.rearrange("p a b -> p (a b)"), in1=zs.ap, op=ALU.mult),
                     reads=[(t1.b, None), (zs.b, None)], writes=[(t1.b, None)])
                P.op("act", lambda e: e.activation(out=t2.ap.rearrange("p a b -> p (a b)"), in_=t1.ap.rearrange("p a b -> p (a b)"), func=AF.Square, accum_out=ss.ap[:, 0:1]),
                     reads=[(t1.b, None)], writes=[(t2.b, None), (ss.b, 0)])
                P.op("pool", lambda e: e.tensor_scalar(out=ss.ap[:, 1:2], in0=ss.ap[:, 0:1], scalar1=1.0 / 256.0, scalar2=RMS_EPS, op0=ALU.mult, op1=ALU.add),
                     reads=[(ss.b, 0)], writes=[(ss.b, 1)])
                P.op("pool", lambda e: e.tensor_tensor(out=ss.ap[:, 2:3], in0=ss.ap[:, 1:2], in1=self.mhalf.ap, op=ALU.pow),
                     reads=[(ss.b, 1), (self.mhalf.b, None)], writes=[(ss.b, 2)])
                P.op("dve", lambda e, g=g: e.scalar_tensor_tensor(out=yn.ap, in0=t1.ap.rearrange("p a b -> p (a b)"), scalar=ss.ap[:, 2:3], in1=ng.ap[:, g * 256:(g + 1) * 256],
                                                                  op0=ALU.mult, op1=ALU.mult),
                     reads=[(t1.b, None), (ss.b, 2), (ng.b, None)], writes=[(yn.b, None)])
                for i in range(2):
                    P.op("pe", lambda e, i=i: e.transpose(out=self.bankb(0, 128, c0=i * 128), in_=yn.ap[:, i * 128:(i + 1) * 128], identity=self.idb.ap),
                         reads=[(yn.b, None), (self.idb.b, None)], writes=[(PB[0], None)])
                P.op("act", lambda e, g=g: e.copy(out=ynT.ap[:, 2 * g:2 * g + 2, :], in_=self.bankb(0, 256).rearrange("p (a b) -> p a b", a=2)),
                     reads=[(PB[0], None)], writes=[(ynT.b, g)])
            for half in range(2):
                for c in range(16):
                    P.op("pe", lambda e, c=c, half=half: e.matmul(out=self.bank(1 + half), lhsT=ynT.ap[:, c, :], rhs=wout.ap[:, c, half * 512:(half + 1) * 512],
                                                                  start=(c == 0), stop=(c == 15)),
                         reads=[(ynT.b, None), (wout.b, None)], writes=[(PB[1 + half], None)])
            self.epilogue(1, xr.ap, (xr.b, None), lng, lnb, xout, t * 128, tmp)


    def ssd_phase2(self, li, j, xin, xout):
        P, d = self.P, self.dram
        self.phase()
        PB = self.PB
        NT = int(os.environ.get('K_NT', L // 128))
        NG = int(os.environ.get('K_NG', 8))
        NB = 5
        xTall = self.alloc("xTall", [8, 3 + L], BF16)
        cw = self.alloc("scw", [32, 4], F32)
        cb = self.alloc("scb", [32], F32)
        small = self.bcast_load("ssmall", d["ssm_small"][j:j + 1, :], 96)
        A_b = self.alloc("A_b", [32], F32)
        P.dma(cw.ap, d["ssm_cw"][j], writes=[(cw.b, None)])
        P.dma(cb.ap, d["ssm_cb"][j], writes=[(cb.b, None)])
        dtb, alog, dsk = small.ap[:, 0:32], small.ap[:, 32:64], small.ap[:, 64:96]
        Mle = self.masks.ap[:, 0, :]
        P.op("act", lambda e: e.activation(out=A_b.ap, in_=alog, func=AF.Exp), reads=[(small.b, None)], writes=[(A_b.b, None)])
        P.op("dve", lambda e: e.tensor_scalar(out=A_b.ap, in0=A_b.ap, scalar1=-1.0, scalar2=None, op0=ALU.mult), reads=[(A_b.b, None)], writes=[(A_b.b, None)])
        mark = self.off
        xld = [self.alloc("xld%d" % i, [D], F32) for i in range(2)]
        xbf = [self.alloc("xbf%d" % i, [D], BF16) for i in range(2)]
        P.op("pool", lambda e: e.memset(xTall.ap[:, :, 0:3], 0.0), writes=[(xTall.b, "pad")])
        for t in range(NT):
            self.make_xT_block(xin, t * 128, xld[t % 2].ap, xld[t % 2].b, None, xbf[t % 2], xTall, t, 3 + t * 128, (0, 5)[t % 2])
        P.barrier()
        self.off = mark
        wst = self.alloc("wst", [8, 772], F32)
        wg = [self.alloc("wg%d" % i, [8, 772], BF16) for i in range(2)]
        ngg = [self.alloc("ngg%d" % i, [256], F32) for i in range(2)]
        H = self.alloc("H", [4, 64], F32)
        Hb0 = self.alloc("Hb0", [256], BF16)
        Hb1 = self.alloc("Hb1", [256], BF16)
        acc = [self.alloc("acc%d" % i, [384], F32) for i in range(4)]
        XS0 = [self.alloc("XS0_%d" % i, [384], F32) for i in range(2)]
        XS1 = [self.alloc("XS1_%d" % i, [384], F32) for i in range(2)]
        BTs = [self.alloc("BTs_%d" % i, [384], BF16) for i in range(2)]
        CTs = [self.alloc("CTs_%d" % i, [384], BF16) for i in range(2)]

        def U(name, shape, dt):
            return [self.alloc("%s_%d" % (name, i), shape, dt) for i in range(NB)]
        dttu, dteu, dtu, dtAu, dtwu = U("dtt", [4], F32), U("dte", [4], F32), U("dt_", [4], F32), U("dtA", [4], F32), U("dtw", [4], F32)
        exu, dtw2u = U("ex", [4, 4], F32), U("dtw2", [2, 4], F32)
        Lgu, Eu, CBmu, Mu = U("Lg", [4, 128], F32), U("E", [4, 128], F32), U("CBm", [128], F32), U("M", [4, 128], BF16)
        xstu, Btoku, zsu = U("xs_tok", [4, 64], F32), U("Btok", [128], BF16), U("zs", [256], F32)
        xdtu, xwu = U("xdt", [4, 64], BF16), U("xw", [2, 4, 64], BF16)
        t1u, t2u, hcdu = U("t1", [4, 64], F32), U("t2", [4, 64], F32), U("hcd", [4, 64], F32)
        ssu, ynu, ynTu = U("ss", [4], F32), U("yn", [256], BF16), U("ynTu", [2, 128], BF16)
        wsrc = d["ssm_w_in"][j].rearrange("(c p) n -> p c n", p=128)
        ynTd = d["ynT"].rearrange("(c p) t -> p c t", p=128)

        def load_group(g):
            segs = [(2048 + g * 256, 256, 0), (4096 + g * 128, 128, 256), (5120 + g * 128, 128, 384), (g * 256, 256, 512), (6144 + 4 * g, 4, 768)]
            for (c0, w, o) in segs:
                P.dma(wst.ap[:, :, o:o + w], wsrc[:, :, c0:c0 + w], writes=[(wst.b, o)])
            wgt = wg[g % 2]
            P.op("pool", lambda e, wgt=wgt: e.tensor_copy(out=wgt.ap[:, 0:4, :], in_=wst.ap[:, 0:4, :]), reads=[(wst.b, None)], writes=[(wgt.b, 0)])
            P.op("act", lambda e, wgt=wgt: e.copy(out=wgt.ap[:, 4:8, :], in_=wst.ap[:, 4:8, :]), reads=[(wst.b, None)], writes=[(wgt.b, 1)])
            P.dma(ngg[g % 2].ap, d["ssm_norm_g"][j:j + 1, g * 256:(g + 1) * 256].partition_broadcast(128), writes=[(ngg[g % 2].b, None)])

        load_group(0)
        ccb = [0]
        for g in range(NG):
            if g + 1 < NG:
                load_group(g + 1)
            W = wg[g % 2]
            ng = ngg[g % 2]
            hs = slice(4 * g, 4 * g + 4)
            P.op("pool", lambda e: e.memset(H.ap, 0.0), writes=[(H.b, None)])
            P.op("pool", lambda e: e.memset(Hb0.ap, 0.0), writes=[(Hb0.b, None)])
            Hf = H.ap.rearrange("p a b -> p (a b)")
            def unit(t, sidx):
                def SS(k):
                    P.active = (k == sidx)
                u = t % NB
                xc = slice(t * 128, t * 128 + 131)
                xk = slice(3 + t * 128, 3 + (t + 1) * 128)
                dtt, dte, dt_, dtA, dtw, ex, dtw2 = dttu[u], dteu[u], dtu[u], dtAu[u], dtwu[u], exu[u], dtw2u[u]
                Lg, E, CBm, M = Lgu[u], Eu[u], CBmu[u], Mu[u]
                xs_tok, Btok, zs, xdt, xw = xstu[u], Btoku[u], zsu[u], xdtu[u], xwu[u]
                t1, t2, hcd, ss, yn, ynT = t1u[u], t2u[u], hcdu[u], ssu[u], ynu[u], ynTu[u]
                s3 = (t // 3) % 2
                o3 = (t % 3) * 128
                btT, ctT = BTs[s3], CTs[s3]
                btv, ctv = btT.ap[:, o3:o3 + 128], ctT.ap[:, o3:o3 + 128]
                SS(0)
                for k in range(8):
                    P.op("pe", lambda e, k=k, xk=xk, W=W: e.matmul(out=self.bank(0, 4, c0=256), lhsT=xTall.ap[:, k, xk], rhs=W.ap[:, k, 768:772], start=(k == 0), stop=(k == 7)),
                         reads=[(xTall.b, None), (W.b, None)], writes=[(PB[0], None)])
                P.op("dve", lambda e, dtt=dtt, hs=hs: e.tensor_tensor(out=dtt.ap, in0=self.bank(0, 4, c0=256), in1=dtb[:, hs], op=ALU.add), reads=[(PB[0], None), (small.b, None)], writes=[(dtt.b, None)])
                P.op("act", lambda e, dtt=dtt, dte=dte: e.activation(out=dte.ap, in_=dtt.ap, func=AF.Exp), reads=[(dtt.b, None)], writes=[(dte.b, None)])
                P.op("act", lambda e, dte=dte, dt_=dt_: e.activation(out=dt_.ap, in_=dte.ap, func=AF.Ln, bias=1.0), reads=[(dte.b, None)], writes=[(dt_.b, None)])
                P.op("dve", lambda e, dt_=dt_, dtA=dtA, hs=hs: e.tensor_tensor(out=dtA.ap, in0=dt_.ap, in1=A_b.ap[:, hs], op=ALU.mult), reads=[(dt_.b, None), (A_b.b, None)], writes=[(dtA.b, None)])
                SS(1)
                for m in range(4):
                    P.op("pe", lambda e, m=m, dtA=dtA: e.matmul(out=self.bank(0, 4, c0=320 + m * 4), lhsT=self.masks.ap[:, m, :], rhs=dtA.ap, start=True, stop=True),
                         reads=[(self.masks.b, None), (dtA.b, None)], writes=[(PB[0], None)])
                P.op("act", lambda e, ex=ex: e.activation(out=ex.ap, in_=self.bank(0, 16, c0=320).rearrange("p (a b) -> p a b", a=4), func=AF.Exp), reads=[(PB[0], None)], writes=[(ex.b, None)])
                P.op("dve", lambda e, dtw=dtw, dt_=dt_, ex=ex: e.tensor_tensor(out=dtw.ap, in0=dt_.ap, in1=ex.ap[:, 1, :], op=ALU.mult), reads=[(dt_.b, None), (ex.b, None)], writes=[(dtw.b, None)])
                for c_ in range(2):
                    P.op("dve", lambda e, c_=c_, dtw=dtw, dtw2=dtw2: e.tensor_scalar(out=dtw2.ap[:, c_, :], in0=dtw.ap, scalar1=self.masks.ap[:, 2 + c_, 0:1], scalar2=None, op0=ALU.mult),
                         reads=[(dtw.b, None), (self.masks.b, None)], writes=[(dtw2.b, c_)])
                SS(0)
                if t % 3 == 0:
                    Wd = min(3, NT - t) * 128
                    xcs = slice(t * 128, t * 128 + Wd + 3)
                    chunks = [(0, 2 * g, XS0[s3]), (128, 2 * g + 1, XS1[s3]), (256, 16 + g, btT), (384, 24 + g, ctT)]
                    for (col, jc, dst) in chunks:
                        bk = (1, 2)[ccb[0] % 2]
                        a = acc[ccb[0] % 4]
                        ccb[0] += 1
                        for k in range(8):
                            P.op("pe", lambda e, k=k, col=col, bk=bk, xcs=xcs, Wd=Wd, W=W: e.matmul(out=self.bank(bk, Wd + 3), lhsT=W.ap[:, k, col:col + 128], rhs=xTall.ap[:, k, xcs], start=(k == 0), stop=(k == 7)),
                                 reads=[(W.b, None), (xTall.b, None)], writes=[(PB[bk], None)])
                        P.op("act", lambda e, jc=jc, bk=bk, a=a, Wd=Wd: e.activation(out=a.ap[:, 0:Wd], in_=self.bank(bk, Wd, c0=3), func=AF.Identity, scale=cw.ap[:, jc, 3:4], bias=cb.ap[:, jc:jc + 1]),
                             reads=[(PB[bk], None), (cw.b, None), (cb.b, None)], writes=[(a.b, None)])
                        for kk in (2, 1, 0):
                            P.op("dve", lambda e, jc=jc, bk=bk, a=a, kk=kk, Wd=Wd: e.scalar_tensor_tensor(out=a.ap[:, 0:Wd], in0=self.bank(bk, Wd, c0=kk), scalar=cw.ap[:, jc, kk:kk + 1], in1=a.ap[:, 0:Wd], op0=ALU.mult, op1=ALU.add),
                                 reads=[(PB[bk], None), (cw.b, None), (a.b, None)], writes=[(a.b, None)])
                        P.op("act", lambda e, a=a, dst=dst, Wd=Wd: e.activation(out=dst.ap[:, 0:Wd], in_=a.ap[:, 0:Wd], func=AF.Silu), reads=[(a.b, None)], writes=[(dst.b, None)])
                SS(1)
                for i, xf in enumerate((XS0[s3], XS1[s3])):
                    P.op("pe", lambda e, i=i, xf=xf, o3=o3: e.transpose(out=self.bank(5, 128, c0=128 + i * 128), in_=xf.ap[:, o3:o3 + 128], identity=self.idf.ap),
                         reads=[(xf.b, None), (self.idf.b, None)], writes=[(PB[5], None)])
                P.op("act", lambda e, xs_tok=xs_tok: e.copy(out=xs_tok.ap.rearrange("p a b -> p (a b)"), in_=self.bank(5, 256, c0=128)), reads=[(PB[5], None)], writes=[(xs_tok.b, None)])
                P.op("pe", lambda e, btv=btv: e.transpose(out=self.bankb(5, 128, c0=768), in_=btv, identity=self.idb.ap), reads=[(btT.b, None), (self.idb.b, None)], writes=[(PB[5], None)])
                P.op("dve", lambda e, Btok=Btok: e.tensor_copy(out=Btok.ap, in_=self.bankb(5, 128, c0=768)), reads=[(PB[5], None)], writes=[(Btok.b, None)])
                for k in range(8):
                    P.op("pe", lambda e, k=k, xk=xk, W=W: e.matmul(out=self.bank(6, 256), lhsT=xTall.ap[:, k, xk], rhs=W.ap[:, k, 512:768], start=(k == 0), stop=(k == 7)),
                         reads=[(xTall.b, None), (W.b, None)], writes=[(PB[6], None)])
                P.op("act", lambda e, zs=zs: e.activation(out=zs.ap, in_=self.bank(6, 256), func=AF.Silu), reads=[(PB[6], None)], writes=[(zs.b, None)])
                P.op("pool", lambda e, xdt=xdt, xs_tok=xs_tok, dt_=dt_: e.tensor_tensor(out=xdt.ap, in0=xs_tok.ap, in1=dt_.ap.unsqueeze(2).to_broadcast([128, 4, 64]), op=ALU.mult),
                     reads=[(xs_tok.b, None), (dt_.b, None)], writes=[(xdt.b, None)])
                for c_ in range(2):
                    P.op("pool", lambda e, c_=c_, xw=xw, xs_tok=xs_tok, dtw2=dtw2: e.tensor_tensor(out=xw.ap[:, c_], in0=xs_tok.ap, in1=dtw2.ap[:, c_, :].unsqueeze(2).to_broadcast([128, 4, 64]), op=ALU.mult),
                         reads=[(xs_tok.b, None), (dtw2.b, None)], writes=[(xw.b, c_)])
                SS(0)
                P.op("pool", lambda e, Lg=Lg, dtA=dtA: e.tensor_tensor(out=Lg.ap, in0=self.masks.ap[:, 1:2, :].to_broadcast([128, 4, 128]), in1=dtA.ap.unsqueeze(2).to_broadcast([128, 4, 128]), op=ALU.mult),
                     reads=[(self.masks.b, None), (dtA.b, None)], writes=[(Lg.b, None)])
                SS(1)
                for hh in range(4):
                    P.op("pe", lambda e, hh=hh, Lg=Lg: e.matmul(out=self.bank(4, 128, c0=hh * 128), lhsT=Lg.ap[:, hh, :], rhs=Mle, start=True, stop=True),
                         reads=[(Lg.b, None), (self.masks.b, None)], writes=[(PB[4], None)])
                P.op("act", lambda e, E=E: e.activation(out=E.ap.rearrange("p a b -> p (a b)"), in_=self.bank(4, 512), func=AF.Exp), reads=[(PB[4], None)], writes=[(E.b, None)])
                P.op("pe", lambda e, btv=btv, ctv=ctv: e.matmul(out=self.bank(5, 128), lhsT=btv, rhs=ctv, start=True, stop=True), reads=[(btT.b, None), (ctT.b, None)], writes=[(PB[5], None)])
                P.op("dve", lambda e, CBm=CBm: e.tensor_tensor(out=CBm.ap, in0=self.bank(5, 128), in1=Mle, op=ALU.mult), reads=[(PB[5], None), (self.masks.b, None)], writes=[(CBm.b, None)])
                P.op("dve", lambda e, M=M, E=E, CBm=CBm: e.tensor_tensor(out=M.ap, in0=E.ap, in1=CBm.ap.unsqueeze(1).to_broadcast([128, 4, 128]), op=ALU.mult), reads=[(E.b, None), (CBm.b, None)], writes=[(M.b, None)])
                SS(2)
                P.op("pe", lambda e, Btok=Btok, xw=xw: e.matmul(out=self.bank(7, 256, c0=256), lhsT=Btok.ap, rhs=xw.ap[:, 0].rearrange("p a b -> p (a b)"), start=True, stop=True),
                     reads=[(Btok.b, None), (xw.b, None)], writes=[(PB[7], None)])
                P.op("pe", lambda e, Btok=Btok, xw=xw: e.matmul(out=self.bank(3, 256, c0=256), lhsT=Btok.ap, rhs=xw.ap[:, 1].rearrange("p a b -> p (a b)"), start=True, stop=True),
                     reads=[(Btok.b, None), (xw.b, None)], writes=[(PB[3], None)])
                for hh in range(4):
                    P.op("pe", lambda e, hh=hh, ctv=ctv: e.matmul(out=self.bank(7, 64, p0=0, p1=64, c0=hh * 64), lhsT=ctv[:, 0:64], rhs=Hb0.ap[:, hh * 64:(hh + 1) * 64], start=True, stop=True),
                         reads=[(ctT.b, None), (Hb0.b, None)], writes=[(PB[7], None)])
                for hh in range(4):
                    P.op("dve", lambda e, hh=hh, ex=ex: e.scalar_tensor_tensor(out=H.ap[:, hh, :], in0=H.ap[:, hh, :], scalar=ex.ap[:, 2, hh:hh + 1], in1=self.bank(7, 64, c0=256 + hh * 64), op0=ALU.mult, op1=ALU.add),
                         reads=[(H.b, hh), (ex.b, None), (PB[7], None)], writes=[(H.b, hh)])
                P.op("act", lambda e: e.copy(out=Hb1.ap, in_=Hf), reads=[(H.b, None)], writes=[(Hb1.b, None)])
                SS(3)
                for hh in range(4):
                    P.op("pe", lambda e, hh=hh, M=M, xdt=xdt: e.matmul(out=self.bank(6, 64, c0=256 + hh * 64), lhsT=M.ap[:, hh, :], rhs=xdt.ap[:, hh, :], start=True, stop=True),
                         reads=[(M.b, None), (xdt.b, None)], writes=[(PB[6], None)])
                for hh in range(4):
                    P.op("pe", lambda e, hh=hh, ctv=ctv: e.matmul(out=self.bank(7, 64, p0=64, p1=128, c0=hh * 64), lhsT=ctv[:, 64:128], rhs=Hb1.ap[:, hh * 64:(hh + 1) * 64], start=True, stop=True),
                         reads=[(ctT.b, None), (Hb1.b, None)], writes=[(PB[7], None)])
                SS(2)
                for hh in range(4):
                    P.op("dve", lambda e, hh=hh, ex=ex: e.scalar_tensor_tensor(out=H.ap[:, hh, :], in0=H.ap[:, hh, :], scalar=ex.ap[:, 3, hh:hh + 1], in1=self.bank(3, 64, c0=256 + hh * 64), op0=ALU.mult, op1=ALU.add),
                         reads=[(H.b, hh), (ex.b, None), (PB[3], None)], writes=[(H.b, hh)])
                P.op("act", lambda e: e.copy(out=Hb0.ap, in_=Hf), reads=[(H.b, None)], writes=[(Hb0.b, None)])
                SS(3)
                P.op("dve", lambda e, t1=t1, ex=ex: e.tensor_tensor(out=t1.ap, in0=self.bank(7, 256).rearrange("p (a b) -> p a b", a=4), in1=ex.ap[:, 0, :].unsqueeze(2).to_broadcast([128, 4, 64]), op=ALU.mult),
                     reads=[(PB[7], None), (ex.b, None)], writes=[(t1.b, None)])
                P.op("dve", lambda e, t1=t1: e.tensor_tensor(out=t1.ap.rearrange("p a b -> p (a b)"), in0=self.bank(6, 256, c0=256), in1=t1.ap.rearrange("p a b -> p (a b)"), op=ALU.add),
                     reads=[(t1.b, None), (PB[6], None)], writes=[(t1.b, None)])
                P.op("pool", lambda e, t2=t2, xs_tok=xs_tok, hs=hs: e.tensor_tensor(out=t2.ap, in0=xs_tok.ap, in1=dsk[:, hs].unsqueeze(2).to_broadcast([128, 4, 64]), op=ALU.mult),
                     reads=[(xs_tok.b, None), (small.b, None)], writes=[(t2.b, None)])
                P.op("pool", lambda e, t1=t1, t2=t2: e.tensor_tensor(out=t1.ap, in0=t1.ap, in1=t2.ap, op=ALU.add), reads=[(t1.b, None), (t2.b, None)], writes=[(t1.b, None)])
                P.op("pool", lambda e, t1=t1, zs=zs: e.tensor_tensor(out=t1.ap.rearrange("p a b -> p (a b)"), in0=t1.ap.rearrange("p a b -> p (a b)"), in1=zs.ap, op=ALU.mult), reads=[(t1.b, None), (zs.b, None)], writes=[(t1.b, None)])
                SS(4)
                P.op("act", lambda e, t1=t1, t2=t2, ss=ss: e.activation(out=t2.ap.rearrange("p a b -> p (a b)"), in_=t1.ap.rearrange("p a b -> p (a b)"), func=AF.Square, accum_out=ss.ap[:, 0:1]),
                     reads=[(t1.b, None)], writes=[(t2.b, None), (ss.b, 0)])
                P.op("pool", lambda e, ss=ss: e.tensor_scalar(out=ss.ap[:, 1:2], in0=ss.ap[:, 0:1], scalar1=1.0 / 256.0, scalar2=RMS_EPS, op0=ALU.mult, op1=ALU.add), reads=[(ss.b, 0)], writes=[(ss.b, 1)])
                P.op("pool", lambda e, ss=ss: e.tensor_tensor(out=ss.ap[:, 2:3], in0=ss.ap[:, 1:2], in1=self.mhalf.ap, op=ALU.pow), reads=[(ss.b, 1), (self.mhalf.b, None)], writes=[(ss.b, 2)])
                P.op("dve", lambda e, yn=yn, t1=t1, ss=ss, ng=ng: e.scalar_tensor_tensor(out=yn.ap, in0=t1.ap.rearrange("p a b -> p (a b)"), scalar=ss.ap[:, 2:3], in1=ng.ap, op0=ALU.mult, op1=ALU.mult),
                     reads=[(t1.b, None), (ss.b, 2), (ng.b, None)], writes=[(yn.b, None)])
                SS(5)
                for i in range(2):
                    P.op("pe", lambda e, i=i, yn=yn: e.transpose(out=self.bankb(0, 128, c0=i * 128), in_=yn.ap[:, i * 128:(i + 1) * 128], identity=self.idb.ap),
                         reads=[(yn.b, None), (self.idb.b, None)], writes=[(PB[0], None)])
                P.op("act", lambda e, ynT=ynT: e.copy(out=ynT.ap, in_=self.bankb(0, 256).rearrange("p (a b) -> p a b", a=2)), reads=[(PB[0], None)], writes=[(ynT.b, None)])
                P.dma(ynTd[:, 2 * g:2 * g + 2, t * 128:(t + 1) * 128], ynT.ap, reads=[(ynT.b, None)], writes=[(self.ynT_dep, (g, t))])
                P.active = True
            for i in range(NT + 5):
                for sidx in (5, 4, 3, 2, 1, 0):
                    t = i - sidx
                    if 0 <= t < NT:
                        unit(t, sidx)
        stop_at(11)
        P.barrier()
        self.off = mark
        wout = self.alloc("wout", [16, D], BF16)
        lng = self.bcast_load("lng", d["ln_g"][li * 2:li * 2 + 1, :], D)
        lnb = self.bcast_load("lnb", d["ln_b"][li * 2:li * 2 + 1, :], D)
        stg = [self.alloc("stg%d" % i, [2048], F32) for i in range(3)]
        self.load_w(wout, d["ssm_w_out"][j].rearrange("(c p) n -> p c n", p=128), 16, D, stg)
        yt = [self.alloc("yt%d" % i, [16, 512], BF16) for i in range(2)]
        xres = [self.alloc("xresc%d" % i, [D], F32) for i in range(3)]
        tmps = [self.ln_tmp("s2a", inplace=True), self.ln_tmp("s2b", inplace=True)]
        nb = 0
        ntt = max(1, NT // 4)

        def yload(tt):
            P.dma(yt[tt % 2].ap, ynTd[:, :, tt * 512:(tt + 1) * 512], reads=[(self.ynT_dep, None)], writes=[(yt[tt % 2].b, None)])

        def xload(i):
            P.dma(xres[i % 3].ap, xin[i * 128:(i + 1) * 128, :], writes=[(xres[i % 3].b, None)])
        yload(0)
        xload(0)
        for tt in range(ntt):
            y_ = yt[tt % 2]
            if tt + 1 < ntt:
                yload(tt + 1)
            for blk in range(4):
                tok0 = tt * 512 + blk * 128
                if tt * 4 + blk + 1 < ntt * 4:
                    xload(tt * 4 + blk + 1)
                xr = xres[(tt * 4 + blk) % 3]
                nb += 1
                yb = 1 + 2 * (nb % 2)
                for half in range(2):
                    for c in range(16):
                        P.op("pe", lambda e, c=c, half=half, blk=blk, y_=y_, yb=yb: e.matmul(out=self.bank(yb + half), lhsT=y_.ap[:, c, blk * 128:(blk + 1) * 128], rhs=wout.ap[:, c, half * 512:(half + 1) * 512],
                                                                                             start=(c == 0), stop=(c == 15)),
                             reads=[(y_.b, None), (wout.b, None)], writes=[(PB[yb + half], None)])
                self.epilogue(yb, xr.ap, (xr.b, None), lng, lnb, xout, tok0, tmps[nb % 2])

    def sg_phase(self, li, xin, xout):
        P, d = self.P, self.dram
        self.phase()
        PB = self.PB
        win = self.alloc("gwin", [8, 4096], BF16)
        wout = self.alloc("gwout", [16, D], BF16)
        wsT = self.alloc("wsT", [8, 128], BF16)
        bsp = self.alloc("bsp", [8], F32)
        bin_b = self.bcast_load("bin_b", d["sg_b_in"][0:1, :], 4096)
        vg = self.bcast_load("vg", d["sg_ln_g"][0:1, :], 2048)
        vb = self.bcast_load("vb", d["sg_ln_b"][0:1, :], 2048)
        lng = self.bcast_load("lng", d["ln_g"][li * 2:li * 2 + 1, :], D)
        lnb = self.bcast_load("lnb", d["ln_b"][li * 2:li * 2 + 1, :], D)
        mark = self.off
        stg = [self.alloc("stg%d" % i, [2048], F32) for i in range(3)]
        wsf = self.alloc("wsf", [8, 128], F32)
        P.dma(bsp.ap, d["sg_bs"][:, :], writes=[(bsp.b, None)])
        P.dma(wsf.ap, d["sg_w_s"][0].rearrange("g t s -> t g s"), writes=[(wsf.b, None)])
        for g in range(8):
            P.op("pe", lambda e, g=g: e.transpose(out=self.bank(3 + g // 4, 128, c0=(g % 4) * 128), in_=wsf.ap[:, g, :], identity=self.idf.ap),
                 reads=[(wsf.b, None), (self.idf.b, None)], writes=[(PB[3 + g // 4], None)])
        for g in range(8):
            P.op("dve", lambda e, g=g: e.tensor_tensor(out=wsT.ap[:, g, :], in0=self.bank(3 + g // 4, 128, c0=(g % 4) * 128), in1=self.masks.ap[:, 4, :], op=ALU.mult),
                 reads=[(PB[3 + g // 4], None), (self.masks.b, None)], writes=[(wsT.b, g)])
        self.load_w(win, d["sg_w_in"][0].rearrange("(c p) n -> p c n", p=128), 8, 4096, stg)
        self.load_w(wout, d["sg_w_out"][0].rearrange("(c p) n -> p c n", p=128), 16, D, stg)
        P.barrier()
        self.off = mark
        xres = [self.alloc("xres%d" % i, [D], F32) for i in range(2)]
        xbf = self.alloc("xbf", [D], BF16)
        xT = [self.alloc("xT%d" % i, [8, 128], BF16) for i in range(2)]
        u_tok = self.alloc("u_tok", [2048], F32)
        v_tok = self.alloc("v_tok", [2048], F32)
        vo = self.alloc("vo", [2048], F32)
        hA = [self.alloc("hA%d" % i, [512], F32) for i in range(2)]
        hB = [self.alloc("hB%d" % i, [512], F32) for i in range(2)]
        vln = self.alloc("vln", [2048], BF16)
        uv = self.alloc("uv", [2048], BF16)
        uvT = self.alloc("uvT", [16, 128], BF16)
        vst = self.alloc("vst", [24], F32)
        vmv = self.alloc("vmv", [2], F32)
        vsc = self.alloc("vsc", [4], F32)
        tmp = self.ln_tmp("g", inplace=True)
        GC = 2.0 * math.sqrt(2.0 / math.pi)
        for t in range(int(os.environ.get('K_NT', L // 128))):
            cur = xT[t % 2]
            xr = xres[t % 2]
            if t == 0:
                self.make_xT_block(xin, 0, xr.ap, xr.b, None, xbf, cur, None, 0, 0)
            for q in range(8):
                bk = 1 + q % 2
                a, b_ = hA[q % 2], hB[q % 2]
                for k in range(8):
                    P.op("pe", lambda e, k=k, q=q, bk=bk, cur=cur: e.matmul(out=self.bank(bk), lhsT=cur.ap[:, k, :], rhs=win.ap[:, k, q * 512:(q + 1) * 512],
                                                                           start=(k == 0), stop=(k == 7)),
                         reads=[(cur.b, None), (win.b, None)], writes=[(PB[bk], None)])
                P.op("dve", lambda e, q=q, bk=bk, a=a: e.tensor_tensor(out=a.ap, in0=self.bank(bk), in1=bin_b.ap[:, q * 512:(q + 1) * 512], op=ALU.add),
                     reads=[(PB[bk], None), (bin_b.b, None)], writes=[(a.b, None)])
                P.op("act", lambda e, a=a, b_=b_: e.activation(out=b_.ap, in_=a.ap, func=AF.Square), reads=[(a.b, None)], writes=[(b_.b, None)])
                P.op("dve", lambda e, a=a, b_=b_: e.scalar_tensor_tensor(out=b_.ap, in0=b_.ap, scalar=1.0 / 0.044715, in1=a.ap, op0=ALU.add, op1=ALU.mult),
                     reads=[(a.b, None), (b_.b, None)], writes=[(b_.b, None)])
                P.op("act", lambda e, b_=b_: e.activation(out=b_.ap, in_=b_.ap, func=AF.Sigmoid, scale=GC * 0.044715), reads=[(b_.b, None)], writes=[(b_.b, None)])
                dst = u_tok if q < 4 else v_tok
                dv = dst.ap[:, (q % 4) * 512:(q % 4 + 1) * 512]
                P.op("pool", lambda e, a=a, b_=b_, dv=dv: e.tensor_tensor(out=dv, in0=a.ap, in1=b_.ap, op=ALU.mult),
                     reads=[(a.b, None), (b_.b, None)], writes=[(dst.b, q % 4)])
            if t + 1 < int(os.environ.get('K_NT', L // 128)):
                self.make_xT_block(xin, (t + 1) * 128, xres[(t + 1) % 2].ap, xres[(t + 1) % 2].b, None, xbf, xT[(t + 1) % 2], None, 0, 0)
            self.layernorm(v_tok, 2048, vg, vb, vo, vst, vmv, vsc, LN_EPS, out_ap=vln.ap, out_buf=vln)
            for g in range(8):
                bk = 3 + g // 2
                P.op("pe", lambda e, g=g, bk=bk: e.matmul(out=self.bank(bk, 256, c0=(g % 2) * 256), lhsT=wsT.ap[:, g, :], rhs=vln.ap[:, g * 256:(g + 1) * 256], start=True, stop=True),
                     reads=[(wsT.b, None), (vln.b, None)], writes=[(PB[bk], None)])
            for g in range(8):
                bk = 3 + g // 2
                P.op("dve", lambda e, g=g, bk=bk: e.scalar_tensor_tensor(out=uv.ap[:, g * 256:(g + 1) * 256], in0=self.bank(bk, 256, c0=(g % 2) * 256), scalar=bsp.ap[:, g:g + 1],
                                                                          in1=u_tok.ap[:, g * 256:(g + 1) * 256], op0=ALU.add, op1=ALU.mult),
                     reads=[(PB[bk], None), (bsp.b, None), (u_tok.b, None)], writes=[(uv.b, g)])
            for hf in range(2):
                for c in range(8):
                    P.op("pe", lambda e, c=c, hf=hf: e.transpose(out=self.bankb(7, 128, c0=c * 128), in_=uv.ap[:, (hf * 8 + c) * 128:(hf * 8 + c + 1) * 128], identity=self.idb.ap),
                         reads=[(uv.b, None), (self.idb.b, None)], writes=[(PB[7], None)])
                P.op("act", lambda e, hf=hf: e.copy(out=uvT.ap[:, hf * 8:(hf + 1) * 8, :], in_=self.bankb(7, 1024).rearrange("p (c t) -> p c t", c=8)),
                     reads=[(PB[7], None)], writes=[(uvT.b, hf)])
            for half in range(2):
                for c in range(16):
                    P.op("pe", lambda e, c=c, half=half: e.matmul(out=self.bank(1 + half), lhsT=uvT.ap[:, c, :], rhs=wout.ap[:, c, half * 512:(half + 1) * 512],
                                                                  start=(c == 0), stop=(c == 15)),
                         reads=[(uvT.b, None), (wout.b, None)], writes=[(PB[1 + half], None)])
            self.epilogue(1, xr.ap, (xr.b, None), lng, lnb, xout, t * 128, tmp)


    def mla_phase(self, li, xin, xout):
        P, d = self.P, self.dram
        self.phase()
        PB = self.PB
        SCALE = 96.0 ** -0.5
        TWO_PI = 2.0 * math.pi
        C1 = 6.28125
        C2 = TWO_PI - C1
        PI_S = 3.1415925
        qT = self.alloc("qT", [3, L], BF16)
        kvT = self.alloc("kvT", [2, L], BF16)
        krT = self.alloc("krT", [L], BF16)
        cosT = self.alloc("cosT", [L], F32)
        sinT = self.alloc("sinT", [L], F32)
        wq = self.alloc("wq", [3, 16, 128], BF16)
        wqs = self.alloc("wqs", [3, 16, 64], BF16)
        wkn = self.alloc("wkn", [2, 16, 128], BF16)
        wv = self.alloc("wv", [2, 16, 64], BF16)
        wout = self.alloc("mwout", [8, D], BF16)
        lng = self.bcast_load("lng", d["ln_g"][li * 2:li * 2 + 1, :], D)
        lnb = self.bcast_load("lnb", d["ln_b"][li * 2:li * 2 + 1, :], D)
        mark = self.off
        win = self.alloc("mwin", [8, 640], BF16)
        wkr = self.alloc("wkr", [8, 64], BF16)
        wkrs = self.alloc("wkrs", [8, 64], BF16)
        gq = self.bcast_load("gq", d["mla_q_norm_g"][0:1, :], 384)
        gkv = self.bcast_load("gkv", d["mla_kv_norm_g"][0:1, :], 256)
        rc = self.alloc("rc", [2], F32)
        P.dma(rc.ap, d["consts"][:, 768:770], writes=[(rc.b, None)])
        mark2 = self.off
        stq = self.alloc("stq", [3, 1536], F32)
        stkv = self.alloc("stkv", [2, 2048], F32)
        stin = self.alloc("stin", [8, 672], F32)
        P.dma(stq.ap, d["mla_w_q_b"][0].rearrange("(c p) n -> p c n", p=128), writes=[(stq.b, None)])
        P.dma(stkv.ap, d["mla_w_kv_b"][0].rearrange("(c p) n -> p c n", p=128), writes=[(stkv.b, None)])
        P.dma(stin.ap, d["mla_w_in"][0].rearrange("(c p) n -> p c n", p=128), writes=[(stin.b, None)])
        for tz in (wq, wqs, wkn, wkr, wkrs):
            P.op("pool", lambda e, tz=tz: e.memset(tz.ap, 0.0), writes=[(tz.b, None)])
        for c in range(3):
            src = stq.ap[:, c, :].rearrange("p (h d) -> p h d", h=16)
            P.op("dve", lambda e, c=c, src=src: e.tensor_copy(out=wq.ap[:, c, :, 64:128], in_=src[:, :, 0:64]), reads=[(stq.b, None)], writes=[(wq.b, (c, 0))])
            P.op("act", lambda e, c=c, src=src: e.copy(out=wq.ap[:, c, :, 32:64], in_=src[:, :, 64:96]), reads=[(stq.b, None)], writes=[(wq.b, (c, 1))])
            P.op("dve", lambda e, c=c, src=src: e.tensor_copy(out=wqs.ap[:, c, :, 32:48], in_=src[:, :, 80:96]), reads=[(stq.b, None)], writes=[(wqs.b, (c, 0))])
            P.op("act", lambda e, c=c, src=src: e.copy(out=wqs.ap[:, c, :, 48:64], in_=src[:, :, 64:80]), reads=[(stq.b, None)], writes=[(wqs.b, (c, 1))])
        for c in range(2):
            src = stkv.ap[:, c, :].rearrange("p (h d) -> p h d", h=16)
            P.op("dve", lambda e, c=c, src=src: e.tensor_copy(out=wkn.ap[:, c, :, 64:128], in_=src[:, :, 0:64]), reads=[(stkv.b, None)], writes=[(wkn.b, (c, 0))])
            P.op("act", lambda e, c=c, src=src: e.copy(out=wv.ap[:, c, :, :], in_=src[:, :, 64:128]), reads=[(stkv.b, None)], writes=[(wv.b, c)])
        P.op("dve", lambda e: e.tensor_copy(out=win.ap, in_=stin.ap[:, :, 0:640]), reads=[(stin.b, None)], writes=[(win.b, None)])
        P.op("act", lambda e: e.copy(out=wkr.ap[:, :, 32:64], in_=stin.ap[:, :, 640:672]), reads=[(stin.b, None)], writes=[(wkr.b, 1)])
        P.op("dve", lambda e: e.tensor_copy(out=wkrs.ap[:, :, 32:48], in_=stin.ap[:, :, 656:672]), reads=[(stin.b, None)], writes=[(wkrs.b, 1)])
        P.op("act", lambda e: e.copy(out=wkrs.ap[:, :, 48:64], in_=stin.ap[:, :, 640:656]), reads=[(stin.b, None)], writes=[(wkrs.b, 2)])
        P.barrier()
        self.off = mark2
        stg = [self.alloc("stg%d" % i, [2048], F32) for i in range(3)]
        self.load_w(wout, d["mla_w_out"][0].rearrange("(c p) n -> p c n", p=128), 8, D, stg)
        P.barrier()
        self.off = mark2
        stop_at(20)
        posi = self.alloc("posi", [1024], I32)
        ki = self.alloc("ki", [1024], I32)
        pf = self.alloc("pf", [1024], F32)
        ang = self.alloc("ang", [1024], F32)
        kf = self.alloc("kf", [1024], F32)
        rr = self.alloc("rr", [1024], F32)
        r2 = self.alloc("r2", [1024], F32)
        mm = self.alloc("mm", [1024], F32)
        R = slice(0, 64)
        for ch in range(4):
            cs = slice(ch * 1024, (ch + 1) * 1024)
            P.dma(posi.ap[R], d["positions"][0:1, cs].partition_broadcast(64), writes=[(posi.b, None)])
            P.op("dve", lambda e: e.tensor_copy(out=pf.ap[R], in_=posi.ap[R]), reads=[(posi.b, None)], writes=[(pf.b, None)])
            P.op("dve", lambda e: e.tensor_scalar(out=ang.ap[R], in0=pf.ap[R], scalar1=rc.ap[R, 0:1], scalar2=None, op0=ALU.mult), reads=[(pf.b, None), (rc.b, None)], writes=[(ang.b, None)])
            P.op("dve", lambda e: e.tensor_scalar(out=pf.ap[R], in0=ang.ap[R], scalar1=1.0 / TWO_PI, scalar2=None, op0=ALU.mult), reads=[(ang.b, None)], writes=[(pf.b, None)])
            P.op("dve", lambda e: e.tensor_copy(out=ki.ap[R], in_=pf.ap[R]), reads=[(pf.b, None)], writes=[(ki.b, None)])
            P.op("dve", lambda e: e.tensor_copy(out=kf.ap[R], in_=ki.ap[R]), reads=[(ki.b, None)], writes=[(kf.b, None)])
            P.op("dve", lambda e: e.scalar_tensor_tensor(out=rr.ap[R], in0=kf.ap[R], scalar=-C1, in1=ang.ap[R], op0=ALU.mult, op1=ALU.add), reads=[(kf.b, None), (ang.b, None)], writes=[(rr.b, None)])
            P.op("dve", lambda e: e.scalar_tensor_tensor(out=rr.ap[R], in0=kf.ap[R], scalar=-C2, in1=rr.ap[R], op0=ALU.mult, op1=ALU.add), reads=[(kf.b, None), (rr.b, None)], writes=[(rr.b, None)])
            P.op("dve", lambda e: e.tensor_scalar(out=r2.ap[R], in0=rr.ap[R], scalar1=math.pi / 2, scalar2=None, op0=ALU.add), reads=[(rr.b, None)], writes=[(r2.b, None)])
            P.op("dve", lambda e: e.tensor_scalar(out=mm.ap[R], in0=r2.ap[R], scalar1=math.pi, scalar2=TWO_PI, op0=ALU.is_gt, op1=ALU.mult), reads=[(r2.b, None)], writes=[(mm.b, None)])
            P.op("dve", lambda e: e.tensor_tensor(out=r2.ap[R], in0=r2.ap[R], in1=mm.ap[R], op=ALU.subtract), reads=[(r2.b, None), (mm.b, None)], writes=[(r2.b, None)])
            P.op("dve", lambda e: e.tensor_scalar(out=rr.ap[R], in0=rr.ap[R], scalar1=PI_S, scalar2=-PI_S, op0=ALU.min, op1=ALU.max), reads=[(rr.b, None)], writes=[(rr.b, None)])
            P.op("dve", lambda e: e.tensor_scalar(out=r2.ap[R], in0=r2.ap[R], scalar1=PI_S, scalar2=-PI_S, op0=ALU.min, op1=ALU.max), reads=[(r2.b, None)], writes=[(r2.b, None)])
            P.op("act", lambda e: e.activation(out=mm.ap[R], in_=rr.ap[R], func=AF.Sin), reads=[(rr.b, None)], writes=[(mm.b, None)])
            P.op("dve", lambda e, cs=cs: e.tensor_scalar(out=sinT.ap[R, cs], in0=mm.ap[R], scalar1=rc.ap[R, 1:2], scalar2=None, op0=ALU.mult), reads=[(mm.b, None), (rc.b, None)], writes=[(sinT.b, ch)])
            P.op("act", lambda e, cs=cs: e.activation(out=cosT.ap[R, cs], in_=r2.ap[R], func=AF.Sin), reads=[(r2.b, None)], writes=[(cosT.b, ch)])
        stop_at(201)
        xres = [self.alloc("xres%d" % i, [D], F32) for i in range(2)]
        xbf = self.alloc("xbf", [D], BF16)
        xT = [self.alloc("xT%d" % i, [8, 128], BF16) for i in range(2)]
        junk = self.alloc("junk", [384], F32)
        ss = self.alloc("ss", [8], F32)
        qn = self.alloc("qn", [384], BF16)
        kvn = self.alloc("kvn", [256], BF16)
        rt1 = self.alloc("rt1", [512], F32)
        rt2 = self.alloc("rt2", [512], F32)
        NT = int(os.environ.get('K_NT', L // 128))
        for t in range(NT):
            cur = xT[t % 2]
            xr = xres[t % 2]
            ts_ = slice(t * 128, (t + 1) * 128)
            self.make_xT_block(xin, t * 128, xr.ap, xr.b, None, xbf, cur, None, 0, 0)
            for k in range(8):
                P.op("pe", lambda e, k=k, cur=cur: e.matmul(out=self.bank(1, 384), lhsT=cur.ap[:, k, :], rhs=win.ap[:, k, 0:384], start=(k == 0), stop=(k == 7)),
                     reads=[(cur.b, None), (win.b, None)], writes=[(PB[1], None)])
            for k in range(8):
                P.op("pe", lambda e, k=k, cur=cur: e.matmul(out=self.bank(2, 256), lhsT=cur.ap[:, k, :], rhs=win.ap[:, k, 384:640], start=(k == 0), stop=(k == 7)),
                     reads=[(cur.b, None), (win.b, None)], writes=[(PB[2], None)])
            P.op("act", lambda e: e.activation(out=junk.ap, in_=self.bank(1, 384), func=AF.Square, accum_out=ss.ap[:, 0:1]), reads=[(PB[1], None)], writes=[(junk.b, None), (ss.b, 0)])
            P.op("act", lambda e: e.activation(out=junk.ap[:, 0:256], in_=self.bank(2, 256), func=AF.Square, accum_out=ss.ap[:, 1:2]), reads=[(PB[2], None)], writes=[(junk.b, None), (ss.b, 1)])
            stop_at(211)
            P.op("pool", lambda e: e.tensor_scalar(out=ss.ap[:, 2:3], in0=ss.ap[:, 0:1], scalar1=1.0 / 384.0, scalar2=RMS_EPS, op0=ALU.mult, op1=ALU.add), reads=[(ss.b, 0)], writes=[(ss.b, 2)])
            P.op("pool", lambda e: e.tensor_scalar(out=ss.ap[:, 3:4], in0=ss.ap[:, 1:2], scalar1=1.0 / 256.0, scalar2=RMS_EPS, op0=ALU.mult, op1=ALU.add), reads=[(ss.b, 1)], writes=[(ss.b, 3)])
            for i_ in range(2):
                P.op("pool", lambda e, i_=i_: e.tensor_tensor(out=ss.ap[:, 4 + i_:5 + i_], in0=ss.ap[:, 2 + i_:3 + i_], in1=self.mhalf.ap, op=ALU.pow),
                     reads=[(ss.b, 2 + i_), (self.mhalf.b, None)], writes=[(ss.b, 4 + i_)])
            stop_at(212)
            P.op("dve", lambda e: e.scalar_tensor_tensor(out=qn.ap, in0=self.bank(1, 384), scalar=ss.ap[:, 4:5], in1=gq.ap, op0=ALU.mult, op1=ALU.mult),
                 reads=[(PB[1], None), (ss.b, 4), (gq.b, None)], writes=[(qn.b, None)])
            stop_at(2121)
            P.op("dve", lambda e: e.scalar_tensor_tensor(out=kvn.ap, in0=self.bank(2, 256), scalar=ss.ap[:, 5:6], in1=gkv.ap, op0=ALU.mult, op1=ALU.mult),
                 reads=[(PB[2], None), (ss.b, 5), (gkv.b, None)], writes=[(kvn.b, None)])
            stop_at(2122)
            for c in range(3):
                P.op("pe", lambda e, c=c: e.transpose(out=self.bankb(3, 128, c0=c * 128), in_=qn.ap[:, c * 128:(c + 1) * 128], identity=self.idb.ap),
                     reads=[(qn.b, None), (self.idb.b, None)], writes=[(PB[3], None)])
            for c in range(2):
                P.op("pe", lambda e, c=c: e.transpose(out=self.bankb(3, 128, c0=(3 + c) * 128), in_=kvn.ap[:, c * 128:(c + 1) * 128], identity=self.idb.ap),
                     reads=[(kvn.b, None), (self.idb.b, None)], writes=[(PB[3], None)])
            stop_at(2123)
            P.op("act", lambda e, ts_=ts_: e.copy(out=qT.ap[:, :, ts_], in_=self.bankb(3, 384).rearrange("p (c t) -> p c t", c=3)), reads=[(PB[3], None)], writes=[(qT.b, t)])
            stop_at(2124)
            P.op("act", lambda e, ts_=ts_: e.copy(out=kvT.ap[:, :, ts_], in_=self.bankb(3, 256, c0=384).rearrange("p (c t) -> p c t", c=2)), reads=[(PB[3], None)], writes=[(kvT.b, t)])
            stop_at(213)
            for k in range(8):
                P.op("pe", lambda e, k=k, cur=cur: e.matmul(out=self.bank(4, 128, p0=0, p1=64), lhsT=wkr.ap[:, k, :], rhs=cur.ap[:, k, :], start=(k == 0), stop=(k == 7)),
                     reads=[(cur.b, None), (wkr.b, None)], writes=[(PB[4], None)])
            for k in range(8):
                P.op("pe", lambda e, k=k, cur=cur: e.matmul(out=self.bank(4, 128, p0=0, p1=64, c0=128), lhsT=wkrs.ap[:, k, :], rhs=cur.ap[:, k, :], start=(k == 0), stop=(k == 7)),
                     reads=[(cur.b, None), (wkrs.b, None)], writes=[(PB[4], None)])
            stop_at(214)
            P.op("dve", lambda e, ts_=ts_: e.tensor_tensor(out=rt1.ap[32:64, 0:128], in0=self.bank(4, 128, p0=32, p1=64), in1=cosT.ap[32:64, ts_], op=ALU.mult),
                 reads=[(PB[4], None), (cosT.b, None)], writes=[(rt1.b, None)])
            P.op("dve", lambda e, ts_=ts_: e.tensor_tensor(out=rt2.ap[32:64, 0:128], in0=self.bank(4, 128, p0=32, p1=64, c0=128), in1=sinT.ap[32:64, ts_], op=ALU.mult),
                 reads=[(PB[4], None), (sinT.b, None)], writes=[(rt2.b, None)])
            P.op("pool", lambda e, ts_=ts_: e.tensor_tensor(out=krT.ap[32:64, ts_], in0=rt1.ap[32:64, 0:128], in1=rt2.ap[32:64, 0:128], op=ALU.add),
                 reads=[(rt1.b, None), (rt2.b, None)], writes=[(krT.b, t)])
        stop_at(21)
        P.barrier()
        self.off = mark
        QT = [self.alloc("QT%d" % i, [L], BF16) for i in range(2)]
        KT = [self.alloc("KT%d" % i, [L], BF16) for i in range(2)]
        V = [self.alloc("V%d" % i, [32, 65], BF16) for i in range(2)]
        PT = [self.alloc("PT%d" % i, [512], BF16) for i in range(3)]
        oTh = [self.alloc("oTh%d" % i, [L], BF16) for i in range(2)]
        rq1 = self.alloc("rt1b", [512], F32)
        rq2 = self.alloc("rt2b", [512], F32)
        on = [self.alloc("on%d" % i, [64], BF16) for i in range(2)]
        rden = self.alloc("rden", [4], F32)
        for i in range(2):
            P.op("pool", lambda e, i=i: e.memset(QT[i].ap[0:32, :], 0.0), writes=[(QT[i].b, "z")])
            P.op("pool", lambda e, i=i: e.memset(KT[i].ap[0:32, :], 0.0), writes=[(KT[i].b, "z")])
            P.op("pool", lambda e, i=i: e.memset(V[i].ap[:, :, 64:65], 1.0), writes=[(V[i].b, "one")])
        NQT = max(1, NT // 4)
        cnt = 0
        for h in range(int(os.environ.get('K_NH', 16))):
            qt_, kt_, v_, oh = QT[h % 2], KT[h % 2], V[h % 2], oTh[h % 2]
            P.op("pool", lambda e, kt_=kt_: e.tensor_copy(out=kt_.ap[32:64, 0:NT * 128], in_=krT.ap[32:64, 0:NT * 128]), reads=[(krT.b, None)], writes=[(kt_.b, "r")])
            for tt in range(NQT):
                cs = slice(tt * 512, (tt + 1) * 512)
                for c in range(2):
                    P.op("pe", lambda e, c=c, h=h, cs=cs: e.matmul(out=self.bank(1), lhsT=wkn.ap[:, c, h, :], rhs=kvT.ap[:, c, cs], start=(c == 0), stop=(c == 1)),
                         reads=[(wkn.b, None), (kvT.b, None)], writes=[(PB[1], None)])
                P.op("act", lambda e, kt_=kt_, cs=cs: e.copy(out=kt_.ap[64:128, cs], in_=self.bank(1, 512, p0=64, p1=128)), reads=[(PB[1], None)], writes=[(kt_.b, ("n", tt))])
            for tt in range(NQT):
                cs = slice(tt * 512, (tt + 1) * 512)
                for c in range(3):
                    P.op("pe", lambda e, c=c, h=h, cs=cs: e.matmul(out=self.bank(0), lhsT=wq.ap[:, c, h, :], rhs=qT.ap[:, c, cs], start=(c == 0), stop=(c == 2)),
                         reads=[(wq.b, None), (qT.b, None)], writes=[(PB[0], None)])
                for c in range(3):
                    P.op("pe", lambda e, c=c, h=h, cs=cs: e.matmul(out=self.bank(1, 512, p0=0, p1=64), lhsT=wqs.ap[:, c, h, :], rhs=qT.ap[:, c, cs], start=(c == 0), stop=(c == 2)),
                         reads=[(wqs.b, None), (qT.b, None)], writes=[(PB[1], None)])
                P.op("act", lambda e, qt_=qt_, cs=cs: e.copy(out=qt_.ap[64:128, cs], in_=self.bank(0, 512, p0=64, p1=128)), reads=[(PB[0], None)], writes=[(qt_.b, ("n", tt))])
                P.op("dve", lambda e, cs=cs: e.tensor_tensor(out=rq1.ap[32:64, :], in0=self.bank(0, 512, p0=32, p1=64), in1=cosT.ap[32:64, cs], op=ALU.mult),
                     reads=[(PB[0], None), (cosT.b, None)], writes=[(rq1.b, None)])
                P.op("dve", lambda e, cs=cs: e.tensor_tensor(out=rq2.ap[32:64, :], in0=self.bank(1, 512, p0=32, p1=64), in1=sinT.ap[32:64, cs], op=ALU.mult),
                     reads=[(PB[1], None), (sinT.b, None)], writes=[(rq2.b, None)])
                P.op("pool", lambda e, qt_=qt_, cs=cs: e.tensor_tensor(out=qt_.ap[32:64, cs], in0=rq1.ap[32:64, :], in1=rq2.ap[32:64, :], op=ALU.add),
                     reads=[(rq1.b, None), (rq2.b, None)], writes=[(qt_.b, ("r", tt))])
            for kg in range((NQT * 4 + 7) // 8):
                for kb in range(kg * 8, min(kg * 8 + 8, NQT * 4)):
                    for c in range(2):
                        P.op("pe", lambda e, c=c, h=h, kb=kb: e.matmul(out=self.bank(1, 64, c0=(kb % 8) * 64), lhsT=kvT.ap[:, c, kb * 128:(kb + 1) * 128], rhs=wv.ap[:, c, h, :],
                                                                       start=(c == 0), stop=(c == 1)),
                             reads=[(wv.b, None), (kvT.b, None)], writes=[(PB[1], None)])
                nk = min(8, NQT * 4 - kg * 8)
                P.op("dve", lambda e, v_=v_, kg=kg, nk=nk: e.tensor_copy(out=v_.ap[:, kg * 8:kg * 8 + nk, 0:64], in_=self.bank(1, nk * 64).rearrange("p (a b) -> p a b", a=nk)),
                     reads=[(PB[1], None)], writes=[(v_.b, ("v", kg))])
            steps = [(qt, kb) for qt in range(NQT) for kb in range(4 * qt + 4)]
            SB = (2, 3, 1)

            def geom(i):
                qt, kb = steps[i]
                diag = kb >= 4 * qt
                j = kb - 4 * qt if diag else 0
                return qt, kb, diag, j, qt * 512 + 128 * j, 512 - 128 * j, SB[i % 3], PT[i % 3]

            def emit_score(i):
                qt, kb, diag, j, q0, N, sb, pt = geom(i)
                P.op("pe", lambda e, kb=kb, q0=q0, N=N, sb=sb, kt_=kt_, qt_=qt_: e.matmul(out=self.bank(sb, N), lhsT=kt_.ap[:, kb * 128:(kb + 1) * 128], rhs=qt_.ap[:, q0:q0 + N], start=True, stop=True),
                     reads=[(kt_.b, None), (qt_.b, None)], writes=[(PB[sb], None)])
                P.op("act", lambda e, pt=pt, N=N, sb=sb: e.activation(out=pt.ap[:, 0:N], in_=self.bank(sb, N), func=AF.Exp, scale=SCALE),
                     reads=[(PB[sb], None)], writes=[(pt.b, None)])
                if diag:
                    P.op("pool", lambda e, pt=pt: e.memset(pt.ap[64:128, 0:64], 0.0), reads=[(pt.b, None)], writes=[(pt.b, None)])

            def emit_pv(i):
                qt, kb, diag, j, q0, N, sb, pt = geom(i)
                for qb in range(j, 4):
                    P.op("pe", lambda e, pt=pt, kb=kb, qb=qb, j=j, qt=qt, v_=v_: e.matmul(out=self.bank(4 + qb, 65), lhsT=pt.ap[:, (qb - j) * 128:(qb - j + 1) * 128], rhs=v_.ap[:, kb, :],
                                                                                  start=(kb == 0), stop=(kb == 4 * qt + qb)),
                         reads=[(pt.b, None), (v_.b, None)], writes=[(PB[4 + qb], None)])
                if kb == 4 * qt + 3:
                    for qb in range(4):
                        o_ = on[qb % 2]
                        P.op("dve", lambda e, qb=qb: e.reciprocal(out=rden.ap[:, qb:qb + 1], in_=self.bank(4 + qb, 1, c0=64)), reads=[(PB[4 + qb], None)], writes=[(rden.b, qb)])
                        P.op("dve", lambda e, qb=qb, o_=o_: e.tensor_scalar(out=o_.ap, in0=self.bank(4 + qb, 64), scalar1=rden.ap[:, qb:qb + 1], scalar2=None, op0=ALU.mult),
                             reads=[(PB[4 + qb], None), (rden.b, qb)], writes=[(o_.b, None)])
                        P.op("pe", lambda e, qb=qb, o_=o_: e.transpose(out=self.bankb(0, 128, p0=0, p1=64, c0=qb * 128), in_=o_.ap, identity=self.idb.ap),
                             reads=[(o_.b, None), (self.idb.b, None)], writes=[(PB[0], None)])
                    P.op("act", lambda e, qt=qt, oh=oh: e.copy(out=oh.ap[0:64, qt * 512:(qt + 1) * 512], in_=self.bankb(0, 512, p0=0, p1=64)), reads=[(PB[0], None)], writes=[(oh.b, qt)])

            LA = 2
            for i in range(min(LA, len(steps))):
                emit_score(i)
            for i in range(len(steps)):
                if i + LA < len(steps):
                    emit_score(i + LA)
                emit_pv(i)
            P.dma(d["oT"][h * 64:(h + 1) * 64, 0:NT * 128], oh.ap[0:64, 0:NT * 128], reads=[(oh.b, None)], writes=[(self.oT_dep, h)])
        stop_at(22)
        P.barrier()
        self.off = mark
        oTt = [self.alloc("oTt%d" % i, [8, 512], BF16) for i in range(2)]
        xres = [self.alloc("xresc%d" % i, [D], F32) for i in range(3)]
        tmps = [self.ln_tmp("ma", inplace=True), self.ln_tmp("mb", inplace=True)]
        oTv = d["oT"].rearrange("(c p) t -> p c t", p=128)
        nb = 0

        def oload(tt):
            P.dma(oTt[tt % 2].ap, oTv[:, :, tt * 512:(tt + 1) * 512], reads=[(self.oT_dep, None)], writes=[(oTt[tt % 2].b, None)])

        def xload(i):
            P.dma(xres[i % 3].ap, xin[i * 128:(i + 1) * 128, :], writes=[(xres[i % 3].b, None)])
        oload(0)
        xload(0)
        for tt in range(NQT):
            ot = oTt[tt % 2]
            if tt + 1 < NQT:
                oload(tt + 1)
            for blk in range(4):
                tok0 = tt * 512 + blk * 128
                if tt * 4 + blk + 1 < NQT * 4:
                    xload(tt * 4 + blk + 1)
                xr = xres[(tt * 4 + blk) % 3]
                nb += 1
                yb = 1 + 2 * (nb % 2)
                for half in range(2):
                    for c in range(8):
                        P.op("pe", lambda e, c=c, half=half, blk=blk, ot=ot, yb=yb: e.matmul(out=self.bank(yb + half), lhsT=ot.ap[:, c, blk * 128:(blk + 1) * 128], rhs=wout.ap[:, c, half * 512:(half + 1) * 512],
                                                                                             start=(c == 0), stop=(c == 7)),
                             reads=[(ot.b, None), (wout.b, None)], writes=[(PB[yb + half], None)])
                self.epilogue(yb, xr.ap, (xr.b, None), lng, lnb, xout, tok0, tmps[nb % 2])


def make_consts():
    c = np.zeros((128, 1024), np.float32)
    c[:, 0:128] = np.eye(128, dtype=np.float32)
    t = np.arange(128)
    same = (t[:, None] // 64) == (t[None, :] // 64)
    c[:, 128:256] = (same & (t[:, None] <= t[None, :])).astype(np.float32)
    c[:, 256:384] = (same & (t[:, None] > t[None, :])).astype(np.float32)
    c[:, 384:512] = (t[:, None] < 64).astype(np.float32) * np.ones((1, 128), np.float32)
    c[:, 512:640] = (t[:, None] >= 64).astype(np.float32) * np.ones((1, 128), np.float32)
    c[:, 640:768] = (t[:, None] <= t[None, :]).astype(np.float32)
    i = t % 32
    c[:, 768] = (10000.0 ** (-((i % 16).astype(np.float32) * 2.0 / 32.0))).astype(np.float32)
    c[:, 769] = np.where(i < 16, -1.0, 1.0)
    return c


def build(sublayers, x_from_input=True):
    nc = bass.Bass("TRN2", target_bir_lowering=False)
    dram = {}

    def din(name, shape, dt=F32):
        dram[name] = nc.dram_tensor(name, list(shape), dt, kind="ExternalInput").ap()

    din("x", [L, D])
    din("consts", [128, 1024])
    din("ffn_w_in", [DEPTH, D, 2 * FH])
    din("ffn_w_out", [DEPTH, FH, D])
    din("ffn_cw", [DEPTH, 128, 44, 3])
    din("ffn_cb", [DEPTH, 128, 44])
    din("ssm_w_in", [2, D, 6176])
    din("ssm_w_out", [2, 2048, D])
    din("ssm_cw", [2, 128, 32, 4])
    din("ssm_cb", [2, 128, 32])
    din("ssm_small", [2, 96])
    din("ssm_norm_g", [2, 2048])
    din("sg_w_in", [1, D, 4096])
    din("sg_w_out", [1, 2048, D])
    din("sg_b_in", [1, 4096])
    din("sg_ln_g", [1, 2048])
    din("sg_ln_b", [1, 2048])
    din("sg_w_s", [1, 8, 128, 128])
    din("sg_bs", [128, 8])
    din("positions", [1, L], I32)
    din("mla_w_in", [1, D, 672])
    din("mla_w_q_b", [1, 384, 1536])
    din("mla_w_kv_b", [1, 256, 2048])
    din("mla_w_out", [1, D, D])
    din("mla_q_norm_g", [1, 384])
    din("mla_kv_norm_g", [1, 256])
    din("ln_g", [DEPTH * 2, D])
    din("ln_b", [DEPTH * 2, D])
    out = nc.dram_tensor("out", [L, D], F32, kind="ExternalOutput").ap()
    xa = nc.dram_tensor("xa", [L, D], F32, kind="Internal").ap()
    xb = nc.dram_tensor("xb", [L, D], F32, kind="Internal").ap()
    dram["oT"] = nc.dram_tensor("oT", [D, L], BF16, kind="Internal").ap()
    dram["ynT"] = nc.dram_tensor("ynT", [2048, L], BF16, kind="Internal").ap()
    P = Prog(nc)
    with ExitStack() as es:
        arena = es.enter_context(nc.sbuf_tensor("arena", [128, ARENA_ELEMS], BF16))
        ps = es.enter_context(nc.psum_tensor("ps", [128, 4096], F32))
        sems = {e: es.enter_context(nc.semaphore("s_" + e)) for e in ENGS}
        dsems = [es.enter_context(nc.semaphore("d%d" % i)) for i in range(NDSEM)]
        block = es.enter_context(nc.Block())
        kb = KB(nc, P, arena, ps, dram)
        kb.xbuf_dep = {id(xa): Buf("xa"), id(xb): Buf("xb"), id(out): Buf("out"), id(dram["x"]): Buf("xin")}
        kb.oT_dep = Buf("oT")
        kb.ynT_dep = Buf("ynT")
        kb.setup_consts()
        cur = dram["x"]
        n = len(sublayers)
        for si, (kind, li) in enumerate(sublayers):
            dst = out if si == n - 1 else (xa if cur is not xa else xb)
            try:
                if kind == "ffn":
                    kb.ffn_phase(li, cur, dst)
                elif li % 3 == 0:
                    (kb.ssd_phase if os.environ.get("K_OLDSSD") else kb.ssd_phase2)(li, li // 3, cur, dst)
                elif li % 3 == 1:
                    kb.sg_phase(li, cur, dst)
                elif li % 3 == 2:
                    kb.mla_phase(li, cur, dst)
                else:
                    raise NotImplementedError((kind, li))
            except StopBuild:
                break
            cur = dst
        P.barrier()
        P.finalize(block, sems, dsems)
    kb_stats = {e: len(P.ins[e]) for e in ENGS}
    print("instr counts", kb_stats, "dmas", P.ndma)
    return nc


def prep_shared(inputs):
    f = lambda a: np.ascontiguousarray(a, dtype=np.float32)
    sh = {"consts": make_consts()}
    sh["ffn_w_in"] = f(inputs["ffn_w_in"])
    sh["ffn_w_out"] = f(inputs["ffn_w_out"])
    sh["ffn_cw"] = f(np.asarray(inputs["ffn_conv_w"]).reshape(DEPTH, 3, 44, 128).transpose(0, 3, 2, 1))
    sh["ffn_cb"] = f(np.asarray(inputs["ffn_conv_b"]).reshape(DEPTH, 44, 128).transpose(0, 2, 1))
    sh["ssm_w_in"] = f(inputs["ssm_w_in"])
    sh["ssm_w_out"] = f(inputs["ssm_w_out"])
    sh["ssm_cw"] = f(np.asarray(inputs["ssm_conv_w"]).reshape(2, 4, 32, 128).transpose(0, 3, 2, 1))
    sh["ssm_cb"] = f(np.asarray(inputs["ssm_conv_b"]).reshape(2, 32, 128).transpose(0, 2, 1))
    sh["ssm_small"] = f(np.concatenate([np.asarray(inputs["ssm_dt_bias"]), np.asarray(inputs["ssm_a_log"]), np.asarray(inputs["ssm_d"])], axis=1))
    sh["ssm_norm_g"] = f(inputs["ssm_norm_g"])
    sh["sg_w_in"] = f(inputs["sg_w_in"])
    sh["sg_w_out"] = f(inputs["sg_w_out"])
    sh["sg_b_in"] = f(inputs["sg_b_in"])
    sh["sg_ln_g"] = f(inputs["sg_ln_g"])
    sh["sg_ln_b"] = f(inputs["sg_ln_b"])
    sh["sg_w_s"] = f(inputs["sg_w_s"])
    sh["sg_bs"] = f(np.asarray(inputs["sg_b_s"])[0].T)
    for k_ in ("mla_w_in", "mla_w_q_b", "mla_w_kv_b", "mla_w_out", "mla_q_norm_g", "mla_kv_norm_g"):
        sh[k_] = f(inputs[k_])
    sh["ln_g"] = f(np.asarray(inputs["ln_g"]).reshape(DEPTH * 2, D))
    sh["ln_b"] = f(np.asarray(inputs["ln_b"]).reshape(DEPTH * 2, D))
    return sh


def per_core(inputs, c):
    return {"positions": np.ascontiguousarray(np.asarray(inputs["positions"])[c:c + 1].astype(np.int32))}


ALL_SUBLAYERS = [(k, i) for i in range(DEPTH) for k in ("mix", "ffn")]


def kernel(**inputs):
    x = np.asarray(inputs["x"], dtype=np.float32)
    nc = build(ALL_SUBLAYERS)
    sh = prep_shared(inputs)
    in_maps = [dict(sh, x=np.ascontiguousarray(x[c]), **per_core(inputs, c)) for c in range(8)]
    res = run_bass_kernel_spmd(nc, in_maps, core_ids=list(range(8)))
    return np.stack([res.results[c]["out"] for c in range(8)], axis=0).astype(np.float32)
```
